# Optimizing a Trainium2 kernel written in Bass

```python
import math
import jax, jax.numpy as jnp
from jax import lax
import numpy as np

D_MODEL = 1024
BATCH = 4
SEQ = 4096
DEPTH = 1

GRID_W = 64
CTX_LEN = 256
D_A = 2 * D_MODEL
A_HEADS = 4
A_HEAD_DIM = D_A // A_HEADS
QKV_BLOCK = 4
A_NB = D_A // QKV_BLOCK
A_CONV = 3
CHUNK = 128
D_B = D_MODEL
B_CONV = 3
FILTER_BANDS = 16
FILTER_EMB = 1 + 2 * FILTER_BANDS
FILTER_HIDDEN = 64
DECAY_TARGET = 1e-2
FAST_DECAY_PCT = 0.3
SLOW_DECAY_PCT = 1.5
MAX_DECAY = math.log(DECAY_TARGET) / FAST_DECAY_PCT
MIN_DECAY = math.log(DECAY_TARGET) / SLOW_DECAY_PCT
FILTER_SHIFT = 0.05
D_FF = 2816
EPS = 1e-6
N_MOD = 9
IN_WIDTH = 2 * D_A + 3 * D_B + 2 * D_MODEL
IN_SPLITS = (D_A, 2 * D_A, 2 * D_A + 3 * D_B, 2 * D_A + 3 * D_B + D_MODEL)

kernel_name = "hybrid_mlstm_hyena_dit_layer"

F32 = jnp.float32


def rmsnorm(x, g):
    xf = x.astype(F32)
    y = xf * lax.rsqrt(jnp.mean(xf * xf, -1, keepdims=True) + EPS)
    return (y * g.astype(F32)).astype(x.dtype)


def modulate(h, shift, scale):
    return h * (1.0 + scale[:, None, :]) + shift[:, None, :]


def swiglu(h, w_up, w_down):
    g, u = jnp.split(h @ w_up, 2, axis=-1)
    return (jax.nn.silu(g) * u) @ w_down


def short_conv(x, w, b):
    K = w.shape[0]
    L = x.shape[1]
    xp = jnp.pad(x, ((0, 0), (K // 2, K // 2), (0, 0)))
    y = xp[:, 0:L] * w[0]
    for j in range(1, K):
        y = y + xp[:, j:j + L] * w[j]
    return y + b


def sincos_2d(L, dtype):
    rows = L // GRID_W
    t = jnp.arange(rows * GRID_W)
    r = (t // GRID_W).astype(F32)
    c = (t % GRID_W).astype(F32)
    nf = D_MODEL // 4
    omega = 1.0 / (10000.0 ** (jnp.arange(nf, dtype=F32) / nf))

    def emb(p):
        a = p[:, None] * omega[None, :]
        return jnp.concatenate([jnp.sin(a), jnp.cos(a)], -1)

    return jnp.concatenate([emb(r), emb(c)], -1).astype(dtype)


def blockdiag(x, w):
    B, L, _ = x.shape
    xb = x.reshape(B, L, A_NB, QKV_BLOCK)
    return jnp.einsum('blni,nij->blnj', xb, w).reshape(B, L, A_NB * QKV_BLOCK)


def flip_t(a):
    return jnp.flip(a, axis=2)


def zero_state(B):
    return (jnp.zeros((B, A_HEADS, A_HEAD_DIM, A_HEAD_DIM), F32),
            jnp.zeros((B, A_HEADS, A_HEAD_DIM), F32),
            jnp.zeros((B, A_HEADS), F32))


def mlstm_features(xm, lp):
    B, L, _ = xm.shape
    xc = jax.nn.silu(short_conv(xm, lp['a_conv_w'], lp['a_conv_b']))
    q = blockdiag(xc, lp['a_wq'])
    k = blockdiag(xc, lp['a_wk'])
    v = blockdiag(xm, lp['a_wv'])
    g = jnp.concatenate([q, k, v], -1) @ lp['a_w_gate'] + lp['a_b_gate']
    g = jnp.moveaxis(g.astype(F32), -1, 1)
    ig_f, fg_f, ig_b, fg_b = jnp.split(g, 4, axis=1)

    def heads(a):
        return jnp.moveaxis(a.reshape(B, L, A_HEADS, A_HEAD_DIM), 2, 1).astype(F32)

    q = heads(q) * (A_HEAD_DIM ** -0.5)
    return (xc, q, heads(k), heads(v),
            (ig_f, jax.nn.log_sigmoid(fg_f)), (ig_b, jax.nn.log_sigmoid(fg_b)))


def _state_update(k, v, ig, b, state):
    C, n, m = state
    b_last = b[..., -1]
    log_w = b_last[..., None] - b + ig
    m_new = jnp.maximum(b_last + m, jnp.max(log_w, -1))
    decay = jnp.exp(b_last + m - m_new)
    kw = k * jnp.exp(log_w - m_new[..., None])[..., None]
    C_new = decay[..., None, None] * C + jnp.einsum('bhsk,bhsv->bhkv', kw, v)
    n_new = decay[..., None] * n + jnp.sum(kw, axis=2)
    return (C_new, n_new, m_new)


def _chunk_step(state, inp):
    q, k, v, ig, lf = inp
    C, n, m = state
    Lc = q.shape[2]
    b = jnp.cumsum(lf, -1)
    lower = jnp.tril(jnp.ones((Lc, Lc), dtype=bool))
    log_d = jnp.where(lower, b[..., :, None] - b[..., None, :] + ig[..., None, :], -jnp.inf)
    m_inter = b + m[..., None]
    m_t = jnp.maximum(m_inter, jnp.max(log_d, -1))
    d = jnp.exp(log_d - m_t[..., None])
    inter = jnp.exp(m_inter - m_t)
    s = jnp.einsum('bhtk,bhsk->bhts', q, k) * d
    num = jnp.einsum('bhts,bhsv->bhtv', s, v) + inter[..., None] * jnp.einsum('bhtk,bhkv->bhtv', q, C)
    den = jnp.sum(s, -1) + inter * jnp.einsum('bhtk,bhk->bht', q, n)
    h = num / jnp.maximum(jnp.abs(den), jnp.exp(-m_t))[..., None]
    return _state_update(k, v, ig, b, state), h


def mlstm_scan(q, k, v, ig, lf, state):
    B, H, L, dh = q.shape
    nc = L // CHUNK

    def to_chunks(a):
        a = a.reshape(B, H, nc, CHUNK, *a.shape[3:])
        return jnp.moveaxis(a, 2, 0)

    state, h = lax.scan(_chunk_step, state, (to_chunks(q), to_chunks(k), to_chunks(v), to_chunks(ig), to_chunks(lf)))
    h = jnp.moveaxis(h, 0, 2).reshape(B, H, L, dh)
    return h, state


def mlstm_out(h, xc, z, lp):
    B, H, L, dh = h.shape
    hn = h * lax.rsqrt(jnp.mean(h * h, -1, keepdims=True) + EPS)
    hn = jnp.moveaxis(hn, 1, 2).reshape(B, L, H * dh).astype(xc.dtype)
    return jax.nn.sigmoid(z) * (hn * lp['a_norm_g'] + lp['a_skip'] * xc)


def mlstm_mixer(xm, z, lp, init_f, init_b):
    xc, q, k, v, (ig_f, lf_f), (ig_b, lf_b) = mlstm_features(xm, lp)
    h_f, st_f = mlstm_scan(q, k, v, ig_f, lf_f, init_f)
    h_b, st_b = mlstm_scan(flip_t(q), flip_t(k), flip_t(v), flip_t(ig_b), flip_t(lf_b), init_b)
    return mlstm_out(h_f + flip_t(h_b), xc, z, lp), st_f, st_b


def mlstm_context_states(xm, lp):
    _, q, k, v, (ig_f, lf_f), (ig_b, lf_b) = mlstm_features(xm, lp)
    init = zero_state(k.shape[0])
    st_f = _state_update(k, v, ig_f, jnp.cumsum(lf_f, -1), init)
    st_b = _state_update(flip_t(k), flip_t(v), flip_t(ig_b), jnp.cumsum(flip_t(lf_b), -1), init)
    return st_f, st_b


def hyena_filters(L, lp):
    t = jnp.linspace(0.0, 1.0, L, dtype=F32)[:, None]
    w = 2.0 * math.pi * jnp.arange(L, dtype=F32)[:, None] / L
    f = jnp.linspace(1e-4, FILTER_BANDS - 1, FILTER_BANDS, dtype=F32)[None, :]
    z = jnp.concatenate([t, jnp.cos(f * w), -jnp.sin(f * w)], -1)
    freq = lp['b_filt_freq'].astype(F32)
    a = jnp.sin(freq * (z @ lp['b_filt_w1'].astype(F32) + lp['b_filt_b1'].astype(F32)))
    a = jnp.sin(freq * (a @ lp['b_filt_w2'].astype(F32) + lp['b_filt_b2'].astype(F32)))
    a = jnp.sin(freq * (a @ lp['b_filt_w3'].astype(F32) + lp['b_filt_b3'].astype(F32)))
    hf = a @ lp['b_filt_w4'].astype(F32)
    deltas = jnp.abs(jnp.linspace(MIN_DECAY, MAX_DECAY, D_B, dtype=F32))
    window = jnp.exp(-t * deltas[None, :]) + FILTER_SHIFT
    h_fwd, h_bwd = jnp.split(hf, 2, axis=-1)
    return h_fwd * window, h_bwd * window


def bidir_long_conv(u, h_fwd, h_bwd, d_skip):
    B, L, C = u.shape
    k = jnp.concatenate([h_fwd, jnp.zeros((1, C), F32), jnp.flip(h_bwd[1:], axis=0)], 0)
    uf = jnp.fft.rfft(u.astype(F32), n=2 * L, axis=1)
    kf = jnp.fft.rfft(k, n=2 * L, axis=0)
    y = jnp.fft.irfft(uf * kf[None], n=2 * L, axis=1)[:, :L]
    return (y + u.astype(F32) * d_skip.astype(F32)).astype(u.dtype)


def hyena_mixer(hy, lp):
    L = hy.shape[1]
    hy = short_conv(hy, lp['b_conv_w'], lp['b_conv_b'])
    x0, x1, v = jnp.split(hy, 3, axis=-1)
    h_fwd, h_bwd = hyena_filters(L, lp)
    return x0 * bidir_long_conv(x1 * v, h_fwd, h_bwd, lp['b_skip'])


def parallel_mixers(h, lp, init_f, init_b):
    xm, z, hy, ga, gb = jnp.split(h @ lp['w_in'], IN_SPLITS, axis=-1)
    ya, st_f, st_b = mlstm_mixer(xm, z, lp, init_f, init_b)
    yb = hyena_mixer(hy, lp)
    mix = jax.nn.sigmoid(ga) * (ya @ lp['w_pa']) + jax.nn.sigmoid(gb) * (yb @ lp['w_pb'])
    return mix @ lp['w_out'], st_f, st_b


def layer(x, ctx, c_silu, c_ctx_silu, lp, last):
    m = jnp.split(c_silu @ lp['w_ada'] + lp['b_ada'], N_MOD, axis=-1)
    mc = jnp.split(c_ctx_silu @ lp['w_ada'] + lp['b_ada'], N_MOD, axis=-1)
    g = lp['norm_g']

    def pre(t, i, mods):
        return modulate(rmsnorm(t, g[i]), mods[3 * i], mods[3 * i + 1])

    x = x + 0.5 * m[2][:, None, :] * swiglu(pre(x, 0, m), lp['ffn1_up'], lp['ffn1_down'])
    ctx = ctx + 0.5 * mc[2][:, None, :] * swiglu(pre(ctx, 0, mc), lp['ffn1_up'], lp['ffn1_down'])
    h = pre(x, 1, m)
    hc = pre(ctx, 1, mc)
    if last:
        st_f, st_b = mlstm_context_states(hc @ lp['w_in'][:, :D_A], lp)
    else:
        init = zero_state(ctx.shape[0])
        ctx_mix, st_f, st_b = parallel_mixers(hc, lp, init, init)
        ctx = ctx + mc[5][:, None, :] * ctx_mix
        ctx = ctx + 0.5 * mc[8][:, None, :] * swiglu(pre(ctx, 2, mc), lp['ffn2_up'], lp['ffn2_down'])
    x_mix, _, _ = parallel_mixers(h, lp, st_f, st_b)
    x = x + m[5][:, None, :] * x_mix
    x = x + 0.5 * m[8][:, None, :] * swiglu(pre(x, 2, m), lp['ffn2_up'], lp['ffn2_down'])
    return x, ctx


def setup_inputs(seed: int = 0) -> dict:
    key = jax.random.key(seed)
    ks = iter(jax.random.split(key, 48))

    def nrm(shape, scale):
        return scale * jax.random.normal(next(ks), shape, F32)

    Dp = DEPTH
    f_lin = jnp.linspace(3.0, 6.0, A_HEADS, dtype=F32)
    a_b_gate = jnp.concatenate([
        nrm((Dp, A_HEADS), 0.1),
        f_lin + nrm((Dp, A_HEADS), 0.1),
        nrm((Dp, A_HEADS), 0.1),
        f_lin + nrm((Dp, A_HEADS), 0.1)], axis=-1)
    return {
        'x': nrm((BATCH, SEQ, D_MODEL), 1.0),
        'c': nrm((BATCH, D_MODEL), 1.0),
        'ctx': nrm((BATCH, CTX_LEN, D_MODEL), 1.0),
        'c_ctx': nrm((D_MODEL,), 1.0),
        'w_ada': nrm((Dp, D_MODEL, N_MOD * D_MODEL), 0.5 * D_MODEL ** -0.5),
        'b_ada': nrm((Dp, N_MOD * D_MODEL), 0.02),
        'norm_g': 1.0 + nrm((Dp, 3, D_MODEL), 0.05),
        'ffn1_up': nrm((Dp, D_MODEL, 2 * D_FF), D_MODEL ** -0.5),
        'ffn1_down': nrm((Dp, D_FF, D_MODEL), D_FF ** -0.5),
        'ffn2_up': nrm((Dp, D_MODEL, 2 * D_FF), D_MODEL ** -0.5),
        'ffn2_down': nrm((Dp, D_FF, D_MODEL), D_FF ** -0.5),
        'w_in': nrm((Dp, D_MODEL, IN_WIDTH), D_MODEL ** -0.5),
        'a_conv_w': nrm((Dp, A_CONV, D_A), 0.5),
        'a_conv_b': nrm((Dp, D_A), 0.02),
        'a_wq': nrm((Dp, A_NB, QKV_BLOCK, QKV_BLOCK), QKV_BLOCK ** -0.5),
        'a_wk': nrm((Dp, A_NB, QKV_BLOCK, QKV_BLOCK), QKV_BLOCK ** -0.5),
        'a_wv': nrm((Dp, A_NB, QKV_BLOCK, QKV_BLOCK), QKV_BLOCK ** -0.5),
        'a_w_gate': nrm((Dp, 3 * D_A, 4 * A_HEADS), (3 * D_A) ** -0.5),
        'a_b_gate': a_b_gate,
        'a_norm_g': 1.0 + nrm((Dp, D_A), 0.05),
        'a_skip': 1.0 + nrm((Dp, D_A), 0.05),
        'b_conv_w': nrm((Dp, B_CONV, 3 * D_B), 0.5),
        'b_conv_b': nrm((Dp, 3 * D_B), 0.02),
        'b_filt_w1': nrm((Dp, FILTER_EMB, FILTER_HIDDEN), FILTER_EMB ** -0.5),
        'b_filt_b1': nrm((Dp, FILTER_HIDDEN), 0.1),
        'b_filt_w2': nrm((Dp, FILTER_HIDDEN, FILTER_HIDDEN), FILTER_HIDDEN ** -0.5),
        'b_filt_b2': nrm((Dp, FILTER_HIDDEN), 0.1),
        'b_filt_w3': nrm((Dp, FILTER_HIDDEN, FILTER_HIDDEN), FILTER_HIDDEN ** -0.5),
        'b_filt_b3': nrm((Dp, FILTER_HIDDEN), 0.1),
        'b_filt_w4': nrm((Dp, FILTER_HIDDEN, 2 * D_B), 0.1 * FILTER_HIDDEN ** -0.5),
        'b_filt_freq': 1.0 + nrm((Dp, FILTER_HIDDEN), 0.05),
        'b_skip': nrm((Dp, D_B), 0.5),
        'w_pa': nrm((Dp, D_A, D_MODEL), D_A ** -0.5),
        'w_pb': nrm((Dp, D_B, D_MODEL), D_B ** -0.5),
        'w_out': nrm((Dp, D_MODEL, D_MODEL), D_MODEL ** -0.5),
        'final_g': 1.0 + nrm((D_MODEL,), 0.05),
    }


def reference(x, c, ctx, c_ctx, w_ada, b_ada, norm_g, ffn1_up, ffn1_down, ffn2_up, ffn2_down,
              w_in, a_conv_w, a_conv_b, a_wq, a_wk, a_wv, a_w_gate, a_b_gate, a_norm_g, a_skip,
              b_conv_w, b_conv_b, b_filt_w1, b_filt_b1, b_filt_w2, b_filt_b2, b_filt_w3, b_filt_b3,
              b_filt_w4, b_filt_freq, b_skip, w_pa, w_pb, w_out, final_g):
    L = x.shape[1]
    x = x + sincos_2d(L, x.dtype)[None]
    c_silu = jax.nn.silu(c)
    c_ctx_silu = jax.nn.silu(c_ctx)[None]
    for l in range(DEPTH):
        lp = dict(w_ada=w_ada[l], b_ada=b_ada[l], norm_g=norm_g[l],
                  ffn1_up=ffn1_up[l], ffn1_down=ffn1_down[l], ffn2_up=ffn2_up[l], ffn2_down=ffn2_down[l],
                  w_in=w_in[l], a_conv_w=a_conv_w[l], a_conv_b=a_conv_b[l],
                  a_wq=a_wq[l], a_wk=a_wk[l], a_wv=a_wv[l], a_w_gate=a_w_gate[l], a_b_gate=a_b_gate[l],
                  a_norm_g=a_norm_g[l], a_skip=a_skip[l],
                  b_conv_w=b_conv_w[l], b_conv_b=b_conv_b[l],
                  b_filt_w1=b_filt_w1[l], b_filt_b1=b_filt_b1[l], b_filt_w2=b_filt_w2[l], b_filt_b2=b_filt_b2[l],
                  b_filt_w3=b_filt_w3[l], b_filt_b3=b_filt_b3[l], b_filt_w4=b_filt_w4[l],
                  b_filt_freq=b_filt_freq[l], b_skip=b_skip[l],
                  w_pa=w_pa[l], w_pb=w_pb[l], w_out=w_out[l])
        x, ctx = layer(x, ctx, c_silu, c_ctx_silu, lp, l == DEPTH - 1)
    return rmsnorm(x, final_g)
```

```python
import math
from contextlib import ExitStack

import numpy as np
import ml_dtypes

import concourse.bass as bass
import concourse.mybir as mybir
from concourse.bass_utils import run_bass_kernel_spmd

F32 = mybir.dt.float32
BF16 = mybir.dt.bfloat16
AF = mybir.ActivationFunctionType
ALU = mybir.AluOpType
AX = mybir.AxisListType


class Cfg:
    def __init__(self, D=1024, L=4096, CTX=256, FF=2816, H=4, GRID_W=64, NB=16, FH=64, n_batch=4):
        self.D, self.L, self.CTX, self.FF, self.H, self.GRID_W = D, L, CTX, FF, H, GRID_W
        self.DA = 2 * D
        self.DB = D
        self.HD = self.DA // H
        self.NB = NB
        self.FE = 1 + 2 * NB
        self.FH = FH
        self.INW = 2 * self.DA + 3 * self.DB + 2 * D
        self.n_batch = n_batch
        self.EPS = 1e-6
        self.TT = min(1024, L)
        self.CB = min(256, self.DB)
        self.HW = 512


class Sched:
    NPOOL = 12

    def __init__(self, nc, stack):
        self.nc = nc
        self.eng = {"pe": nc.tensor, "act": nc.scalar, "dve": nc.vector, "pool": nc.gpsimd, "sp": nc.sync}
        self.ops = []
        self.stack = stack
        self.sems = {}
        for e in ("pe", "act", "dve", "pool"):
            self.sems[e] = stack.enter_context(nc.semaphore("s_" + e))
        self.dsems = {}
        for q in ("sp", "pool"):
            self.dsems[q] = [stack.enter_context(nc.semaphore(f"d_{q}{i}")) for i in range(self.NPOOL)]

    def op(self, eng, fn, reads=(), writes=(), dma=False):
        self.ops.append(dict(eng=eng, fn=fn, reads=tuple(reads), writes=tuple(writes), dma=dma, barrier=False))

    def barrier(self):
        for e in ("pe", "act", "dve", "pool", "sp"):
            self.ops.append(dict(eng=e, fn=None, reads=(), writes=(), dma=False, barrier=True))

    def finalize(self):
        ops = self.ops
        n = len(ops)
        last_w = {}
        readers = {}
        deps = [None] * n
        last_by_stream = {}
        dma_count = {"sp": 0, "pool": 0}
        dma_slot = [None] * n
        for i, o in enumerate(ops):
            d = set()
            if o["barrier"]:
                d.update(last_by_stream.values())
            else:
                for k in o["reads"]:
                    if k in last_w:
                        d.add(last_w[k])
                for k in o["writes"]:
                    if k in last_w:
                        d.add(last_w[k])
                    d.update(readers.get(k, ()))
                if o["dma"]:
                    q = o["eng"]
                    c = dma_count[q]
                    dma_count[q] = c + 1
                    slot = c % self.NPOOL
                    dma_slot[i] = (slot, 16 * (c // self.NPOOL + 1))
                    prev = last_by_stream.get((q, slot))
                    if prev is not None:
                        d.add(prev)
                    last_by_stream[(q, slot)] = i
                else:
                    last_by_stream[o["eng"]] = i
                for k in o["reads"]:
                    readers.setdefault(k, []).append(i)
                for k in o["writes"]:
                    last_w[k] = i
                    readers[k] = []
            d.discard(i)
            deps[i] = d
        needed = [False] * n
        for i in range(n):
            pe_i = ops[i]["eng"] == "pe" and not ops[i]["barrier"]
            for j in deps[i]:
                if pe_i and ops[j]["eng"] == "pe" and not ops[j]["barrier"]:
                    continue
                needed[j] = True
        count = {e: 0 for e in ("pe", "act", "dve", "pool")}
        event = [None] * n
        evclock = [None] * n
        clock = {e: {} for e in self.eng}
        for i, o in enumerate(ops):
            e = o["eng"]
            eh = self.eng[e]
            ck = clock[e]
            for j in sorted(deps[i]):
                oj = ops[j]
                if oj["barrier"] or event[j] is None:
                    continue
                sk, val = event[j]
                if (not oj["dma"]) and oj["eng"] == e and e == "pe":
                    continue
                if ck.get(sk, 0) >= val:
                    continue
                sem = self.sems[sk] if isinstance(sk, str) else self.dsems[sk[0]][sk[1]]
                eh.wait_ge(sem, val)
                for k2, v2 in evclock[j].items():
                    if ck.get(k2, 0) < v2:
                        ck[k2] = v2
            if o["barrier"]:
                continue
            ins = o["fn"](eh)
            if o["dma"]:
                slot, val = dma_slot[i]
                sk = (e, slot)
                ins.then_inc(self.dsems[e][slot], 16)
                event[i] = (sk, val)
                snap = dict(ck)
                snap[sk] = val
                evclock[i] = snap
            elif needed[i]:
                assert e != "sp", "sp only issues DMAs"
                count[e] += 1
                ins.then_inc(self.sems[e], 1)
                event[i] = (e, count[e])
                snap = dict(ck)
                snap[e] = count[e]
                evclock[i] = snap
        self.ops = []
        return count


def _bcast_rows(ap_row, nparts):
    return ap_row.broadcast(0, nparts) if hasattr(ap_row, "broadcast") else ap_row


class Prog:
    def __init__(self, cfg, phases="ABCDEF"):
        self.cfg = c = cfg
        self.phases = phases
        self.nc = nc = bass.Bass("TRN2", target_bir_lowering=False)
        self.DC = c.D // 128
        self.FC = c.FF // 128
        self.AC = c.DA // 128
        self.BC = c.DB // 128
        self.NT = c.L // 128
        self.NCC = c.CTX // 128
        self.NTC = self.NT + self.NCC
        self.KH = c.HD // 128
        self.LT = c.L + c.CTX
        self.uid = 0
        d = self.din
        DC, FC, AC, BC, NT = self.DC, self.FC, self.AC, self.BC, self.NT
        self.xT = d("xT", [c.D, c.L])
        self.posT = d("posT", [c.D, c.L])
        self.ctxT = d("ctxT", [c.D, c.CTX])
        self.cs = d("cs", [128, DC, 2])
        self.w_ada = d("w_ada", [c.D, 9 * c.D])
        self.b_adac = d("b_adac", [128, 9 * DC])
        self.normg = d("normg", [128, 3, DC])
        self.finalg = d("finalg", [128, DC])
        self.ffn1_up = d("ffn1_up", [c.D, 2 * c.FF])
        self.ffn1_down = d("ffn1_down", [c.FF, c.D])
        self.ffn2_up = d("ffn2_up", [c.D, 2 * c.FF])
        self.ffn2_down = d("ffn2_down", [c.FF, c.D])
        self.w_in = d("w_in", [c.D, c.INW])
        self.w_pa = d("w_pa", [c.DA, c.D])
        self.w_pb = d("w_pb", [c.DB, c.D])
        self.w_out = d("w_out", [c.D, c.D])
        self.a_convw = d("a_convw", [128, 3, AC])
        self.a_convb = d("a_convb", [128, AC])
        self.bdq = d("bdq", [128, AC, 128])
        self.bdk = d("bdk", [128, AC, 128])
        self.bdv = d("bdv", [128, AC, 128])
        self.wgate = d("wgate", [128, 3 * AC, 16])
        self.bgate = d("bgate", [1, 16])
        self.a_normg = d("a_normg", [128, AC])
        self.a_skip = d("a_skip", [128, AC])
        self.b_convw = d("b_convw", [128, 3, 3 * BC])
        self.b_convb = d("b_convb", [128, 3 * BC])
        self.b_skip = d("b_skip", [128, BC])
        self.zT = d("zT", [c.FE, c.L])
        self.fw1 = d("fw1", [c.FE, c.FH])
        self.fw2 = d("fw2", [c.FH, c.FH])
        self.fw3 = d("fw3", [c.FH, c.FH])
        self.fw4 = d("fw4", [c.FH, 2 * c.DB])
        self.fb = d("fb", [c.FH, 4])
        self.deltas = d("deltas", [1, c.DB])
        self.tn = d("tn", [128, NT])
        self.altc = d("altc", [128, 2])
        self.altr = d("altr", [1, 128])
        self.ctab = d("ctab", [NT, 128, NT, 128], BF16)
        self.stab = d("stab", [NT, 128, NT, 128], BF16)
        self.outT = nc.dram_tensor("outT", [c.D, c.L], F32, kind="ExternalOutput").ap()
        s = self.dscr
        self.X1T = s("X1T", [c.D, c.L], F32)
        self.PROJ = s("PROJ", [c.INW, c.L], BF16)
        self.CXM = s("CXM", [c.DA, c.CTX], BF16)
        self.XC = s("XC", [c.DA, c.L], BF16)
        self.QT = s("QT", [c.DA, c.L], BF16)
        self.KT = s("KT", [c.DA, self.LT], BF16)
        self.KTM = s("KTM", [self.LT, c.DA], BF16)
        self.VTM = s("VTM", [self.LT, c.DA], BF16)
        self.HF = s("HF", [c.L, c.DA], F32)
        self.HB = s("HB", [c.L, c.DA], F32)
        self.YA = s("YA", [c.DA, c.L], BF16)
        self.YB = s("YB", [c.DB, c.L], BF16)
        self.dbg = {}

    def din(self, name, shape, dt=F32):
        return self.nc.dram_tensor(name, list(shape), dt, kind="ExternalInput").ap()

    def dscr(self, name, shape, dt):
        return self.nc.dram_tensor(name, list(shape), dt, kind="Internal").ap()

    def u(self, p="k"):
        self.uid += 1
        return (p, self.uid)

    def sb(self, st, name, shape, dt):
        self.uid += 1
        return st.enter_context(self.nc.sbuf_tensor(f"{name}_{self.uid}", list(shape), dt))

    def psum_alloc(self, st, n=8):
        self.ps = [st.enter_context(self.nc.psum_tensor(f"ps{i}_{self.u()[1]}", [128, 512], F32)) for i in range(n)]
        self.ps_i = 0

    def psn(self):
        i = self.ps_i % len(self.ps)
        self.ps_i += 1
        return self.ps[i], ("ps", i)

    def dma(self, q, out, in_, reads, writes):
        self.S.op(q, lambda e: e.dma_start(out=out, in_=in_), reads=reads, writes=writes, dma=True)

    def mm(self, out, lhsT, rhs, start, stop, reads, writes):
        self.S.op("pe", lambda e: e.matmul(out, lhsT, rhs, start=start, stop=stop), reads=reads, writes=writes)

    def build(self):
        nc, c = self.nc, self.cfg
        with ExitStack() as st0:
            self.S = S = Sched(nc, st0)
            self.st0 = st0
            self.consts(st0)
            if "A" in self.phases:
                with ExitStack() as st:
                    self.phase_A(st)
                    S.barrier()
            if "B" in self.phases:
                with ExitStack() as st:
                    self.phase_B1(st)
                    S.barrier()
            if "C" in self.phases:
                with ExitStack() as st:
                    self.phase_B2(st)
                    S.barrier()
            if "D" in self.phases:
                with ExitStack() as st:
                    self.phase_B3(st)
                    S.barrier()
            if "E" in self.phases:
                with ExitStack() as st:
                    self.phase_B4(st)
                    S.barrier()
            if "F" in self.phases:
                with ExitStack() as st:
                    self.phase_C(st)
                    S.barrier()
            S.barrier()
            self.counts = S.finalize()
        return nc

    def consts(self, st):
        S, c, nc = self.S, self.cfg, self.nc
        DC = self.DC
        self.idf = idf = self.sb(st, "idf", [128, 128], F32)
        self.idb = idb = self.sb(st, "idb", [128, 128], BF16)
        self.onesb = onesb = self.sb(st, "onesb", [128, 128], BF16)
        self.onesf = onesf = self.sb(st, "onesf", [128, 128], F32)
        self.triu_f = triu_f = self.sb(st, "triu_f", [128, 128], F32)
        self.tril_f = tril_f = self.sb(st, "tril_f", [128, 128], F32)
        self.mods = mods = self.sb(st, "mods", [128, 9 * DC, 2], F32)
        self.na = na = self.sb(st, "na", [128, 3, DC, 2], F32)
        self.gt = gt = self.sb(st, "gt", [128, 3, DC, 2], F32)
        self.fing = fing = self.sb(st, "fing", [128, DC], F32)
        self.epsD = epsD = self.sb(st, "epsD", [128, 2], F32)
        shp = [128, c.H, self.NTC]
        self.gw = [self.sb(st, f"gw{d}", shp, F32) for d in range(2)]
        self.gcl = [self.sb(st, f"gcl{d}", shp, F32) for d in range(2)]
        self.gr = [self.sb(st, f"gr{d}", shp, F32) for d in range(2)]
        self.grq = [self.sb(st, f"grq{d}", shp, F32) for d in range(2)]
        S.op("pool", lambda e: e.memset(epsD[:, 0:1], c.D * c.EPS), writes=["epsD"])
        S.op("pool", lambda e: e.memset(epsD[:, 1:2], c.EPS), writes=["epsD"])
        S.op("pool", lambda e: e.memset(idf[:], 1.0), writes=["idf"])
        S.op("pool", lambda e: e.affine_select(out=idf[:], in_=idf[:], pattern=[[-1, 128]], compare_op=ALU.is_equal,
                                               fill=0.0, base=0, channel_multiplier=1), reads=["idf"], writes=["idf"])
        S.op("dve", lambda e: e.tensor_copy(out=idb[:], in_=idf[:]), reads=["idf"], writes=["idb"])
        S.op("pool", lambda e: e.memset(onesb[:], 1.0), writes=["onesb"])
        S.op("pool", lambda e: e.memset(onesf[:], 1.0), writes=["onesf"])
        S.op("pool", lambda e: e.memset(triu_f[:], 1.0), writes=["triu_f"])
        S.op("pool", lambda e: e.affine_select(out=triu_f[:], in_=triu_f[:], pattern=[[1, 128]], compare_op=ALU.is_ge,
                                               fill=0.0, base=0, channel_multiplier=-1), reads=["triu_f"], writes=["triu_f"])
        S.op("pool", lambda e: e.memset(tril_f[:], 1.0), writes=["tril_f"])
        S.op("pool", lambda e: e.affine_select(out=tril_f[:], in_=tril_f[:], pattern=[[-1, 128]], compare_op=ALU.is_ge,
                                               fill=0.0, base=0, channel_multiplier=1), reads=["tril_f"], writes=["tril_f"])
        with ExitStack() as st2:
            self.psum_alloc(st2, 4)
            csf = self.sb(st2, "csf", [128, DC, 2], F32)
            csb = self.sb(st2, "csb", [128, DC, 2], BF16)
            bad = self.sb(st2, "bad", [128, 9 * DC], F32)
            ng = self.sb(st2, "ng", [128, 3, DC], F32)
            wsl = [self.sb(st2, f"adaw{i}", [128, DC, 512], BF16) for i in range(2)]
            self.dma("sp", csf[:], self.cs, [], ["csf"])
            self.dma("sp", bad[:], self.b_adac, [], ["bad"])
            self.dma("sp", ng[:], self.normg, [], ["ng"])
            self.dma("sp", fing[:], self.finalg, [], ["fing"])
            S.op("act", lambda e: e.activation(out=csb[:], in_=csf[:], func=AF.Silu), reads=["csf"], writes=["csb"])
            ncol = 9 * c.D
            wv = self.w_ada.rearrange("(kc p) n -> p kc n", p=128)
            nsl = (ncol + 511) // 512
            for si in range(nsl):
                w0 = si * 512
                ww = min(512, ncol - w0)
                wt = wsl[si % 2]
                wk = ("adaw", si % 2)
                self.dma("pool", wt[:, :, 0:ww], wv[:, :, w0:w0 + ww], [], [wk])
                ps, pk = self.psn()
                for j in range(ww // 128):
                    for kc in range(DC):
                        self.mm(ps[:, 2 * j:2 * j + 2], wt[:, kc, j * 128:(j + 1) * 128], csb[:, kc, :],
                                kc == 0, kc == DC - 1, [wk, "csb"], [pk])
                n0 = w0 // 128
                nn = ww // 128
                S.op("dve", lambda e, ps=ps, n0=n0, nn=nn: e.tensor_tensor(
                    out=mods[:, n0:n0 + nn, :], in0=ps[:, 0:2 * nn].rearrange("p (n r) -> p n r", r=2),
                    in1=bad[:, n0:n0 + nn].unsqueeze(2).broadcast_to([128, nn, 2]), op=ALU.add),
                    reads=[pk, "bad"], writes=["mods"])
            rD = math.sqrt(c.D)
            for i in range(3):
                sc = mods[:, (3 * i + 1) * DC:(3 * i + 2) * DC, :]
                S.op("dve", lambda e, i=i, sc=sc: e.scalar_tensor_tensor(
                    out=na[:, i, :, :], in0=sc, scalar=1.0, in1=ng[:, i, :].unsqueeze(2).broadcast_to([128, DC, 2]),
                    op0=ALU.add, op1=ALU.mult), reads=["mods", "ng"], writes=["na"])
                g = mods[:, (3 * i + 2) * DC:(3 * i + 3) * DC, :]
                gsc = 1.0 if i == 1 else 0.5
                S.op("dve", lambda e, i=i, g=g, gsc=gsc: e.tensor_scalar(
                    out=gt[:, i, :, :], in0=g, scalar1=gsc, scalar2=None, op0=ALU.mult), reads=["mods"], writes=["gt"])
            S.op("dve", lambda e: e.tensor_scalar(out=na[:], in0=na[:], scalar1=rD, scalar2=None, op0=ALU.mult),
                 reads=["na"], writes=["na"])
            S.op("dve", lambda e: e.tensor_scalar(out=fing[:], in0=fing[:], scalar1=rD, scalar2=None, op0=ALU.mult),
                 reads=["fing"], writes=["fing"])
            S.barrier()

    def shift_ap(self, i, dc, r):
        return self.mods[:, 3 * i * self.DC + dc, r:r + 1]

    def linear(self, W, kcn, col0, ncols, rhs_fn, halves, evac, wbufs, wname, slabw, rows0=0):
        Wv = W[rows0:rows0 + kcn * 128, :].rearrange("(kc p) n -> p kc n", p=128)
        rot = self.__dict__.setdefault("_rot", {})
        s0 = 0
        while s0 < ncols:
            sw = min(slabw, ncols - s0)
            bi = rot.get(wname, 0)
            rot[wname] = bi + 1
            wt = wbufs[bi % len(wbufs)]
            wk = (wname, bi % len(wbufs))
            self.dma("pool", wt[:, 0:kcn, 0:sw], Wv[:, :, col0 + s0:col0 + s0 + sw], [], [wk])
            for j in range(sw // 128):
                for hi, (c0, cw) in enumerate(halves):
                    ps, pk = self.psn()
                    for kc in range(kcn):
                        rap, rkey = rhs_fn(kc, c0, cw)
                        self.mm(ps[:, 0:cw], wt[:, kc, j * 128:(j + 1) * 128], rap, kc == 0, kc == kcn - 1,
                                [wk] + list(rkey), [pk])
                    evac((s0 // 128) + j, hi, c0, cw, ps, pk)
            s0 += sw

    def halves_of(self, ntok):
        out, c0 = [], 0
        while c0 < ntok:
            cw = min(self.cfg.HW, ntok - c0)
            out.append((c0, cw))
            c0 += cw
        return out

    def rmsnorm_mod(self, T, ntok, i, r, final=False, out_f32=None):
        S, c, DC = self.S, self.cfg, self.DC
        xt, h, sq = T["xt"], T["h"], T["sq"]
        for (c0, cw) in self.halves_of(ntok):
            S.op("act", lambda e, c0=c0, cw=cw: e.activation(out=sq[:, :, c0:c0 + cw], in_=xt[:, :, c0:c0 + cw], func=AF.Square),
                 reads=[("xt", d) for d in range(DC)], writes=["sq"])
            ps, pk = self.psn()
            for dc in range(DC):
                self.mm(ps[:, 0:cw], self.onesb[:], sq[:, dc, c0:c0 + cw], dc == 0, dc == DC - 1, ["sq", "onesb"], [pk])
            rs = T["rs"]
            S.op("act", lambda e, ps=ps, cw=cw: e.activation(out=rs[:, 0:cw], in_=ps[:, 0:cw], func=AF.Sqrt, bias=self.epsD[:, 0:1], scale=1.0),
                 reads=[pk, "epsD"], writes=["rs"])
            S.op("dve", lambda e, cw=cw: e.reciprocal(out=rs[:, 0:cw], in_=rs[:, 0:cw]), reads=["rs"], writes=["rs"])
            for dc in range(DC):
                if final:
                    scal = self.fing[:, dc:dc + 1]
                    S.op("dve", lambda e, dc=dc, c0=c0, cw=cw, scal=scal: e.scalar_tensor_tensor(
                        out=out_f32[:, dc, c0:c0 + cw], in0=xt[:, dc, c0:c0 + cw], scalar=scal, in1=rs[:, 0:cw],
                        op0=ALU.mult, op1=ALU.mult), reads=[("xt", dc), "rs", "fing"], writes=[("of", dc)])
                    continue
                tb = T["tn"][dc % 2]
                tk = ("tn", dc % 2)
                scal = self.na[:, i, dc, r:r + 1]
                S.op("dve", lambda e, dc=dc, c0=c0, cw=cw, tb=tb, scal=scal: e.scalar_tensor_tensor(
                    out=tb[:, 0:cw], in0=xt[:, dc, c0:c0 + cw], scalar=scal, in1=rs[:, 0:cw], op0=ALU.mult, op1=ALU.mult),
                    reads=[("xt", dc), "rs", "na"], writes=[tk])
                sh = self.shift_ap(i, dc, r)
                S.op("act", lambda e, dc=dc, c0=c0, cw=cw, tb=tb, sh=sh: e.activation(
                    out=h[:, dc, c0:c0 + cw], in_=tb[:, 0:cw], func=AF.Identity, bias=sh, scale=1.0),
                    reads=[tk, "mods"], writes=[("h", dc)])

    def ffn(self, T, ntok, Wup, Wdown, gi, r):
        S, c, DC, FC = self.S, self.cfg, self.DC, self.FC
        xt, h, act = T["xt"], T["h"], T["act"]
        halves = self.halves_of(ntok)
        hkeys = [("h", d) for d in range(DC)]
        Wv = Wup.rearrange("(kc p) n -> p kc n", p=128)
        rot = self.__dict__.setdefault("_rot", {})
        f0 = 0
        while f0 < c.FF:
            sw = min(256, c.FF - f0)
            bi = rot.get("wA", 0)
            rot["wA"] = bi + 1
            wt = T["wA"][bi % 2]
            wk = ("wA", bi % 2)
            self.dma("pool", wt[:, :, 0:sw], Wv[:, :, f0:f0 + sw], [], [wk])
            self.dma("pool", wt[:, :, 256:256 + sw], Wv[:, :, c.FF + f0:c.FF + f0 + sw], [], [wk])
            for j in range(sw // 128):
                fj = f0 // 128 + j
                for (c0, cw) in halves:
                    pg, pgk = self.psn()
                    pu, puk = self.psn()
                    for kc in range(DC):
                        self.mm(pg[:, 0:cw], wt[:, kc, j * 128:(j + 1) * 128], h[:, kc, c0:c0 + cw], kc == 0, kc == DC - 1, [wk] + hkeys, [pgk])
                    for kc in range(DC):
                        self.mm(pu[:, 0:cw], wt[:, kc, 256 + j * 128:256 + (j + 1) * 128], h[:, kc, c0:c0 + cw], kc == 0, kc == DC - 1, [wk] + hkeys, [puk])
                    si = rot.get("sg", 0)
                    rot["sg"] = si + 1
                    sg = T["sg"][si % 2]
                    sgk = ("sg", si % 2)
                    S.op("act", lambda e, pg=pg, cw=cw, sg=sg: e.activation(out=sg[:, 0:cw], in_=pg[:, 0:cw], func=AF.Silu),
                         reads=[pgk], writes=[sgk])
                    S.op("dve", lambda e, pu=pu, cw=cw, sg=sg, fj=fj, c0=c0: e.tensor_tensor(
                        out=act[:, fj, c0:c0 + cw], in0=sg[:, 0:cw], in1=pu[:, 0:cw], op=ALU.mult),
                        reads=[sgk, puk], writes=[("act", fj)])
            f0 += sw
        akeys = [("act", f) for f in range(FC)]

        def evac(j, hi, c0, cw, ps, pk):
            scal = self.gt[:, gi, j, r:r + 1]
            S.op("dve", lambda e: e.scalar_tensor_tensor(out=xt[:, j, c0:c0 + cw], in0=ps[:, 0:cw], scalar=scal,
                                                         in1=xt[:, j, c0:c0 + cw], op0=ALU.mult, op1=ALU.add),
                 reads=[pk, ("xt", j), "gt"], writes=[("xt", j)])
        self.linear(Wdown, FC, 0, c.D, lambda kc, c0, cw: (act[:, kc, c0:c0 + cw], akeys), halves, evac, T["wB"], "wB", 256)

    def alloc_row_tiles(self, st, TT):
        DC, FC = self.DC, self.FC
        T = {}
        T["xt"] = self.sb(st, "xt", [128, DC, TT], F32)
        T["h"] = self.sb(st, "h", [128, DC, TT], BF16)
        T["sq"] = self.sb(st, "sq", [128, DC, TT], BF16)
        T["act"] = self.sb(st, "act", [128, max(FC, self.AC), TT], BF16)
        T["rs"] = self.sb(st, "rs", [128, 512], F32)
        T["tn"] = [self.sb(st, f"tn{i}", [128, 512], F32) for i in range(2)]
        T["sg"] = [self.sb(st, f"sg{i}", [128, 512], F32) for i in range(2)]
        T["wA"] = [self.sb(st, f"wA{i}", [128, DC, 512], BF16) for i in range(2)]
        T["wB"] = [self.sb(st, f"wB{i}", [128, max(FC, self.AC), 256], BF16) for i in range(2)]
        T["stg"] = [self.sb(st, f"stg{i}", [128, TT], BF16) for i in range(4)]
        return T

    def phase_A(self, st):
        S, c, DC = self.S, self.cfg, self.DC
        TT = c.TT
        self.psum_alloc(st, 8)
        T = self.alloc_row_tiles(st, TT)
        pos = self.sb(st, "pos", [128, DC, TT], F32)
        xt, h = T["xt"], T["h"]
        xkeys = [("xt", d) for d in range(DC)]
        hkeys = [("h", d) for d in range(DC)]
        tiles = [("ctx", 0, c.CTX)] + [("x", t0, TT) for t0 in range(0, c.L, TT)]
        rot = self.__dict__.setdefault("_rot", {})
        for (kind, t0, ntok) in tiles:
            r = 1 if kind == "ctx" else 0
            halves = self.halves_of(ntok)
            if kind == "ctx":
                self.dma("sp", xt[:, :, 0:ntok], self.ctxT.rearrange("(dc p) t -> p dc t", p=128), [], xkeys)
            else:
                self.dma("sp", xt[:, :, 0:ntok], self.xT[:, t0:t0 + ntok].rearrange("(dc p) t -> p dc t", p=128), [], xkeys)
                self.dma("sp", pos[:, :, 0:ntok], self.posT[:, t0:t0 + ntok].rearrange("(dc p) t -> p dc t", p=128), [], ["pos"])
                S.op("dve", lambda e, ntok=ntok: e.tensor_tensor(out=xt[:, :, 0:ntok], in0=xt[:, :, 0:ntok], in1=pos[:, :, 0:ntok], op=ALU.add),
                     reads=xkeys + ["pos"], writes=xkeys)
            self.rmsnorm_mod(T, ntok, 0, r)
            self.ffn(T, ntok, self.ffn1_up, self.ffn1_down, 0, r)
            if kind == "x":
                self.dma("sp", self.X1T[:, t0:t0 + ntok].rearrange("(dc p) t -> p dc t", p=128), xt[:, :, 0:ntok], xkeys, [("X1T", t0)])
            self.rmsnorm_mod(T, ntok, 1, r)
            ncols = c.DA if kind == "ctx" else c.INW
            sig_lo, sig_hi = c.DA // 128, 2 * c.DA // 128
            g_lo = (2 * c.DA + 3 * c.DB) // 128

            def evac(j, hi, c0, cw, ps, pk, kind=kind, t0=t0, ntok=ntok, halves=halves):
                si = rot.get("stg", 0)
                if hi == 0:
                    rot["stg"] = si + 1
                else:
                    si = si - 1
                stg = T["stg"][si % 4]
                sk = ("stg", si % 4)
                fn = AF.Sigmoid if ((sig_lo <= j < sig_hi) or j >= g_lo) else AF.Identity
                S.op("act", lambda e: e.activation(out=stg[:, c0:c0 + cw], in_=ps[:, 0:cw], func=fn), reads=[pk], writes=[sk])
                if hi == len(halves) - 1:
                    dst = self.CXM[j * 128:(j + 1) * 128, 0:ntok] if kind == "ctx" else self.PROJ[j * 128:(j + 1) * 128, t0:t0 + ntok]
                    self.dma("sp", dst, stg[:, 0:ntok], [sk], [("PROJ", kind, j, t0)])
            self.linear(self.w_in, DC, 0, ncols, lambda kc, c0, cw: (h[:, kc, c0:c0 + cw], hkeys), halves, evac, T["wA"], "wA", 512)


def _col(v, nchunks=None):
    v = np.asarray(v, np.float32)
    return np.ascontiguousarray(v.reshape(-1, 128).T)


def _blockdiag_tiles(w):
    nb = w.shape[0]
    n = nb * 4
    ac = n // 128
    out = np.zeros((128, ac, 128), np.float32)
    for a in range(ac):
        for bl in range(32):
            blk = w[a * 32 + bl]
            out[bl * 4:(bl + 1) * 4, a, bl * 4:(bl + 1) * 4] = blk
    return out


def const_tables(cfg):
    c = cfg
    L, D = c.L, c.D
    t = np.arange(L)
    r = (t // c.GRID_W).astype(np.float32)
    cc = (t % c.GRID_W).astype(np.float32)
    nf = D // 4
    omega = (1.0 / (10000.0 ** (np.arange(nf, dtype=np.float32) / nf))).astype(np.float32)

    def emb(p):
        a = p[:, None] * omega[None, :]
        return np.concatenate([np.sin(a), np.cos(a)], -1)
    pos = np.concatenate([emb(r), emb(cc)], -1).astype(np.float32)
    tl = np.linspace(0.0, 1.0, L, dtype=np.float32)[:, None]
    w = (2.0 * math.pi * np.arange(L, dtype=np.float32)[:, None] / L).astype(np.float32)
    f = np.linspace(1e-4, c.NB - 1, c.NB, dtype=np.float32)[None, :]
    z = np.concatenate([tl, np.cos(f * w), -np.sin(f * w)], -1).astype(np.float32)
    max_decay = math.log(1e-2) / 0.3
    min_decay = math.log(1e-2) / 1.5
    deltas = np.abs(np.linspace(min_decay, max_decay, c.DB, dtype=np.float32)).astype(np.float32)
    tn = np.ascontiguousarray(tl[:, 0].reshape(-1, 128).T)
    N = 2 * L
    idx = (np.outer(np.arange(L, dtype=np.int64), np.arange(L, dtype=np.int64)) % N).astype(np.float64)
    ang = 2.0 * math.pi * idx / N
    C = np.cos(ang)
    Sm = -np.sin(ang)
    NT = L // 128

    def blk(M):
        return np.ascontiguousarray(M.reshape(NT, 128, NT, 128).transpose(2, 1, 0, 3)).astype(ml_dtypes.bfloat16)
    alt = ((-1.0) ** np.arange(128)).astype(np.float32)
    altc = np.stack([alt, np.full(128, -math.pi, np.float32)], 1).astype(np.float32)
    return dict(posT=np.ascontiguousarray(pos.T), zT=np.ascontiguousarray(z.T), deltas=deltas[None, :].copy(), tn=tn,
                ctab=blk(C), stab=blk(Sm), altc=np.ascontiguousarray(altc), altr=alt[None, :].copy())


def prep_core_inputs(cfg, inp, b, tabs):
    c = cfg
    DC = c.D // 128
    m = {}
    m["xT"] = np.ascontiguousarray(np.asarray(inp["x"][b], np.float32).T)
    m["ctxT"] = np.ascontiguousarray(np.asarray(inp["ctx"][b], np.float32).T)
    cs = np.stack([_col(inp["c"][b]), _col(inp["c_ctx"])], -1)
    m["cs"] = np.ascontiguousarray(cs.astype(np.float32))
    m["w_ada"] = np.asarray(inp["w_ada"][0], np.float32)
    m["b_adac"] = _col(inp["b_ada"][0])
    m["normg"] = np.ascontiguousarray(np.stack([_col(inp["norm_g"][0][i]) for i in range(3)], 1))
    m["finalg"] = _col(inp["final_g"])
    for k in ("ffn1_up", "ffn1_down", "ffn2_up", "ffn2_down", "w_in", "w_pa", "w_pb", "w_out"):
        m[k] = np.asarray(inp[k][0], np.float32)
    m["a_convw"] = np.ascontiguousarray(np.stack([_col(inp["a_conv_w"][0][j]) for j in range(3)], 1))
    m["a_convb"] = _col(inp["a_conv_b"][0])
    m["bdq"] = _blockdiag_tiles(np.asarray(inp["a_wq"][0], np.float32))
    m["bdk"] = _blockdiag_tiles(np.asarray(inp["a_wk"][0], np.float32))
    m["bdv"] = _blockdiag_tiles(np.asarray(inp["a_wv"][0], np.float32))
    wg = np.asarray(inp["a_w_gate"][0], np.float32)
    m["wgate"] = np.ascontiguousarray(wg.reshape(-1, 128, 16).transpose(1, 0, 2))
    m["bgate"] = np.asarray(inp["a_b_gate"][0], np.float32)[None, :].copy()
    m["a_normg"] = _col(inp["a_norm_g"][0])
    m["a_skip"] = _col(inp["a_skip"][0])
    m["b_convw"] = np.ascontiguousarray(np.stack([_col(inp["b_conv_w"][0][j]) for j in range(3)], 1))
    m["b_convb"] = _col(inp["b_conv_b"][0])
    m["b_skip"] = _col(inp["b_skip"][0])
    m["fw1"] = np.asarray(inp["b_filt_w1"][0], np.float32)
    m["fw2"] = np.asarray(inp["b_filt_w2"][0], np.float32)
    m["fw3"] = np.asarray(inp["b_filt_w3"][0], np.float32)
    m["fw4"] = np.asarray(inp["b_filt_w4"][0], np.float32)
    m["fb"] = np.ascontiguousarray(np.stack([np.asarray(inp[k][0], np.float32) for k in
                                             ("b_filt_b1", "b_filt_b2", "b_filt_b3", "b_filt_freq")], 1))
    for k in ("posT", "zT", "deltas", "tn", "ctab", "stab", "altc", "altr"):
        m[k] = tabs[k]
    return m


def _phase_C(self, st):
    S, c, DC, AC, BC = self.S, self.cfg, self.DC, self.AC, self.BC
    TT = c.TT
    self.psum_alloc(st, 8)
    T = self.alloc_row_tiles(st, TT)
    g2 = self.sb(st, "g2", [128, DC, TT], BF16)
    mixt = self.sb(st, "mixt", [128, DC, TT], BF16)
    xt, h, sq, act = T["xt"], T["h"], T["sq"], T["act"]
    xkeys = [("xt", d) for d in range(DC)]
    rot = self.__dict__.setdefault("_rot", {})
    ga0 = 2 * c.DA + 3 * c.DB
    fv = lambda ap: ap.rearrange("(kc p) t -> p kc t", p=128)
    for t0 in range(0, c.L, TT):
        ntok = TT
        halves = self.halves_of(ntok)
        yak = [("act", f) for f in range(AC)]
        self.dma("sp", act[:, 0:AC, 0:ntok], fv(self.YA[:, t0:t0 + ntok]), [], yak)
        self.dma("sp", sq[:, 0:BC, 0:ntok], fv(self.YB[:, t0:t0 + ntok]), [], ["sq"])
        hk = [("h", d) for d in range(DC)]
        self.dma("sp", h[:, :, 0:ntok], fv(self.PROJ[ga0:ga0 + c.D, t0:t0 + ntok]), [], hk)
        self.dma("sp", g2[:, :, 0:ntok], fv(self.PROJ[ga0 + c.D:ga0 + 2 * c.D, t0:t0 + ntok]), [], ["g2"])
        self.dma("sp", xt[:, :, 0:ntok], fv(self.X1T[:, t0:t0 + ntok]), [], xkeys)

        def evac_a(j, hi, c0, cw, ps, pk):
            S.op("dve", lambda e: e.tensor_tensor(out=mixt[:, j, c0:c0 + cw], in0=h[:, j, c0:c0 + cw], in1=ps[:, 0:cw], op=ALU.mult),
                 reads=[pk, ("h", j)], writes=[("mixt", j)])
        self.linear(self.w_pa, AC, 0, c.D, lambda kc, c0, cw: (act[:, kc, c0:c0 + cw], yak), halves, evac_a, T["wB"], "wB", 256)

        def evac_b(j, hi, c0, cw, ps, pk):
            si = rot.get("sg", 0)
            rot["sg"] = si + 1
            sg = T["sg"][si % 2]
            sgk = ("sg", si % 2)
            S.op("dve", lambda e: e.tensor_tensor(out=sg[:, 0:cw], in0=g2[:, j, c0:c0 + cw], in1=ps[:, 0:cw], op=ALU.mult),
                 reads=[pk, "g2"], writes=[sgk])
            S.op("dve", lambda e: e.tensor_tensor(out=mixt[:, j, c0:c0 + cw], in0=mixt[:, j, c0:c0 + cw], in1=sg[:, 0:cw], op=ALU.add),
                 reads=[sgk, ("mixt", j)], writes=[("mixt", j)])
        self.linear(self.w_pb, BC, 0, c.D, lambda kc, c0, cw: (sq[:, kc, c0:c0 + cw], ["sq"]), halves, evac_b, T["wA"], "wA", 512)

        mk = [("mixt", d) for d in range(DC)]

        def evac_o(j, hi, c0, cw, ps, pk):
            scal = self.gt[:, 1, j, 0:1]
            S.op("dve", lambda e: e.scalar_tensor_tensor(out=xt[:, j, c0:c0 + cw], in0=ps[:, 0:cw], scalar=scal,
                                                         in1=xt[:, j, c0:c0 + cw], op0=ALU.mult, op1=ALU.add),
                 reads=[pk, ("xt", j), "gt"], writes=[("xt", j)])
        self.linear(self.w_out, DC, 0, c.D, lambda kc, c0, cw: (mixt[:, kc, c0:c0 + cw], mk), halves, evac_o, T["wA"], "wA", 512)
        self.rmsnorm_mod(T, ntok, 2, 0)
        self.ffn(T, ntok, self.ffn2_up, self.ffn2_down, 2, 0)
        self.rmsnorm_mod(T, ntok, 0, 0, final=True, out_f32=xt)
        self.dma("sp", fv(self.outT[:, t0:t0 + ntok]), xt[:, :, 0:ntok], [("of", d) for d in range(DC)] + xkeys, [("outT", t0)])


Prog.phase_C = _phase_C


def _phase_B1(self, st):
    S, c, nc = self.S, self.cfg, self.nc
    AC, KH, NTC, NCC, NT, LT, HD = self.AC, self.KH, self.NTC, self.NCC, self.NT, self.LT, c.HD
    HW = c.HW
    self.psum_alloc(st, 8)
    rot = self.__dict__.setdefault("_rot", {})
    cwc = self.sb(st, "cwc", [128, 3, AC], F32)
    cbc = self.sb(st, "cbc", [128, AC], F32)
    dg = self.sb(st, "dg", [128, 3, AC, 128], BF16)
    bd = [self.sb(st, f"bdt{n}", [128, AC, 128], BF16) for n in "qkv"]
    wg = self.sb(st, "wg", [128, 3 * AC, 16], BF16)
    gacc = self.sb(st, "gacc", [128, NTC, 16], F32)
    self.dma("sp", cwc[:], self.a_convw, [], ["cwc"])
    self.dma("sp", cbc[:], self.a_convb, [], ["cbc"])
    for n, src in zip(range(3), (self.bdq, self.bdk, self.bdv)):
        self.dma("pool", bd[n][:], src, [], [("bd", n)])
    self.dma("pool", wg[:], self.wgate, [], ["wg"])
    for j in range(3):
        for cc in range(AC):
            S.op("dve", lambda e, j=j, cc=cc: e.tensor_scalar(out=dg[:, j, cc, :], in0=self.idf[:], scalar1=cwc[:, j, cc:cc + 1],
                                                               scalar2=None, op0=ALU.mult), reads=["idf", "cwc"], writes=[("dg", cc)])
    xm = self.sb(st, "xm", [128, KH, LT + 4], BF16)
    xc = self.sb(st, "xc", [128, KH, LT], BF16)
    fT = [[self.sb(st, f"fT{n}{i}", [128, LT], BF16) for i in range(2)] for n in "qkv"]
    stg = [[self.sb(st, f"tm{n}{i}", [128, 4, HD], BF16) for i in range(2)] for n in "kv"]
    S.op("pool", lambda e: e.memset(xm[:], 0.0), writes=[("xm", k) for k in range(KH)])
    colx = c.CTX + 3
    ttiles = [(0, c.CTX, 1)] + [(c.CTX + t0, min(HW, c.L - t0), colx + t0) for t0 in range(0, c.L, HW)]
    first_g = True
    for hd in range(c.H):
        for kc in range(KH):
            cc = hd * KH + kc
            xk = ("xm", kc)
            self.dma("sp", xm[:, kc, 1:1 + c.CTX], self.CXM[cc * 128:(cc + 1) * 128, :], [], [xk])
            self.dma("sp", xm[:, kc, colx:colx + c.L], self.PROJ[cc * 128:(cc + 1) * 128, :], [], [xk])
            bi = rot.get("fT", 0)
            rot["fT"] = bi + 1
            qT, kT, vT = fT[0][bi % 2], fT[1][bi % 2], fT[2][bi % 2]
            fk = [("fT", n, bi % 2) for n in range(3)]
            for (tau0, n, col) in ttiles:
                ps, pk = self.psn()
                for j in range(3):
                    self.mm(ps[:, 0:n], dg[:, j, cc, :], xm[:, kc, col + j - 1:col + j - 1 + n], j == 0, j == 2, [("dg", cc), xk], [pk])
                S.op("act", lambda e, ps=ps, n=n, tau0=tau0, kc=kc, cc=cc: e.activation(
                    out=xc[:, kc, tau0:tau0 + n], in_=ps[:, 0:n], func=AF.Silu, bias=cbc[:, cc:cc + 1], scale=1.0),
                    reads=[pk, "cbc"], writes=[("xc", kc)])
                for ni, (dst, src_ap, srck) in enumerate(((qT, xc[:, kc, tau0:tau0 + n], ("xc", kc)),
                                                          (kT, xc[:, kc, tau0:tau0 + n], ("xc", kc)),
                                                          (vT, xm[:, kc, col:col + n], xk))):
                    ps2, pk2 = self.psn()
                    self.mm(ps2[:, 0:n], bd[ni][:, cc, :], src_ap, True, True, [("bd", ni), srck], [pk2])
                    if ni == 1:
                        S.op("act", lambda e, ps2=ps2, n=n, tau0=tau0, dst=dst: e.activation(out=dst[:, tau0:tau0 + n], in_=ps2[:, 0:n], func=AF.Identity),
                             reads=[pk2], writes=[fk[ni]])
                    else:
                        S.op("dve", lambda e, ps2=ps2, n=n, tau0=tau0, dst=dst: e.tensor_copy(out=dst[:, tau0:tau0 + n], in_=ps2[:, 0:n]),
                             reads=[pk2], writes=[fk[ni]])
            g0 = 0
            while g0 < NTC:
                gn = min(32, NTC - g0)
                ps, pk = self.psn()
                for tci in range(gn):
                    tc = g0 + tci
                    for ni, src in enumerate((qT, kT, vT)):
                        self.mm(ps[:, tci * 16:(tci + 1) * 16], src[:, tc * 128:(tc + 1) * 128], wg[:, ni * AC + cc, :],
                                ni == 0, ni == 2, [fk[ni], "wg"], [pk])
                if first_g:
                    S.op("dve", lambda e, ps=ps, g0=g0, gn=gn: e.tensor_copy(out=gacc[:, g0:g0 + gn, :], in_=ps[:, 0:gn * 16].rearrange("p (a b) -> p a b", b=16)),
                         reads=[pk], writes=["gacc"])
                else:
                    S.op("dve", lambda e, ps=ps, g0=g0, gn=gn: e.tensor_tensor(out=gacc[:, g0:g0 + gn, :], in0=gacc[:, g0:g0 + gn, :],
                                                                               in1=ps[:, 0:gn * 16].rearrange("p (a b) -> p a b", b=16), op=ALU.add),
                         reads=[pk, "gacc"], writes=["gacc"])
                g0 += gn
            first_g = False
            self.dma("sp", self.XC[cc * 128:(cc + 1) * 128, :], xc[:, kc, c.CTX:], [("xc", kc)], [("XC", cc)])
            self.dma("sp", self.QT[cc * 128:(cc + 1) * 128, :], qT[:, c.CTX:], [fk[0]], [("QT", cc)])
            self.dma("sp", self.KT[cc * 128:(cc + 1) * 128, :], kT[:, :], [fk[1]], [("KT", cc)])
        for g0 in range(0, NTC, 4):
            gn = min(4, NTC - g0)
            bi = rot.get("tm", 0)
            rot["tm"] = bi + 1
            for ni, (srct, srckey, dstD) in enumerate(((xc, "xc", self.KTM), (xm, "xm", self.VTM))):
                sg = stg[ni][bi % 2]
                sk = ("tm", ni, bi % 2)
                for tci in range(gn):
                    tc = g0 + tci
                    ps, pk = self.psn()
                    for kc in range(KH):
                        cc = hd * KH + kc
                        if ni == 0:
                            lhs = xc[:, kc, tc * 128:(tc + 1) * 128]
                        else:
                            colt = (1 + tc * 128) if tc < NCC else (colx + (tc - NCC) * 128)
                            lhs = xm[:, kc, colt:colt + 128]
                        self.mm(ps[:, kc * 128:(kc + 1) * 128], lhs, bd[1 + ni][:, cc, :], True, True,
                                [(srckey, kc), ("bd", 1 + ni)], [pk])
                    if ni == 0:
                        S.op("act", lambda e, ps=ps, sg=sg, tci=tci: e.activation(out=sg[:, tci, :], in_=ps[:, 0:HD], func=AF.Identity),
                             reads=[pk], writes=[sk])
                    else:
                        S.op("dve", lambda e, ps=ps, sg=sg, tci=tci: e.tensor_copy(out=sg[:, tci, :], in_=ps[:, 0:HD]),
                             reads=[pk], writes=[sk])
                self.dma("sp", dstD[g0 * 128:(g0 + gn) * 128, hd * HD:(hd + 1) * HD].rearrange("(tc p) ch -> p tc ch", p=128),
                         sg[:, 0:gn, :], [sk], [("TM", ni, hd, g0)])
    self.gate_prologue(st, gacc)


Prog.phase_B1 = _phase_B1


def _gate_prologue(self, st, gacc):
    S, c = self.S, self.cfg
    P, H, NCC = self.NTC, c.H, self.NCC
    shp = [128, H, P]
    names = ["ig", "fg", "l", "bl", "tt", "pfx", "aloc", "e2", "aa", "gg", "pg", "tmp"]
    t = {n: self.sb(st, "gp_" + n, shp, F32) for n in names}
    bgb = self.sb(st, "bgb", [128, 16], F32)
    ones_hp = self.sb(st, "ones_hp", [128, P], F32)
    self.dma("sp", bgb[:], self.bgate.partition_broadcast(128), [], ["bgb"])
    S.op("pool", lambda e: e.memset(ones_hp[:], 1.0), writes=["ones_hp"])
    flat = lambda a: a[:].rearrange("p h c -> p (h c)")
    one = self.onesf[:, 0:1]

    def tt_op(eng, out, a, b, op, r, w):
        S.op(eng, lambda e: e.tensor_tensor(out=out, in0=a, in1=b, op=op), reads=r, writes=w)

    for d in range(2):
        ci = 0 if d == 0 else 8
        for nm, col in (("ig", ci), ("fg", ci + 4)):
            bias = bgb[:, col:col + 4]
            if d == 0:
                src = gacc[:, :, col:col + 4].rearrange("p c g -> p g c")
                tt_op("dve", t[nm][:], src, bias.unsqueeze(2).broadcast_to(shp), ALU.add, ["gacc", "bgb"], [nm])
            else:
                src1 = gacc[:, NCC - 1::-1, col:col + 4].rearrange("p c g -> p g c")
                tt_op("dve", t[nm][:, :, 0:NCC], src1, bias.unsqueeze(2).broadcast_to([128, H, NCC]), ALU.add, ["gacc", "bgb"], [nm])
                src2 = gacc[:, P - 1:NCC - 1:-1, col:col + 4].rearrange("p c g -> p g c")
                tt_op("dve", t[nm][:, :, NCC:P], src2, bias.unsqueeze(2).broadcast_to([128, H, P - NCC]), ALU.add, ["gacc", "bgb"], [nm])
        S.op("act", lambda e: e.activation(out=t["e2"][:], in_=t["fg"][:], func=AF.Exp, scale=-1.0), reads=["fg"], writes=["e2"])
        S.op("act", lambda e: e.activation(out=t["l"][:], in_=t["e2"][:], func=AF.Ln, bias=one, scale=1.0), reads=["e2", "onesf"], writes=["l"])
        tri = self.triu_f if d == 0 else self.tril_f
        ps, pk = self.psn()
        self.mm(ps[:, 0:H * P], tri[:], flat(t["l"]), True, True, ["l", "triu_f", "tril_f"], [pk])
        S.op("dve", lambda e, ps=ps: e.tensor_scalar(out=flat(t["bl"]), in0=ps[:, 0:H * P], scalar1=-1.0, scalar2=None, op0=ALU.mult),
             reads=[pk], writes=["bl"])
        ps2, pk2 = self.psn()
        self.mm(ps2[:, 0:H * P], self.onesf[:], flat(t["l"]), True, True, ["l", "onesf"], [pk2])
        S.op("dve", lambda e, ps2=ps2: e.tensor_scalar(out=flat(t["tt"]), in0=ps2[:, 0:H * P], scalar1=-1.0, scalar2=None, op0=ALU.mult),
             reads=[pk2], writes=["tt"])
        for h in range(H):
            S.op("dve", lambda e, h=h: e.tensor_tensor_scan(out=t["tmp"][:, h, :], data0=ones_hp[:], data1=t["tt"][:, h, :], initial=0.0,
                                                            op0=ALU.mult, op1=ALU.add), reads=["tt", "ones_hp"], writes=["tmp"])
        tt_op("dve", t["pfx"][:], t["tmp"][:], t["tt"][:], ALU.subtract, ["tmp", "tt"], ["pfx"])
        tt_op("dve", t["aloc"][:], t["ig"][:], t["bl"][:], ALU.subtract, ["ig", "bl"], ["aloc"])
        S.op("dve", lambda e: e.tensor_scalar(out=t["e2"][:], in0=t["aloc"][:], scalar1=60.0, scalar2=None, op0=ALU.min),
             reads=["aloc"], writes=["e2"])
        S.op("act", lambda e: e.activation(out=t["e2"][:], in_=t["e2"][:], func=AF.Exp), reads=["e2"], writes=["e2"])
        ps3, pk3 = self.psn()
        self.mm(ps3[:, 0:H * P], self.onesf[:], flat(t["e2"]), True, True, ["e2", "onesf"], [pk3])
        S.op("act", lambda e, ps3=ps3: e.activation(out=flat(t["aa"]), in_=ps3[:, 0:H * P], func=AF.Ln), reads=[pk3], writes=["aa"])
        tt_op("dve", t["aa"][:], t["aa"][:], t["pfx"][:], ALU.subtract, ["aa", "pfx"], ["aa"])
        for h in range(H):
            S.op("dve", lambda e, h=h: e.tensor_tensor_scan(out=t["gg"][:, h, :], data0=ones_hp[:], data1=t["aa"][:, h, :], initial=-1e30,
                                                            op0=ALU.mult, op1=ALU.max), reads=["aa", "ones_hp"], writes=["gg"])
        tt_op("dve", t["pg"][:], t["pfx"][:], t["gg"][:], ALU.add, ["pfx", "gg"], ["pg"])
        tt_op("dve", t["tmp"][:], t["aloc"][:], t["pg"][:], ALU.subtract, ["aloc", "pg"], ["tmp"])
        S.op("act", lambda e, d=d: e.activation(out=self.gw[d][:], in_=t["tmp"][:], func=AF.Exp), reads=["tmp"], writes=[("gw", d)])
        tt_op("dve", t["tmp"][:], t["bl"][:], t["pg"][:], ALU.add, ["bl", "pg"], ["tmp"])
        S.op("act", lambda e, d=d: e.activation(out=self.gcl[d][:], in_=t["tmp"][:], func=AF.Exp, scale=-1.0), reads=["tmp"], writes=[("gcl", d)])
        S.op("dve", lambda e: e.tensor_copy(out=t["tmp"][:, :, 1:P], in_=t["gg"][:, :, 0:P - 1]), reads=["gg"], writes=["tmp"])
        S.op("dve", lambda e: e.tensor_copy(out=t["tmp"][:, :, 0:1], in_=t["gg"][:, :, 0:1]), reads=["gg"], writes=["tmp"])
        tt_op("dve", t["tmp"][:], t["tmp"][:], t["gg"][:], ALU.subtract, ["tmp", "gg"], ["tmp"])
        S.op("act", lambda e, d=d: e.activation(out=self.gr[d][:], in_=t["tmp"][:], func=AF.Exp), reads=["tmp"], writes=[("gr", d)])
        S.op("dve", lambda e, d=d: e.tensor_scalar(out=self.grq[d][:], in0=self.gr[d][:], scalar1=float(c.HD) ** -0.5, scalar2=None, op0=ALU.mult),
             reads=[("gr", d)], writes=[("grq", d)])
    self.dbg_gates = t


Prog.gate_prologue = _gate_prologue


def _phase_B2(self, st):
    S, c, nc = self.S, self.cfg, self.nc
    KH, NTC, NCC, NT, LT, HD, H = self.KH, self.NTC, self.NCC, self.NT, self.LT, c.HD, c.H
    pst = lambda n: st.enter_context(nc.psum_tensor(f"{n}_{self.u()[1]}", [128, 512], F32))
    PC = [pst(f"b2C{k}") for k in range(2)]
    PN = [pst(f"b2N{d}") for d in range(2)]
    PM = [pst(f"b2M{d}") for d in range(2)]
    PS = [pst(f"b2S{d}") for d in range(2)]
    QTt = self.sb(st, "QTt", [128, KH, c.L], BF16)
    KTt = self.sb(st, "KTt", [128, KH, LT], BF16)
    Ktm = self.sb(st, "Ktm", [128, NTC, HD], BF16)
    Vtm = self.sb(st, "Vtm", [128, NTC, HD], BF16)
    C32 = [self.sb(st, f"C32{d}", [128, KH, HD], F32) for d in range(2)]
    C16 = [[self.sb(st, f"C16{d}{p}", [128, KH, HD], BF16) for p in range(2)] for d in range(2)]
    n32 = [self.sb(st, f"n32{d}", [128, KH], F32) for d in range(2)]
    n16 = [[self.sb(st, f"n16{d}{p}", [128, KH], BF16) for p in range(2)] for d in range(2)]
    msk = [self.sb(st, f"msk{d}", [128, 128], F32) for d in range(2)]
    SD = [[self.sb(st, f"SD{d}{i}", [128, 128], BF16) for i in range(2)] for d in range(2)]
    qr = [[self.sb(st, f"qr{d}{i}", [128, KH, 128], BF16) for i in range(2)] for d in range(2)]
    kw = [[self.sb(st, f"kw{d}{i}", [128, HD], BF16) for i in range(2)] for d in range(2)]
    dn = [[self.sb(st, f"dn{d}{i}", [128, 2], F32) for i in range(2)] for d in range(2)]
    hfs = [[self.sb(st, f"hfs{d}{i}", [128, HD], F32) for i in range(2)] for d in range(2)]
    qs = float(HD) ** -0.5
    for d, tri in enumerate((self.triu_f, self.tril_f)):
        S.op("dve", lambda e, d=d, tri=tri: e.tensor_scalar(out=msk[d][:], in0=tri[:], scalar1=qs, scalar2=None, op0=ALU.mult),
             reads=["triu_f", "tril_f"], writes=[("msk", d)])
    order = [list(range(NTC)), list(range(NCC - 1, -1, -1)) + list(range(NTC - 1, NCC - 1, -1))]
    one_col = self.onesb[:, 0:1]
    HOUT = [self.HF, self.HB]
    pcrot = [0]

    def gkeys(d):
        return [("gw", d), ("gr", d), ("grq", d), ("gcl", d)]

    def is_x(d, i):
        return i < NTC and order[d][i] >= NCC

    def pre_act(hd, d, i):
        if i >= NTC:
            return
        sc = order[d][i]
        b = i % 2
        if i < NTC - 1:
            wc = self.gw[d][:, hd, i:i + 1]
            S.op("act", lambda e: e.activation(out=kw[d][b][:], in_=Ktm[:, sc, :], func=AF.Copy, scale=wc),
                 reads=["Ktm"] + gkeys(d), writes=[("kw", d, b)])
        if is_x(d, i):
            xi = sc - NCC
            rqc = self.grq[d][:, hd, i:i + 1]
            S.op("act", lambda e: e.activation(out=qr[d][b][:], in_=QTt[:, :, xi * 128:(xi + 1) * 128], func=AF.Copy, scale=rqc),
                 reads=["QTt"] + gkeys(d), writes=[("qr", d, b)])

    def pre_scores(hd, d, i):
        if not is_x(d, i):
            return
        sc = order[d][i]
        xi = sc - NCC
        b = i % 2
        wc = self.gw[d][:, hd, i:i + 1]
        for kc in range(KH):
            self.mm(PS[d][:, 0:128], KTt[:, kc, sc * 128:(sc + 1) * 128], QTt[:, kc, xi * 128:(xi + 1) * 128],
                    kc == 0, kc == KH - 1, ["KTt", "QTt"], [("psS", d)])
        S.op("dve", lambda e: e.scalar_tensor_tensor(out=SD[d][b][:], in0=PS[d][:, 0:128], scalar=wc, in1=msk[d][:], op0=ALU.mult, op1=ALU.mult),
             reads=[("psS", d), ("msk", d)] + gkeys(d), writes=[("SD", d, b)])

    def state_pe(hd, d, i):
        if i >= NTC - 1:
            return []
        sc = order[d][i]
        b = i % 2
        banks = []
        for kc in range(KH):
            pi = pcrot[0] % 2
            pcrot[0] += 1
            banks.append(pi)
            self.mm(PC[pi][:, 0:HD], kw[d][b][:, kc * 128:(kc + 1) * 128], Vtm[:, sc, :], True, True, [("kw", d, b), "Vtm"], [("psC", pi)])
            self.mm(PM[d][:, kc:kc + 1], kw[d][b][:, kc * 128:(kc + 1) * 128], one_col, True, True, [("kw", d, b), "onesb"], [("psM", d)])
            rc_ = self.gr[d][:, hd, i:i + 1]
            S.op("dve", lambda e, kc=kc, pi=pi, rc_=rc_: e.scalar_tensor_tensor(
                out=C32[d][:, kc, :], in0=C32[d][:, kc, :], scalar=rc_, in1=PC[pi][:, 0:HD], op0=ALU.mult, op1=ALU.add),
                reads=[("psC", pi), ("C32", d, kc)] + gkeys(d), writes=[("C32", d, kc)])
        return banks

    def state_post(hd, d, i):
        if i >= NTC - 1:
            return
        nxt = (i + 1) % 2
        rc_ = self.gr[d][:, hd, i:i + 1]
        hk = KH // 2 if KH >= 2 else KH
        S.op("act", lambda e: e.activation(out=C16[d][nxt][:, 0:hk, :], in_=C32[d][:, 0:hk, :], func=AF.Copy),
             reads=[("C32", d, k) for k in range(hk)], writes=[("C16", d, nxt, 0)])
        if hk < KH:
            S.op("dve", lambda e: e.tensor_copy(out=C16[d][nxt][:, hk:KH, :], in_=C32[d][:, hk:KH, :]),
                 reads=[("C32", d, k) for k in range(hk, KH)], writes=[("C16", d, nxt, 1)])
        S.op("dve", lambda e: e.scalar_tensor_tensor(out=n32[d][:], in0=n32[d][:], scalar=rc_, in1=PM[d][:, 0:KH], op0=ALU.mult, op1=ALU.add),
             reads=[("psM", d), ("n32", d)] + gkeys(d), writes=[("n32", d)])
        S.op("dve", lambda e: e.tensor_copy(out=n16[d][nxt][:], in_=n32[d][:]), reads=[("n32", d)], writes=[("n16", d, nxt)])

    def read_pe(hd, d, i):
        if not is_x(d, i):
            return
        sc = order[d][i]
        b, cur = i % 2, i % 2
        ck = [("C16", d, cur, 0), ("C16", d, cur, 1)]
        self.mm(PN[d][:, 0:HD], SD[d][b][:], Vtm[:, sc, :], True, False, [("SD", d, b), "Vtm"], [("psN", d)])
        for kc in range(KH):
            self.mm(PN[d][:, 0:HD], qr[d][b][:, kc, :], C16[d][cur][:, kc, :], False, kc == KH - 1, [("qr", d, b)] + ck, [("psN", d)])
        self.mm(PS[d][:, 128:129], SD[d][b][:], one_col, True, False, [("SD", d, b), "onesb"], [("psS", d)])
        for kc in range(KH):
            self.mm(PS[d][:, 128:129], qr[d][b][:, kc, :], n16[d][cur][:, kc:kc + 1], False, kc == KH - 1, [("qr", d, b), ("n16", d, cur)], [("psS", d)])

    def read_post(hd, d, i):
        if not is_x(d, i):
            return
        sc = order[d][i]
        xi = sc - NCC
        r0 = hd * HD
        b = i % 2
        clc = self.gcl[d][:, hd, i:i + 1]
        S.op("act", lambda e: e.activation(out=dn[d][b][:, 0:1], in_=PS[d][:, 128:129], func=AF.Abs), reads=[("psS", d)], writes=[("dn", d, b)])
        S.op("dve", lambda e: e.tensor_scalar(out=dn[d][b][:, 0:1], in0=dn[d][b][:, 0:1], scalar1=clc, scalar2=None, op0=ALU.max),
             reads=[("dn", d, b)] + gkeys(d), writes=[("dn", d, b)])
        S.op("dve", lambda e: e.reciprocal(out=dn[d][b][:, 1:2], in_=dn[d][b][:, 0:1]), reads=[("dn", d, b)], writes=[("dn", d, b)])
        S.op("act", lambda e: e.activation(out=hfs[d][b][:], in_=PN[d][:, 0:HD], func=AF.Copy, scale=dn[d][b][:, 1:2]),
             reads=[("psN", d), ("dn", d, b)], writes=[("hfs", d, b)])
        self.dma("sp", HOUT[d][xi * 128:(xi + 1) * 128, r0:r0 + HD], hfs[d][b][:], [("hfs", d, b)], [("HFB", d, hd, xi)])

    for hd in range(H):
        r0 = hd * HD
        self.dma("sp", QTt[:], self.QT[r0:r0 + HD, :].rearrange("(kc p) t -> p kc t", p=128), [], ["QTt"])
        self.dma("sp", KTt[:], self.KT[r0:r0 + HD, :].rearrange("(kc p) t -> p kc t", p=128), [], ["KTt"])
        self.dma("sp", Ktm[:], self.KTM[:, r0:r0 + HD].rearrange("(tc p) ch -> p tc ch", p=128), [], ["Ktm"])
        self.dma("sp", Vtm[:], self.VTM[:, r0:r0 + HD].rearrange("(tc p) ch -> p tc ch", p=128), [], ["Vtm"])
        for d in range(2):
            S.op("pool", lambda e, d=d: e.memset(C32[d][:], 0.0), writes=[("C32", d, k) for k in range(KH)])
            S.op("pool", lambda e, d=d: e.memset(C16[d][0][:], 0.0), writes=[("C16", d, 0, 0), ("C16", d, 0, 1)])
            S.op("pool", lambda e, d=d: e.memset(n32[d][:], 0.0), writes=[("n32", d)])
            S.op("pool", lambda e, d=d: e.memset(n16[d][0][:], 0.0), writes=[("n16", d, 0)])
        for d in range(2):
            pre_act(hd, d, 0)
        for d in range(2):
            pre_scores(hd, d, 0)
        for i in range(NTC):
            for d in range(2):
                pre_act(hd, d, i + 1)
            for d in range(2):
                state_pe(hd, d, i)
                read_pe(hd, d, i)
                state_post(hd, d, i)
            for d in range(2):
                read_post(hd, d, i)
            for d in range(2):
                pre_scores(hd, d, i + 1)


Prog.phase_B2 = _phase_B2


def _phase_B4(self, st):
    S, c = self.S, self.cfg
    AC, HW = self.AC, c.HW
    self.psum_alloc(st, 8)
    ntc = HW // 128
    gnc = self.sb(st, "gnc", [128, AC], F32)
    skc = self.sb(st, "skc", [128, AC], F32)
    self.dma("sp", gnc[:], self.a_normg, [], ["gnc"])
    self.dma("sp", skc[:], self.a_skip, [], ["skc"])
    hnt = [self.sb(st, f"hnt{i}", [128, ntc, c.DA], BF16) for i in range(2)]
    xct = [self.sb(st, f"xct{i}", [128, AC, HW], BF16) for i in range(2)]
    szt = [self.sb(st, f"szt{i}", [128, AC, HW], BF16) for i in range(2)]
    yat = [self.sb(st, f"yat{i}", [128, AC, HW], BF16) for i in range(2)]
    xs = [self.sb(st, f"xs{i}", [128, HW], F32) for i in range(2)]
    u1 = [self.sb(st, f"u1{i}", [128, HW], F32) for i in range(2)]
    hft = [self.sb(st, f"hft{i}", [128, c.DA], F32) for i in range(2)]
    hbt = [self.sb(st, f"hbt{i}", [128, c.DA], F32) for i in range(2)]
    junk4 = self.sb(st, "junk4", [128, c.HD], BF16)
    ss4 = [self.sb(st, f"ss4{i}", [128, 2 * c.H], F32) for i in range(2)]
    rot4 = {}
    fv = lambda ap: ap.rearrange("(kc p) t -> p kc t", p=128)
    for ti, t0 in enumerate(range(0, c.L, HW)):
        b = ti % 2
        for tci in range(ntc):
            tok0 = t0 + tci * 128
            pi_ = rot4.get("hf", 0)
            rot4["hf"] = pi_ + 1
            pb = pi_ % 2
            self.dma("sp", hft[pb][:], self.HF[tok0:tok0 + 128, :], [], [("hft", pb)])
            self.dma("sp", hbt[pb][:], self.HB[tok0:tok0 + 128, :], [], [("hbt", pb)])
            S.op("dve", lambda e, pb=pb: e.tensor_tensor(out=hft[pb][:], in0=hft[pb][:], in1=hbt[pb][:], op=ALU.add),
                 reads=[("hft", pb), ("hbt", pb)], writes=[("hft", pb)])
            for hh in range(c.H):
                S.op("act", lambda e, pb=pb, hh=hh: e.activation(out=junk4[:], in_=hft[pb][:, hh * c.HD:(hh + 1) * c.HD], func=AF.Square,
                                                                 accum_out=ss4[pb][:, hh:hh + 1]), reads=[("hft", pb)], writes=["junk4", ("ss4", pb)])
            S.op("act", lambda e, pb=pb: e.activation(out=ss4[pb][:, c.H:2 * c.H], in_=ss4[pb][:, 0:c.H], func=AF.Sqrt, bias=self.epsD[:, 1:2], scale=1.0 / c.HD),
                 reads=[("ss4", pb), "epsD"], writes=[("ss4", pb)])
            S.op("dve", lambda e, pb=pb: e.reciprocal(out=ss4[pb][:, c.H:2 * c.H], in_=ss4[pb][:, c.H:2 * c.H]), reads=[("ss4", pb)], writes=[("ss4", pb)])
            S.op("dve", lambda e, pb=pb, b=b, tci=tci: e.tensor_tensor(
                out=hnt[b][:, tci, :].rearrange("p (h e) -> p h e", h=c.H), in0=hft[pb][:].rearrange("p (h e) -> p h e", h=c.H),
                in1=ss4[pb][:, c.H:2 * c.H].unsqueeze(2).broadcast_to([128, c.H, c.HD]), op=ALU.mult),
                reads=[("hft", pb), ("ss4", pb)], writes=[("hnt", b)])
        self.dma("sp", xct[b][:], fv(self.XC[:, t0:t0 + HW]), [], [("xct", b)])
        self.dma("sp", szt[b][:], fv(self.PROJ[c.DA:2 * c.DA, t0:t0 + HW]), [], [("szt", b)])
        for cc in range(AC):
            ps, pk = self.psn()
            psb = ps[:].bitcast(BF16)
            for tc in range(ntc):
                S.op("pe", lambda e, psb=psb, tc=tc, cc=cc, b=b: e.transpose(out=psb[:, tc * 128:(tc + 1) * 128],
                                                                         in_=hnt[b][:, tc, cc * 128:(cc + 1) * 128], identity=self.idb[:]),
                     reads=[("hnt", b), "idb"], writes=[pk])
            j = cc % 2
            S.op("act", lambda e, cc=cc, b=b, j=j: e.activation(out=xs[j][:], in_=xct[b][:, cc, :], func=AF.Copy, scale=skc[:, cc:cc + 1]),
                 reads=[("xct", b), "skc"], writes=[("xs", j)])
            S.op("dve", lambda e, psb=psb, cc=cc, j=j: e.scalar_tensor_tensor(out=u1[j][:], in0=psb[:, 0:HW], scalar=gnc[:, cc:cc + 1], in1=xs[j][:],
                                                                             op0=ALU.mult, op1=ALU.add), reads=[pk, ("xs", j), "gnc"], writes=[("u1", j)])
            S.op("dve", lambda e, cc=cc, b=b, j=j: e.tensor_tensor(out=yat[b][:, cc, :], in0=u1[j][:], in1=szt[b][:, cc, :], op=ALU.mult),
                 reads=[("u1", j), ("szt", b)], writes=[("yat", b)])
        self.dma("sp", fv(self.YA[:, t0:t0 + HW]), yat[b][:], [("yat", b)], [("YA", t0)])


Prog.phase_B4 = _phase_B4


def _phase_B3(self, st):
    S, c = self.S, self.cfg
    NT, L, CB, HW, FH, DB = self.NT, c.L, c.CB, c.HW, c.FH, c.DB
    CBC = CB // 128
    NBLK = DB // CB
    ntc = HW // 128
    N2 = 2 * L
    self.psum_alloc(st, 8)
    rot = self.__dict__.setdefault("_rot", {})
    sbt = lambda n, shp, dt: self.sb(st, n, shp, dt)
    hy0 = 2 * c.DA
    cw = sbt("h_cw", [128, 3, 3 * self.BC], F32)
    cbv = sbt("h_cb", [128, 3 * self.BC], F32)
    dsk = sbt("h_dsk", [128, self.BC], F32)
    altc = sbt("h_altc", [128, 2], F32)
    altcb = sbt("h_altcb", [128, 1], BF16)
    altrf = sbt("h_altrf", [1, 128], F32)
    altrb = sbt("h_altrb", [1, 128], BF16)
    tnc = sbt("h_tn", [128, NT], F32)
    dlb = sbt("h_dl", [128, DB], F32)
    fbt = sbt("h_fb", [FH, 4], F32)
    fbb = sbt("h_fbb", [FH, 3], F32)
    aA = sbt("h_aA", [FH, L], F32)
    st_mlp = ExitStack()
    sbm = lambda n, shp, dt: self.sb(st_mlp, n, shp, dt)
    w1 = sbm("h_w1", [c.FE, FH], F32)
    w2 = sbm("h_w2", [FH, FH], F32)
    w3 = sbm("h_w3", [FH, FH], F32)
    zt = sbm("h_zt", [c.FE, L], F32)
    aB = sbm("h_aB", [FH, L], F32)
    for dst, src, k in ((cw, self.b_convw, "h_cw"), (cbv, self.b_convb, "h_cb"), (dsk, self.b_skip, "h_dsk"), (altc, self.altc, "h_altc"),
                        (altrf, self.altr, "h_altrf"), (tnc, self.tn, "h_tn"), (fbt, self.fb, "h_fb"), (w1, self.fw1, "h_w1"),
                        (w2, self.fw2, "h_w2"), (w3, self.fw3, "h_w3"), (zt, self.zT, "h_zt")):
        self.dma("sp", dst[:], src, [], [k])
    self.dma("sp", dlb[:], self.deltas.partition_broadcast(128), [], ["h_dl"])
    S.op("dve", lambda e: e.tensor_copy(out=altcb[:], in_=altc[:, 0:1]), reads=["h_altc"], writes=["h_altcb"])
    S.op("dve", lambda e: e.tensor_copy(out=altrb[:], in_=altrf[:]), reads=["h_altrf"], writes=["h_altrb"])
    S.op("dve", lambda e: e.tensor_scalar(out=tnc[:], in0=tnc[:], scalar1=-1.0, scalar2=None, op0=ALU.mult), reads=["h_tn"], writes=["h_tn"])
    for l in range(3):
        S.op("dve", lambda e, l=l: e.tensor_scalar(out=fbb[:, l:l + 1], in0=fbt[:, l:l + 1], scalar1=fbt[:, 3:4], scalar2=0.0,
                                                    op0=ALU.mult, op1=ALU.add), reads=["h_fb"], writes=["h_fbb"])
    tmpa = [sbm(f"h_tmpa{i}", [FH, HW], F32) for i in range(2)]
    tmpk = [sbm(f"h_tmpk{i}", [FH, HW], F32) for i in range(2)]
    tmpm = [sbm(f"h_tmpm{i}", [FH, HW], F32) for i in range(2)]
    tmpi = [sbm(f"h_tmpi{i}", [FH, HW], mybir.dt.int32) for i in range(2)]
    src_t, srck = zt, "h_zt"
    for l, (wl, wk, dst, dk) in enumerate(((w1, "h_w1", aA, "h_aA"), (w2, "h_w2", aB, "h_aB"), (w3, "h_w3", aA, "h_aA"))):
        for t0 in range(0, L, HW):
            ps, pk = self.psn()
            self.mm(ps[0:FH, 0:HW], wl[:], src_t[:, t0:t0 + HW], True, True, [wk, srck], [pk])
            j = (t0 // HW) % 2
            S.op("dve", lambda e, ps=ps, j=j, l=l: e.tensor_scalar(out=tmpa[j][:], in0=ps[0:FH, 0:HW], scalar1=fbt[:, 3:4], scalar2=fbb[:, l:l + 1],
                                                                   op0=ALU.mult, op1=ALU.add), reads=[pk, "h_fb", "h_fbb"], writes=[("h_tmpa", j)])
            tk = ("h_tmpa", j)
            S.op("dve", lambda e, j=j: e.tensor_scalar(out=tmpk[j][:], in0=tmpa[j][:], scalar1=1.0 / (2.0 * math.pi), scalar2=None, op0=ALU.mult),
                 reads=[tk], writes=[("h_tmpk", j)])
            S.op("dve", lambda e, j=j: e.tensor_copy(out=tmpi[j][:], in_=tmpk[j][:]), reads=[("h_tmpk", j)], writes=[("h_tmpi", j)])
            S.op("dve", lambda e, j=j: e.tensor_copy(out=tmpk[j][:], in_=tmpi[j][:]), reads=[("h_tmpi", j)], writes=[("h_tmpk", j)])
            S.op("dve", lambda e, j=j: e.scalar_tensor_tensor(out=tmpa[j][:], in0=tmpk[j][:], scalar=-2.0 * math.pi, in1=tmpa[j][:], op0=ALU.mult, op1=ALU.add),
                 reads=[tk, ("h_tmpk", j)], writes=[tk])
            S.op("dve", lambda e, j=j: e.tensor_scalar(out=tmpk[j][:], in0=tmpa[j][:], scalar1=math.pi, scalar2=None, op0=ALU.is_gt),
                 reads=[tk], writes=[("h_tmpk", j)])
            S.op("dve", lambda e, j=j: e.tensor_scalar(out=tmpm[j][:], in0=tmpa[j][:], scalar1=-math.pi, scalar2=None, op0=ALU.is_lt),
                 reads=[tk], writes=[("h_tmpm", j)])
            S.op("dve", lambda e, j=j: e.scalar_tensor_tensor(out=tmpa[j][:], in0=tmpk[j][:], scalar=-2.0 * math.pi, in1=tmpa[j][:], op0=ALU.mult, op1=ALU.add),
                 reads=[tk, ("h_tmpk", j)], writes=[tk])
            S.op("dve", lambda e, j=j: e.scalar_tensor_tensor(out=tmpa[j][:], in0=tmpm[j][:], scalar=2.0 * math.pi, in1=tmpa[j][:], op0=ALU.mult, op1=ALU.add),
                 reads=[tk, ("h_tmpm", j)], writes=[tk])
            S.op("act", lambda e, j=j, dst=dst, t0=t0: e.activation(out=dst[:, t0:t0 + HW], in_=tmpa[j][:], func=AF.Sin),
                 reads=[tk], writes=[dk])
        src_t, srck = dst, dk
    a3 = aA
    S.barrier()
    st_mlp.close()
    dgb = sbt("h_dgb", [128, 3, 3 * CBC, 128], BF16)
    hyp = [sbt(f"h_hyp{i}", [128, L + 2], BF16) for i in range(2)]
    uT = sbt("h_uT", [128, CBC, L], BF16)
    x1s = [sbt(f"h_x1s{i}", [128, HW], BF16) for i in range(2)]
    UHD = sbt("h_UHD", [128, NT, 3 * CB], BF16)
    YF = sbt("h_YF", [128, NT, 2, CB], BF16)
    w4b = sbt("h_w4b", [FH, 2, CB], F32)
    win = [sbt(f"h_win{i}", [128, CB], F32) for i in range(2)]
    hfw = [sbt(f"h_hfw{i}", [128, CB], F32) for i in range(2)]
    hbw = [sbt(f"h_hbw{i}", [128, CB], F32) for i in range(2)]
    tabs = [[sbt(f"h_tab{n}{i}", [128, NT, 128], BF16) for i in range(2)] for n in "cs"]
    kre = [sbt(f"h_kre{i}", [128, CB], F32) for i in range(1)]
    kim = [sbt(f"h_kim{i}", [128, CB], F32) for i in range(1)]
    tq = [[sbt(f"h_t{n}{i}", [128, CB], F32) for i in range(1)] for n in range(4)]
    ny = sbt("h_ny", [1, 2 * CB], F32)
    ynyb = sbt("h_ynyb", [1, CB], BF16)
    tmpf = [sbt(f"h_tmpf{i}", [128, HW], F32) for i in range(2)]
    ybs = [sbt(f"h_ybs{i}", [128, HW], BF16) for i in range(2)]
    for i in range(2):
        S.op("pool", lambda e, i=i: e.memset(hyp[i][:], 0.0), writes=[("h_hyp", i)])
    ytm = UHD

    def load_stream(s_idx, cb, kc):
        row0 = hy0 + s_idx * DB + cb * CB + kc * 128
        bi = rot.get("hyp", 0)
        rot["hyp"] = bi + 1
        b = bi % 2
        self.dma("sp", hyp[b][:, 1:1 + L], self.PROJ[row0:row0 + 128, :], [], [("h_hyp", b)])
        return b

    def conv_tile(b, s_idx, kc, t0, n):
        ps, pk = self.psn()
        for j in range(3):
            self.mm(ps[:, 0:n], dgb[:, j, s_idx * CBC + kc, :], hyp[b][:, t0 + j:t0 + j + n], j == 0, j == 2, ["h_dgb", ("h_hyp", b)], [pk])
        return ps, pk

    for cb in getattr(self, 'dbg_blocks', range(NBLK)):
        ch0 = cb * CB
        if getattr(self, 'dbg_bar', False):
            S.barrier()
        for s_idx in range(3):
            for kc in range(CBC):
                col = s_idx * self.BC + (ch0 // 128) + kc
                for j in range(3):
                    S.op("dve", lambda e, j=j, s_idx=s_idx, kc=kc, col=col: e.tensor_scalar(
                        out=dgb[:, j, s_idx * CBC + kc, :], in0=self.idf[:], scalar1=cw[:, j, col:col + 1], scalar2=None, op0=ALU.mult),
                        reads=["idf", "h_cw"], writes=["h_dgb"])
        self.dma("sp", w4b[:, 0, :], self.fw4[:, ch0:ch0 + CB], [], ["h_w4b"])
        self.dma("sp", w4b[:, 1, :], self.fw4[:, DB + ch0:DB + ch0 + CB], [], ["h_w4b"])
        for kc in range(CBC):
            b1 = load_stream(1, cb, kc)
            b2 = load_stream(2, cb, kc)
            c1 = 1 * self.BC + ch0 // 128 + kc
            c2 = 2 * self.BC + ch0 // 128 + kc
            for t0 in range(0, L, HW):
                p1, k1 = conv_tile(b1, 1, kc, t0, HW)
                j = (t0 // HW) % 2
                S.op("act", lambda e, p1=p1, j=j, c1=c1: e.activation(out=x1s[j][:], in_=p1[:, 0:HW], func=AF.Identity, bias=cbv[:, c1:c1 + 1], scale=1.0),
                     reads=[k1, "h_cb"], writes=[("h_x1s", j)])
                p2, k2 = conv_tile(b2, 2, kc, t0, HW)
                S.op("dve", lambda e, p2=p2, j=j, c2=c2, kc=kc, t0=t0: e.scalar_tensor_tensor(
                    out=uT[:, kc, t0:t0 + HW], in0=p2[:, 0:HW], scalar=cbv[:, c2:c2 + 1], in1=x1s[j][:], op0=ALU.add, op1=ALU.mult),
                    reads=[k2, ("h_x1s", j), "h_cb"], writes=[("h_uT", kc)])
        for tc0 in range(0, NT, ntc):
            for kc in range(CBC):
                ps, pk = self.psn()
                psb = ps[:].bitcast(BF16)
                for i in range(ntc):
                    tc = tc0 + i
                    S.op("pe", lambda e, psb=psb, i=i, tc=tc, kc=kc: e.transpose(out=psb[:, i * 128:(i + 1) * 128], in_=uT[:, kc, tc * 128:(tc + 1) * 128],
                                                                             identity=self.idb[:]), reads=[("h_uT", kc), "idb"], writes=[pk])
                S.op("act", lambda e, psb=psb, tc0=tc0, kc=kc: e.activation(
                    out=UHD[:, tc0:tc0 + ntc, CB + kc * 128:CB + (kc + 1) * 128], in_=psb[:, 0:ntc * 128].rearrange("p (a b) -> p a b", b=128), func=AF.Copy),
                    reads=[pk], writes=[("h_UHD", tc0 + i) for i in range(ntc)])
        for tc in range(NT):
            j = tc % 2
            ps, pk = self.psn()
            self.mm(ps[:, 0:2 * CB], a3[:, tc * 128:(tc + 1) * 128], w4b[:].rearrange("p a b -> p (a b)"), True, True, ["h_aA", "h_w4b"], [pk])
            S.op("act", lambda e, j=j, tc=tc, ch0=ch0: e.activation(out=win[j][:], in_=dlb[:, ch0:ch0 + CB], func=AF.Exp, scale=tnc[:, tc:tc + 1]),
                 reads=["h_dl", "h_tn"], writes=[("h_win", j)])
            S.op("dve", lambda e, ps=ps, j=j: e.scalar_tensor_tensor(out=hfw[j][:], in0=win[j][:], scalar=0.05, in1=ps[:, 0:CB], op0=ALU.add, op1=ALU.mult),
                 reads=[pk, ("h_win", j)], writes=[("h_hfw", j)])
            S.op("dve", lambda e, ps=ps, j=j: e.scalar_tensor_tensor(out=hbw[j][:], in0=win[j][:], scalar=0.05, in1=ps[:, CB:2 * CB], op0=ALU.add, op1=ALU.mult),
                 reads=[pk, ("h_win", j)], writes=[("h_hbw", j)])
            if tc == 0:
                S.op("dve", lambda e, j=j: e.memset(hbw[j][0:1, :], 0.0), reads=[("h_hbw", j)], writes=[("h_hbw", j)])
            S.op("dve", lambda e, j=j, tc=tc: e.tensor_tensor(out=UHD[:, tc, 0:CB], in0=hfw[j][:], in1=hbw[j][:], op=ALU.add),
                 reads=[("h_hfw", j), ("h_hbw", j)], writes=[("h_UHD", tc)])
            S.op("dve", lambda e, j=j, tc=tc: e.tensor_tensor(out=UHD[:, tc, 2 * CB:3 * CB], in0=hfw[j][:], in1=hbw[j][:], op=ALU.subtract),
                 reads=[("h_hfw", j), ("h_hbw", j)], writes=[("h_UHD", tc)])
        ukeys = [("h_UHD", tc) for tc in range(NT)]
        for fc in range(NT):
            bi = rot.get("tab", 0)
            rot["tab"] = bi + 1
            tb = bi % 2
            self.dma("sp", tabs[0][tb][:], self.ctab[fc], [], [("h_tabc", tb)])
            self.dma("sp", tabs[1][tb][:], self.stab[fc], [], [("h_tabs", tb)])
            psR, kR = self.psn()
            psI, kI = self.psn()
            for tc in range(NT):
                self.mm(psR[:, 0:2 * CB], tabs[0][tb][:, tc, :], UHD[:, tc, 0:2 * CB], tc == 0, tc == NT - 1, [("h_tabc", tb), ("h_UHD", tc)], [kR])
            for tc in range(NT):
                self.mm(psI[:, 0:2 * CB], tabs[1][tb][:, tc, :], UHD[:, tc, CB:3 * CB], tc == 0, tc == NT - 1, [("h_tabs", tb), ("h_UHD", tc)], [kI])
            j = 0
            S.op("act", lambda e, psR=psR, j=j: e.activation(out=kre[j][:], in_=psR[:, 0:CB], func=AF.Copy), reads=[kR], writes=[("h_kre", j)])
            S.op("act", lambda e, psI=psI, j=j: e.activation(out=kim[j][:], in_=psI[:, CB:2 * CB], func=AF.Copy), reads=[kI], writes=[("h_kim", j)])
            xre, xim = psR[:, CB:2 * CB], psI[:, 0:CB]
            for n, (a, bsrc, ak, bk) in enumerate(((xre, kre, kR, "h_kre"), (xim, kim, kI, "h_kim"), (xre, kim, kR, "h_kim"), (xim, kre, kI, "h_kre"))):
                S.op("dve", lambda e, n=n, a=a, bsrc=bsrc, j=j: e.tensor_tensor(out=tq[n][j][:], in0=bsrc[j][:], in1=a, op=ALU.mult),
                     reads=[ak, (bk, j)], writes=[("h_tq", n, j)])
            S.op("dve", lambda e, j=j, fc=fc: e.tensor_tensor(out=YF[:, fc, 0, :], in0=tq[0][j][:], in1=tq[1][j][:], op=ALU.subtract),
                 reads=[("h_tq", 0, j), ("h_tq", 1, j)], writes=[("h_YF", fc)])
            S.op("dve", lambda e, j=j, fc=fc: e.tensor_tensor(out=YF[:, fc, 1, :], in0=tq[2][j][:], in1=tq[3][j][:], op=ALU.add),
                 reads=[("h_tq", 2, j), ("h_tq", 3, j)], writes=[("h_YF", fc)])
            if fc == 0:
                S.op("dve", lambda e, j=j: e.tensor_scalar(out=YF[0:1, 0, 0, :], in0=tq[0][j][0:1, :], scalar1=0.5, scalar2=None, op0=ALU.mult),
                     reads=[("h_tq", 0, j), ("h_YF", 0)], writes=[("h_YF", 0)])
                psNy, kNy = self.psn()
                for tc in range(NT):
                    self.mm(psNy[0:1, 0:2 * CB], altcb[:], UHD[:, tc, 0:2 * CB], tc == 0, tc == NT - 1, ["h_altcb", ("h_UHD", tc)], [kNy])
                S.op("dve", lambda e, psNy=psNy: e.tensor_copy(out=ny[:], in_=psNy[0:1, 0:2 * CB]), reads=[kNy], writes=["h_ny"])
                S.op("dve", lambda e: e.scalar_tensor_tensor(out=ynyb[:], in0=ny[:, 0:CB], scalar=0.5, in1=ny[:, CB:2 * CB], op0=ALU.mult, op1=ALU.mult),
                     reads=["h_ny"], writes=["h_ynyb"])
        ykeys = [("h_YF", fc) for fc in range(NT)]
        for tc in range(NT):
            bi = rot.get("tab", 0)
            rot["tab"] = bi + 1
            tb = bi % 2
            self.dma("sp", tabs[0][tb][:], self.ctab[tc], [], [("h_tabc", tb)])
            self.dma("sp", tabs[1][tb][:], self.stab[tc], [], [("h_tabs", tb)])
            psY, kY = self.psn()
            for fc in range(NT):
                self.mm(psY[:, 0:CB], tabs[0][tb][:, fc, :], YF[:, fc, 0, :], fc == 0, False, [("h_tabc", tb), ("h_YF", fc)], [kY])
                self.mm(psY[:, 0:CB], tabs[1][tb][:, fc, :], YF[:, fc, 1, :], False, False, [("h_tabs", tb), ("h_YF", fc)], [kY])
            self.mm(psY[:, 0:CB], altrb[:], ynyb[:], False, True, ["h_altrb", "h_ynyb"], [kY])
            S.op("act", lambda e, psY=psY, tc=tc: e.activation(out=ytm[:, tc, 0:CB], in_=psY[:, 0:CB], func=AF.Copy, scale=2.0 / N2),
                 reads=[kY] + ukeys, writes=[("h_UHD", tc)])
        for kc in range(CBC):
            b0 = load_stream(0, cb, kc)
            c0 = 0 * self.BC + ch0 // 128 + kc
            dcol = ch0 // 128 + kc
            for t0 in range(0, L, HW):
                tc0 = t0 // 128
                ps, pk = self.psn()
                psb = ps[:].bitcast(BF16)
                for i in range(ntc):
                    S.op("pe", lambda e, psb=psb, i=i, tc0=tc0, kc=kc: e.transpose(out=psb[:, i * 128:(i + 1) * 128],
                                                                               in_=ytm[:, tc0 + i, kc * 128:(kc + 1) * 128], identity=self.idb[:]),
                         reads=[("h_UHD", tc0 + i), "idb"], writes=[pk])
                j = (t0 // HW) % 2
                S.op("dve", lambda e, psb=psb, j=j, kc=kc, t0=t0, dcol=dcol: e.scalar_tensor_tensor(
                    out=tmpf[j][:], in0=uT[:, kc, t0:t0 + HW], scalar=dsk[:, dcol:dcol + 1], in1=psb[:, 0:HW], op0=ALU.mult, op1=ALU.add),
                    reads=[pk, ("h_uT", kc), "h_dsk"], writes=[("h_tmpf", j)])
                p0, k0 = conv_tile(b0, 0, kc, t0, HW)
                S.op("dve", lambda e, p0=p0, j=j, c0=c0: e.scalar_tensor_tensor(
                    out=ybs[j][:], in0=p0[:, 0:HW], scalar=cbv[:, c0:c0 + 1], in1=tmpf[j][:], op0=ALU.add, op1=ALU.mult),
                    reads=[k0, ("h_tmpf", j), "h_cb"], writes=[("h_ybs", j)])
                self.dma("sp", self.YB[ch0 + kc * 128:ch0 + (kc + 1) * 128, t0:t0 + HW], ybs[j][:], [("h_ybs", j)], [("YB", cb, kc, t0)])


Prog.phase_B3 = _phase_B3


_CACHE = {}


def _get_prog(cfg_key=()):
    if cfg_key not in _CACHE:
        cfg = Cfg(*cfg_key) if cfg_key else Cfg()
        P = Prog(cfg)
        P.build()
        _CACHE[cfg_key] = (cfg, P, const_tables(cfg))
    return _CACHE[cfg_key]


def kernel(**inputs):
    cfg, P, tabs = _get_prog()
    inp = {k: np.asarray(v) for k, v in inputs.items()}
    nb = inp["x"].shape[0]
    n_cores = 8
    maps = [prep_core_inputs(cfg, inp, b, tabs) for b in range(nb)]
    zero_map = {k: np.zeros_like(v) for k, v in maps[0].items()}
    in_maps = [maps[i // 2] if (i % 2 == 0 and i // 2 < nb) else zero_map for i in range(n_cores)]
    res = run_bass_kernel_spmd(P.nc, in_maps, core_ids=list(range(n_cores)))
    out = np.stack([np.ascontiguousarray(np.asarray(res.results[2 * b]["outT"], np.float32).T) for b in range(nb)], 0)
    return out
```

```python
import math
from contextlib import ExitStack

import numpy as np
import ml_dtypes

import concourse.bass as bass
import concourse.mybir as mybir
from concourse.bass_utils import run_bass_kernel_spmd

F32 = mybir.dt.float32
BF16 = mybir.dt.bfloat16
AF = mybir.ActivationFunctionType
ALU = mybir.AluOpType
AX = mybir.AxisListType


class Cfg:
    def __init__(self, D=1024, L=4096, CTX=256, FF=2816, H=4, GRID_W=64, NB=16, FH=64, n_batch=4):
        self.D, self.L, self.CTX, self.FF, self.H, self.GRID_W = D, L, CTX, FF, H, GRID_W
        self.DA = 2 * D
        self.DB = D
        self.HD = self.DA // H
        self.NB = NB
        self.FE = 1 + 2 * NB
        self.FH = FH
        self.INW = 2 * self.DA + 3 * self.DB + 2 * D
        self.n_batch = n_batch
        self.EPS = 1e-6
        self.TT = min(1024, L)
        self.CB = min(256, self.DB)
        self.HW = 512


class Sched:
    NPOOL = 12

    def __init__(self, nc, stack):
        self.nc = nc
        self.eng = {"pe": nc.tensor, "act": nc.scalar, "dve": nc.vector, "pool": nc.gpsimd, "sp": nc.sync}
        self.ops = []
        self.stack = stack
        self.sems = {}
        for e in ("pe", "act", "dve", "pool"):
            self.sems[e] = stack.enter_context(nc.semaphore("s_" + e))
        self.dsems = {}
        for q in ("sp", "pool"):
            self.dsems[q] = [stack.enter_context(nc.semaphore(f"d_{q}{i}")) for i in range(self.NPOOL)]

    def op(self, eng, fn, reads=(), writes=(), dma=False):
        self.ops.append(dict(eng=eng, fn=fn, reads=tuple(reads), writes=tuple(writes), dma=dma, barrier=False))

    def barrier(self):
        for e in ("pe", "act", "dve", "pool", "sp"):
            self.ops.append(dict(eng=e, fn=None, reads=(), writes=(), dma=False, barrier=True))

    def finalize(self):
        ops = self.ops
        n = len(ops)
        last_w = {}
        readers = {}
        deps = [None] * n
        last_by_stream = {}
        dma_count = {"sp": 0, "pool": 0}
        dma_slot = [None] * n
        for i, o in enumerate(ops):
            d = set()
            if o["barrier"]:
                d.update(last_by_stream.values())
            else:
                for k in o["reads"]:
                    if k in last_w:
                        d.add(last_w[k])
                for k in o["writes"]:
                    if k in last_w:
                        d.add(last_w[k])
                    d.update(readers.get(k, ()))
                if o["dma"]:
                    q = o["eng"]
                    c = dma_count[q]
                    dma_count[q] = c + 1
                    slot = c % self.NPOOL
                    dma_slot[i] = (slot, 16 * (c // self.NPOOL + 1))
                    prev = last_by_stream.get((q, slot))
                    if prev is not None:
                        d.add(prev)
                    last_by_stream[(q, slot)] = i
                else:
                    last_by_stream[o["eng"]] = i
                for k in o["reads"]:
                    readers.setdefault(k, []).append(i)
                for k in o["writes"]:
                    last_w[k] = i
                    readers[k] = []
            d.discard(i)
            deps[i] = d
        needed = [False] * n
        for i in range(n):
            pe_i = ops[i]["eng"] == "pe" and not ops[i]["barrier"]
            for j in deps[i]:
                if pe_i and ops[j]["eng"] == "pe" and not ops[j]["barrier"]:
                    continue
                needed[j] = True
        count = {e: 0 for e in ("pe", "act", "dve", "pool")}
        event = [None] * n
        evclock = [None] * n
        clock = {e: {} for e in self.eng}
        for i, o in enumerate(ops):
            e = o["eng"]
            eh = self.eng[e]
            ck = clock[e]
            for j in sorted(deps[i]):
                oj = ops[j]
                if oj["barrier"] or event[j] is None:
                    continue
                sk, val = event[j]
                if (not oj["dma"]) and oj["eng"] == e and e == "pe":
                    continue
                if ck.get(sk, 0) >= val:
                    continue
                sem = self.sems[sk] if isinstance(sk, str) else self.dsems[sk[0]][sk[1]]
                eh.wait_ge(sem, val)
                for k2, v2 in evclock[j].items():
                    if ck.get(k2, 0) < v2:
                        ck[k2] = v2
            if o["barrier"]:
                continue
            ins = o["fn"](eh)
            if o["dma"]:
                slot, val = dma_slot[i]
                sk = (e, slot)
                ins.then_inc(self.dsems[e][slot], 16)
                event[i] = (sk, val)
                snap = dict(ck)
                snap[sk] = val
                evclock[i] = snap
            elif needed[i]:
                assert e != "sp", "sp only issues DMAs"
                count[e] += 1
                ins.then_inc(self.sems[e], 1)
                event[i] = (e, count[e])
                snap = dict(ck)
                snap[e] = count[e]
                evclock[i] = snap
        self.ops = []
        return count


def _bcast_rows(ap_row, nparts):
    return ap_row.broadcast(0, nparts) if hasattr(ap_row, "broadcast") else ap_row


class Prog:
    def __init__(self, cfg, phases="ABCDEF"):
        self.cfg = c = cfg
        self.phases = phases
        self.nc = nc = bass.Bass("TRN2", target_bir_lowering=False)
        self.DC = c.D // 128
        self.FC = c.FF // 128
        self.AC = c.DA // 128
        self.BC = c.DB // 128
        self.NT = c.L // 128
        self.NCC = c.CTX // 128
        self.NTC = self.NT + self.NCC
        self.KH = c.HD // 128
        self.LT = c.L + c.CTX
        self.uid = 0
        d = self.din
        DC, FC, AC, BC, NT = self.DC, self.FC, self.AC, self.BC, self.NT
        self.xT = d("xT", [c.D, c.L])
        self.posT = d("posT", [c.D, c.L])
        self.ctxT = d("ctxT", [c.D, c.CTX])
        self.cs = d("cs", [128, DC, 2])
        self.w_ada = d("w_ada", [c.D, 9 * c.D])
        self.b_adac = d("b_adac", [128, 9 * DC])
        self.normg = d("normg", [128, 3, DC])
        self.finalg = d("finalg", [128, DC])
        self.ffn1_up = d("ffn1_up", [c.D, 2 * c.FF])
        self.ffn1_down = d("ffn1_down", [c.FF, c.D])
        self.ffn2_up = d("ffn2_up", [c.D, 2 * c.FF])
        self.ffn2_down = d("ffn2_down", [c.FF, c.D])
        self.w_in = d("w_in", [c.D, c.INW])
        self.w_pa = d("w_pa", [c.DA, c.D])
        self.w_pb = d("w_pb", [c.DB, c.D])
        self.w_out = d("w_out", [c.D, c.D])
        self.a_convw = d("a_convw", [128, 3, AC])
        self.a_convb = d("a_convb", [128, AC])
        self.bdq = d("bdq", [128, AC, 128])
        self.bdk = d("bdk", [128, AC, 128])
        self.bdv = d("bdv", [128, AC, 128])
        self.wgate = d("wgate", [128, 3 * AC, 16])
        self.bgate = d("bgate", [1, 16])
        self.a_normg = d("a_normg", [128, AC])
        self.a_skip = d("a_skip", [128, AC])
        self.b_convw = d("b_convw", [128, 3, 3 * BC])
        self.b_convb = d("b_convb", [128, 3 * BC])
        self.b_skip = d("b_skip", [128, BC])
        self.zT = d("zT", [c.FE, c.L])
        self.fw1 = d("fw1", [c.FE, c.FH])
        self.fw2 = d("fw2", [c.FH, c.FH])
        self.fw3 = d("fw3", [c.FH, c.FH])
        self.fw4 = d("fw4", [c.FH, 2 * c.DB])
        self.fb = d("fb", [c.FH, 4])
        self.deltas = d("deltas", [1, c.DB])
        self.tn = d("tn", [128, NT])
        self.altc = d("altc", [128, 2])
        self.altr = d("altr", [1, 128])
        self.ctab = d("ctab", [NT, 128, NT, 128], BF16)
        self.stab = d("stab", [NT, 128, NT, 128], BF16)
        self.outT = nc.dram_tensor("outT", [c.D, c.L], F32, kind="ExternalOutput").ap()
        s = self.dscr
        self.X1T = s("X1T", [c.D, c.L], F32)
        self.PROJ = s("PROJ", [c.INW, c.L], BF16)
        self.CXM = s("CXM", [c.DA, c.CTX], BF16)
        self.XC = s("XC", [c.DA, c.L], BF16)
        self.QT = s("QT", [c.DA, c.L], BF16)
        self.KT = s("KT", [c.DA, self.LT], BF16)
        self.KTM = s("KTM", [self.LT, c.DA], BF16)
        self.VTM = s("VTM", [self.LT, c.DA], BF16)
        self.HF = s("HF", [c.L, c.DA], F32)
        self.HB = s("HB", [c.L, c.DA], F32)
        self.YA = s("YA", [c.DA, c.L], BF16)
        self.YB = s("YB", [c.DB, c.L], BF16)
        self.dbg = {}

    def din(self, name, shape, dt=F32):
        return self.nc.dram_tensor(name, list(shape), dt, kind="ExternalInput").ap()

    def dscr(self, name, shape, dt):
        return self.nc.dram_tensor(name, list(shape), dt, kind="Internal").ap()

    def u(self, p="k"):
        self.uid += 1
        return (p, self.uid)

    def sb(self, st, name, shape, dt):
        self.uid += 1
        return st.enter_context(self.nc.sbuf_tensor(f"{name}_{self.uid}", list(shape), dt))

    def psum_alloc(self, st, n=8):
        self.ps = [st.enter_context(self.nc.psum_tensor(f"ps{i}_{self.u()[1]}", [128, 512], F32)) for i in range(n)]
        self.ps_i = 0

    def psn(self):
        i = self.ps_i % len(self.ps)
        self.ps_i += 1
        return self.ps[i], ("ps", i)

    def dma(self, q, out, in_, reads, writes):
        self.S.op(q, lambda e: e.dma_start(out=out, in_=in_), reads=reads, writes=writes, dma=True)

    def mm(self, out, lhsT, rhs, start, stop, reads, writes):
        self.S.op("pe", lambda e: e.matmul(out, lhsT, rhs, start=start, stop=stop), reads=reads, writes=writes)

    def build(self):
        nc, c = self.nc, self.cfg
        with ExitStack() as st0:
            self.S = S = Sched(nc, st0)
            self.st0 = st0
            self.consts(st0)
            if "A" in self.phases:
                with ExitStack() as st:
                    self.phase_A(st)
                    S.barrier()
            if "B" in self.phases:
                with ExitStack() as st:
                    self.phase_B1(st)
                    S.barrier()
            if "C" in self.phases:
                with ExitStack() as st:
                    self.phase_B2(st)
                    S.barrier()
            if "D" in self.phases:
                with ExitStack() as st:
                    self.phase_B3(st)
                    S.barrier()
            if "E" in self.phases:
                with ExitStack() as st:
                    self.phase_B4(st)
                    S.barrier()
            if "F" in self.phases:
                with ExitStack() as st:
                    self.phase_C(st)
                    S.barrier()
            S.barrier()
            self.counts = S.finalize()
        return nc

    def consts(self, st):
        S, c, nc = self.S, self.cfg, self.nc
        DC = self.DC
        self.idf = idf = self.sb(st, "idf", [128, 128], F32)
        self.idb = idb = self.sb(st, "idb", [128, 128], BF16)
        self.onesb = onesb = self.sb(st, "onesb", [128, 128], BF16)
        self.onesf = onesf = self.sb(st, "onesf", [128, 128], F32)
        self.triu_f = triu_f = self.sb(st, "triu_f", [128, 128], F32)
        self.tril_f = tril_f = self.sb(st, "tril_f", [128, 128], F32)
        self.mods = mods = self.sb(st, "mods", [128, 9 * DC, 2], F32)
        self.na = na = self.sb(st, "na", [128, 3, DC, 2], F32)
        self.gt = gt = self.sb(st, "gt", [128, 3, DC, 2], F32)
        self.fing = fing = self.sb(st, "fing", [128, DC], F32)
        self.epsD = epsD = self.sb(st, "epsD", [128, 2], F32)
        shp = [128, c.H, self.NTC]
        self.gw = [self.sb(st, f"gw{d}", shp, F32) for d in range(2)]
        self.gcl = [self.sb(st, f"gcl{d}", shp, F32) for d in range(2)]
        self.gr = [self.sb(st, f"gr{d}", shp, F32) for d in range(2)]
        self.grq = [self.sb(st, f"grq{d}", shp, F32) for d in range(2)]
        S.op("pool", lambda e: e.memset(epsD[:, 0:1], c.D * c.EPS), writes=["epsD"])
        S.op("pool", lambda e: e.memset(epsD[:, 1:2], c.EPS), writes=["epsD"])
        S.op("pool", lambda e: e.memset(idf[:], 1.0), writes=["idf"])
        S.op("pool", lambda e: e.affine_select(out=idf[:], in_=idf[:], pattern=[[-1, 128]], compare_op=ALU.is_equal,
                                               fill=0.0, base=0, channel_multiplier=1), reads=["idf"], writes=["idf"])
        S.op("dve", lambda e: e.tensor_copy(out=idb[:], in_=idf[:]), reads=["idf"], writes=["idb"])
        S.op("pool", lambda e: e.memset(onesb[:], 1.0), writes=["onesb"])
        S.op("pool", lambda e: e.memset(onesf[:], 1.0), writes=["onesf"])
        S.op("pool", lambda e: e.memset(triu_f[:], 1.0), writes=["triu_f"])
        S.op("pool", lambda e: e.affine_select(out=triu_f[:], in_=triu_f[:], pattern=[[1, 128]], compare_op=ALU.is_ge,
                                               fill=0.0, base=0, channel_multiplier=-1), reads=["triu_f"], writes=["triu_f"])
        S.op("pool", lambda e: e.memset(tril_f[:], 1.0), writes=["tril_f"])
        S.op("pool", lambda e: e.affine_select(out=tril_f[:], in_=tril_f[:], pattern=[[-1, 128]], compare_op=ALU.is_ge,
                                               fill=0.0, base=0, channel_multiplier=1), reads=["tril_f"], writes=["tril_f"])
        with ExitStack() as st2:
            self.psum_alloc(st2, 4)
            csf = self.sb(st2, "csf", [128, DC, 2], F32)
            csb = self.sb(st2, "csb", [128, DC, 2], BF16)
            bad = self.sb(st2, "bad", [128, 9 * DC], F32)
            ng = self.sb(st2, "ng", [128, 3, DC], F32)
            wsl = [self.sb(st2, f"adaw{i}", [128, DC, 512], BF16) for i in range(2)]
            self.dma("sp", csf[:], self.cs, [], ["csf"])
            self.dma("sp", bad[:], self.b_adac, [], ["bad"])
            self.dma("sp", ng[:], self.normg, [], ["ng"])
            self.dma("sp", fing[:], self.finalg, [], ["fing"])
            S.op("act", lambda e: e.activation(out=csb[:], in_=csf[:], func=AF.Silu), reads=["csf"], writes=["csb"])
            ncol = 9 * c.D
            wv = self.w_ada.rearrange("(kc p) n -> p kc n", p=128)
            nsl = (ncol + 511) // 512
            for si in range(nsl):
                w0 = si * 512
                ww = min(512, ncol - w0)
                wt = wsl[si % 2]
                wk = ("adaw", si % 2)
                self.dma("pool", wt[:, :, 0:ww], wv[:, :, w0:w0 + ww], [], [wk])
                ps, pk = self.psn()
                for j in range(ww // 128):
                    for kc in range(DC):
                        self.mm(ps[:, 2 * j:2 * j + 2], wt[:, kc, j * 128:(j + 1) * 128], csb[:, kc, :],
                                kc == 0, kc == DC - 1, [wk, "csb"], [pk])
                n0 = w0 // 128
                nn = ww // 128
                S.op("dve", lambda e, ps=ps, n0=n0, nn=nn: e.tensor_tensor(
                    out=mods[:, n0:n0 + nn, :], in0=ps[:, 0:2 * nn].rearrange("p (n r) -> p n r", r=2),
                    in1=bad[:, n0:n0 + nn].unsqueeze(2).broadcast_to([128, nn, 2]), op=ALU.add),
                    reads=[pk, "bad"], writes=["mods"])
            rD = math.sqrt(c.D)
            for i in range(3):
                sc = mods[:, (3 * i + 1) * DC:(3 * i + 2) * DC, :]
                S.op("dve", lambda e, i=i, sc=sc: e.scalar_tensor_tensor(
                    out=na[:, i, :, :], in0=sc, scalar=1.0, in1=ng[:, i, :].unsqueeze(2).broadcast_to([128, DC, 2]),
                    op0=ALU.add, op1=ALU.mult), reads=["mods", "ng"], writes=["na"])
                g = mods[:, (3 * i + 2) * DC:(3 * i + 3) * DC, :]
                gsc = 1.0 if i == 1 else 0.5
                S.op("dve", lambda e, i=i, g=g, gsc=gsc: e.tensor_scalar(
                    out=gt[:, i, :, :], in0=g, scalar1=gsc, scalar2=None, op0=ALU.mult), reads=["mods"], writes=["gt"])
            S.op("dve", lambda e: e.tensor_scalar(out=na[:], in0=na[:], scalar1=rD, scalar2=None, op0=ALU.mult),
                 reads=["na"], writes=["na"])
            S.op("dve", lambda e: e.tensor_scalar(out=fing[:], in0=fing[:], scalar1=rD, scalar2=None, op0=ALU.mult),
                 reads=["fing"], writes=["fing"])
            S.barrier()

    def shift_ap(self, i, dc, r):
        return self.mods[:, 3 * i * self.DC + dc, r:r + 1]

    def linear(self, W, kcn, col0, ncols, rhs_fn, halves, evac, wbufs, wname, slabw, rows0=0):
        Wv = W[rows0:rows0 + kcn * 128, :].rearrange("(kc p) n -> p kc n", p=128)
        rot = self.__dict__.setdefault("_rot", {})
        s0 = 0
        while s0 < ncols:
            sw = min(slabw, ncols - s0)
            bi = rot.get(wname, 0)
            rot[wname] = bi + 1
            wt = wbufs[bi % len(wbufs)]
            wk = (wname, bi % len(wbufs))
            self.dma("pool", wt[:, 0:kcn, 0:sw], Wv[:, :, col0 + s0:col0 + s0 + sw], [], [wk])
            for j in range(sw // 128):
                for hi, (c0, cw) in enumerate(halves):
                    ps, pk = self.psn()
                    for kc in range(kcn):
                        rap, rkey = rhs_fn(kc, c0, cw)
                        self.mm(ps[:, 0:cw], wt[:, kc, j * 128:(j + 1) * 128], rap, kc == 0, kc == kcn - 1,
                                [wk] + list(rkey), [pk])
                    evac((s0 // 128) + j, hi, c0, cw, ps, pk)
            s0 += sw

    def halves_of(self, ntok):
        out, c0 = [], 0
        while c0 < ntok:
            cw = min(self.cfg.HW, ntok - c0)
            out.append((c0, cw))
            c0 += cw
        return out

    def rmsnorm_mod(self, T, ntok, i, r, final=False, out_f32=None):
        S, c, DC = self.S, self.cfg, self.DC
        xt, h, sq = T["xt"], T["h"], T["sq"]
        XK = T.get("xk", "xt")
        for (c0, cw) in self.halves_of(ntok):
            S.op("act", lambda e, c0=c0, cw=cw: e.activation(out=sq[:, :, c0:c0 + cw], in_=xt[:, :, c0:c0 + cw], func=AF.Square),
                 reads=[(XK, d) for d in range(DC)], writes=["sq"])
            ps, pk = self.psn()
            for dc in range(DC):
                self.mm(ps[:, 0:cw], self.onesb[:], sq[:, dc, c0:c0 + cw], dc == 0, dc == DC - 1, ["sq", "onesb"], [pk])
            rs = T["rs"]
            S.op("act", lambda e, ps=ps, cw=cw: e.activation(out=rs[:, 0:cw], in_=ps[:, 0:cw], func=AF.Sqrt, bias=self.epsD[:, 0:1], scale=1.0),
                 reads=[pk, "epsD"], writes=["rs"])
            S.op("dve", lambda e, cw=cw: e.reciprocal(out=rs[:, 0:cw], in_=rs[:, 0:cw]), reads=["rs"], writes=["rs"])
            for dc in range(DC):
                if final:
                    scal = self.fing[:, dc:dc + 1]
                    S.op("dve", lambda e, dc=dc, c0=c0, cw=cw, scal=scal: e.scalar_tensor_tensor(
                        out=out_f32[:, dc, c0:c0 + cw], in0=xt[:, dc, c0:c0 + cw], scalar=scal, in1=rs[:, 0:cw],
                        op0=ALU.mult, op1=ALU.mult), reads=[(XK, dc), "rs", "fing"], writes=[("of", dc)])
                    continue
                tb = T["tn"][dc % 2]
                tk = ("tn", dc % 2)
                scal = self.na[:, i, dc, r:r + 1]
                S.op("dve", lambda e, dc=dc, c0=c0, cw=cw, tb=tb, scal=scal: e.scalar_tensor_tensor(
                    out=tb[:, 0:cw], in0=xt[:, dc, c0:c0 + cw], scalar=scal, in1=rs[:, 0:cw], op0=ALU.mult, op1=ALU.mult),
                    reads=[(XK, dc), "rs", "na"], writes=[tk])
                sh = self.shift_ap(i, dc, r)
                S.op("act", lambda e, dc=dc, c0=c0, cw=cw, tb=tb, sh=sh: e.activation(
                    out=h[:, dc, c0:c0 + cw], in_=tb[:, 0:cw], func=AF.Identity, bias=sh, scale=1.0),
                    reads=[tk, "mods"], writes=[("h", dc)])

    def ffn(self, T, ntok, Wup, Wdown, gi, r):
        S, c, DC, FC = self.S, self.cfg, self.DC, self.FC
        xt, h, act = T["xt"], T["h"], T["act"]
        XK = T.get("xk", "xt")
        halves = self.halves_of(ntok)
        hkeys = [("h", d) for d in range(DC)]
        Wv = Wup.rearrange("(kc p) n -> p kc n", p=128)
        rot = self.__dict__.setdefault("_rot", {})
        f0 = 0
        while f0 < c.FF:
            sw = min(256, c.FF - f0)
            bi = rot.get("wA", 0)
            rot["wA"] = bi + 1
            wt = T["wA"][bi % 3]
            wk = ("wA", bi % 3)
            self.dma("pool", wt[:, :, 0:sw], Wv[:, :, f0:f0 + sw], [], [wk])
            self.dma("pool", wt[:, :, 256:256 + sw], Wv[:, :, c.FF + f0:c.FF + f0 + sw], [], [wk])
            for j in range(sw // 128):
                fj = f0 // 128 + j
                for (c0, cw) in halves:
                    pg, pgk = self.psn()
                    pu, puk = self.psn()
                    for kc in range(DC):
                        self.mm(pg[:, 0:cw], wt[:, kc, j * 128:(j + 1) * 128], h[:, kc, c0:c0 + cw], kc == 0, kc == DC - 1, [wk] + hkeys, [pgk])
                    for kc in range(DC):
                        self.mm(pu[:, 0:cw], wt[:, kc, 256 + j * 128:256 + (j + 1) * 128], h[:, kc, c0:c0 + cw], kc == 0, kc == DC - 1, [wk] + hkeys, [puk])
                    si = rot.get("sg", 0)
                    rot["sg"] = si + 1
                    sg = T["sg"][si % 2]
                    sgk = ("sg", si % 2)
                    S.op("act", lambda e, pg=pg, cw=cw, sg=sg: e.activation(out=sg[:, 0:cw], in_=pg[:, 0:cw], func=AF.Silu),
                         reads=[pgk], writes=[sgk])
                    S.op("dve", lambda e, pu=pu, cw=cw, sg=sg, fj=fj, c0=c0: e.tensor_tensor(
                        out=act[:, fj, c0:c0 + cw], in0=sg[:, 0:cw], in1=pu[:, 0:cw], op=ALU.mult),
                        reads=[sgk, puk], writes=[("act", fj)])
            f0 += sw
        akeys = [("act", f) for f in range(FC)]

        def evac(j, hi, c0, cw, ps, pk):
            scal = self.gt[:, gi, j, r:r + 1]
            S.op("dve", lambda e: e.scalar_tensor_tensor(out=xt[:, j, c0:c0 + cw], in0=ps[:, 0:cw], scalar=scal,
                                                         in1=xt[:, j, c0:c0 + cw], op0=ALU.mult, op1=ALU.add),
                 reads=[pk, (XK, j), "gt"], writes=[(XK, j)])
        self.linear(Wdown, FC, 0, c.D, lambda kc, c0, cw: (act[:, kc, c0:c0 + cw], akeys), halves, evac, T["wB"], "wB", 256)

    def alloc_row_tiles(self, st, TT):
        DC, FC = self.DC, self.FC
        T = {}
        T["xt"] = self.sb(st, "xt", [128, DC, TT], F32)
        T["h"] = self.sb(st, "h", [128, DC, TT], BF16)
        T["sq"] = self.sb(st, "sq", [128, DC, TT], BF16)
        T["act"] = self.sb(st, "act", [128, max(FC, self.AC), TT], BF16)
        T["rs"] = self.sb(st, "rs", [128, 512], F32)
        T["tn"] = [self.sb(st, f"tn{i}", [128, 512], F32) for i in range(2)]
        T["sg"] = [self.sb(st, f"sg{i}", [128, 512], F32) for i in range(2)]
        T["wA"] = [self.sb(st, f"wA{i}", [128, DC, 512], BF16) for i in range(3)]
        T["wB"] = [self.sb(st, f"wB{i}", [128, max(FC, self.AC), 256], BF16) for i in range(2)]
        T["stg"] = [self.sb(st, f"stg{i}", [128, TT], BF16) for i in range(2)]
        return T

    def phase_A(self, st):
        S, c, DC = self.S, self.cfg, self.DC
        TT = c.TT
        self.psum_alloc(st, 8)
        T = self.alloc_row_tiles(st, TT)
        xts = [T["xt"], self.sb(st, "xtB", [128, DC, TT], F32)]
        h = T["h"]
        hkeys = [("h", d) for d in range(DC)]
        tiles = [("ctx", 0, c.CTX)] + [("x", t0, TT) for t0 in range(0, c.L, TT)]
        rot = self.__dict__.setdefault("_rot", {})

        def load_tile(ti):
            kind, t0, ntok = tiles[ti]
            xb = xts[ti % 2]
            xk = [(f"xt{ti % 2}", d) for d in range(DC)]
            if kind == "ctx":
                self.dma("sp", xb[:, :, 0:ntok], self.ctxT.rearrange("(dc p) t -> p dc t", p=128), [], xk)
            else:
                self.dma("sp", xb[:, :, 0:ntok], self.xT[:, t0:t0 + ntok].rearrange("(dc p) t -> p dc t", p=128), [], xk)
                pv = self.posT[:, t0:t0 + ntok].rearrange("(dc p) t -> p dc t", p=128)
                S.op("pool", lambda e, xb=xb, ntok=ntok, pv=pv: e.dma_start(out=xb[:, :, 0:ntok], in_=pv, accum_op=ALU.add),
                     reads=xk, writes=xk, dma=True)
        load_tile(0)
        for ti, (kind, t0, ntok) in enumerate(tiles):
            if ti + 1 < len(tiles):
                load_tile(ti + 1)
            xt = xts[ti % 2]
            T["xt"] = xt
            T["xk"] = f"xt{ti % 2}"
            xkeys = [(T["xk"], d) for d in range(DC)]
            r = 1 if kind == "ctx" else 0
            halves = self.halves_of(ntok)
            self.rmsnorm_mod(T, ntok, 0, r)
            self.ffn(T, ntok, self.ffn1_up, self.ffn1_down, 0, r)
            if kind == "x":
                self.dma("sp", self.X1T[:, t0:t0 + ntok].rearrange("(dc p) t -> p dc t", p=128), xt[:, :, 0:ntok], xkeys, [("X1T", t0)])
            self.rmsnorm_mod(T, ntok, 1, r)
            ncols = c.DA if kind == "ctx" else c.INW
            sig_lo, sig_hi = c.DA // 128, 2 * c.DA // 128
            g_lo = (2 * c.DA + 3 * c.DB) // 128

            def evac(j, hi, c0, cw, ps, pk, kind=kind, t0=t0, ntok=ntok, halves=halves):
                si = rot.get("stg", 0)
                if hi == 0:
                    rot["stg"] = si + 1
                else:
                    si = si - 1
                stg = T["stg"][si % 2]
                sk = ("stg", si % 2)
                fn = AF.Sigmoid if ((sig_lo <= j < sig_hi) or j >= g_lo) else AF.Identity
                S.op("act", lambda e: e.activation(out=stg[:, c0:c0 + cw], in_=ps[:, 0:cw], func=fn), reads=[pk], writes=[sk])
                if hi == len(halves) - 1:
                    dst = self.CXM[j * 128:(j + 1) * 128, 0:ntok] if kind == "ctx" else self.PROJ[j * 128:(j + 1) * 128, t0:t0 + ntok]
                    self.dma("sp", dst, stg[:, 0:ntok], [sk], [("PROJ", kind, j, t0)])
            self.linear(self.w_in, DC, 0, ncols, lambda kc, c0, cw: (h[:, kc, c0:c0 + cw], hkeys), halves, evac, T["wA"], "wA", 512)


def _col(v, nchunks=None):
    v = np.asarray(v, np.float32)
    return np.ascontiguousarray(v.reshape(-1, 128).T)


def _blockdiag_tiles(w):
    nb = w.shape[0]
    n = nb * 4
    ac = n // 128
    out = np.zeros((128, ac, 128), np.float32)
    for a in range(ac):
        for bl in range(32):
            blk = w[a * 32 + bl]
            out[bl * 4:(bl + 1) * 4, a, bl * 4:(bl + 1) * 4] = blk
    return out


def const_tables(cfg):
    c = cfg
    L, D = c.L, c.D
    t = np.arange(L)
    r = (t // c.GRID_W).astype(np.float32)
    cc = (t % c.GRID_W).astype(np.float32)
    nf = D // 4
    omega = (1.0 / (10000.0 ** (np.arange(nf, dtype=np.float32) / nf))).astype(np.float32)

    def emb(p):
        a = p[:, None] * omega[None, :]
        return np.concatenate([np.sin(a), np.cos(a)], -1)
    pos = np.concatenate([emb(r), emb(cc)], -1).astype(np.float32)
    tl = np.linspace(0.0, 1.0, L, dtype=np.float32)[:, None]
    w = (2.0 * math.pi * np.arange(L, dtype=np.float32)[:, None] / L).astype(np.float32)
    f = np.linspace(1e-4, c.NB - 1, c.NB, dtype=np.float32)[None, :]
    z = np.concatenate([tl, np.cos(f * w), -np.sin(f * w)], -1).astype(np.float32)
    max_decay = math.log(1e-2) / 0.3
    min_decay = math.log(1e-2) / 1.5
    deltas = np.abs(np.linspace(min_decay, max_decay, c.DB, dtype=np.float32)).astype(np.float32)
    tn = np.ascontiguousarray(tl[:, 0].reshape(-1, 128).T)
    N = 2 * L
    idx = (np.outer(np.arange(L, dtype=np.int64), np.arange(L, dtype=np.int64)) % N).astype(np.float64)
    ang = 2.0 * math.pi * idx / N
    C = np.cos(ang)
    Sm = -np.sin(ang)
    NT = L // 128

    def blk(M):
        return np.ascontiguousarray(M.reshape(NT, 128, NT, 128).transpose(2, 1, 0, 3)).astype(ml_dtypes.bfloat16)
    alt = ((-1.0) ** np.arange(128)).astype(np.float32)
    altc = np.stack([alt, np.full(128, -math.pi, np.float32)], 1).astype(np.float32)
    return dict(posT=np.ascontiguousarray(pos.T), zT=np.ascontiguousarray(z.T), deltas=deltas[None, :].copy(), tn=tn,
                ctab=blk(C), stab=blk(Sm), altc=np.ascontiguousarray(altc), altr=alt[None, :].copy())


def prep_core_inputs(cfg, inp, b, tabs):
    c = cfg
    DC = c.D // 128
    m = {}
    m["xT"] = np.ascontiguousarray(np.asarray(inp["x"][b], np.float32).T)
    m["ctxT"] = np.ascontiguousarray(np.asarray(inp["ctx"][b], np.float32).T)
    cs = np.stack([_col(inp["c"][b]), _col(inp["c_ctx"])], -1)
    m["cs"] = np.ascontiguousarray(cs.astype(np.float32))
    m["w_ada"] = np.asarray(inp["w_ada"][0], np.float32)
    m["b_adac"] = _col(inp["b_ada"][0])
    m["normg"] = np.ascontiguousarray(np.stack([_col(inp["norm_g"][0][i]) for i in range(3)], 1))
    m["finalg"] = _col(inp["final_g"])
    for k in ("ffn1_up", "ffn1_down", "ffn2_up", "ffn2_down", "w_in", "w_pa", "w_pb", "w_out"):
        m[k] = np.asarray(inp[k][0], np.float32)
    m["a_convw"] = np.ascontiguousarray(np.stack([_col(inp["a_conv_w"][0][j]) for j in range(3)], 1))
    m["a_convb"] = _col(inp["a_conv_b"][0])
    m["bdq"] = _blockdiag_tiles(np.asarray(inp["a_wq"][0], np.float32))
    m["bdk"] = _blockdiag_tiles(np.asarray(inp["a_wk"][0], np.float32))
    m["bdv"] = _blockdiag_tiles(np.asarray(inp["a_wv"][0], np.float32))
    wg = np.asarray(inp["a_w_gate"][0], np.float32)
    m["wgate"] = np.ascontiguousarray(wg.reshape(-1, 128, 16).transpose(1, 0, 2))
    m["bgate"] = np.asarray(inp["a_b_gate"][0], np.float32)[None, :].copy()
    m["a_normg"] = _col(inp["a_norm_g"][0])
    m["a_skip"] = _col(inp["a_skip"][0])
    m["b_convw"] = np.ascontiguousarray(np.stack([_col(inp["b_conv_w"][0][j]) for j in range(3)], 1))
    m["b_convb"] = _col(inp["b_conv_b"][0])
    m["b_skip"] = _col(inp["b_skip"][0])
    m["fw1"] = np.asarray(inp["b_filt_w1"][0], np.float32)
    m["fw2"] = np.asarray(inp["b_filt_w2"][0], np.float32)
    m["fw3"] = np.asarray(inp["b_filt_w3"][0], np.float32)
    m["fw4"] = np.asarray(inp["b_filt_w4"][0], np.float32)
    m["fb"] = np.ascontiguousarray(np.stack([np.asarray(inp[k][0], np.float32) for k in
                                             ("b_filt_b1", "b_filt_b2", "b_filt_b3", "b_filt_freq")], 1))
    for k in ("posT", "zT", "deltas", "tn", "ctab", "stab", "altc", "altr"):
        m[k] = tabs[k]
    return m


def _phase_C(self, st):
    S, c, DC, AC, BC = self.S, self.cfg, self.DC, self.AC, self.BC
    TT = c.TT
    self.psum_alloc(st, 8)
    T = self.alloc_row_tiles(st, TT)
    g2 = self.sb(st, "g2", [128, DC, TT], BF16)
    mixt = self.sb(st, "mixt", [128, DC, TT], BF16)
    xt, h, sq, act = T["xt"], T["h"], T["sq"], T["act"]
    xkeys = [("xt", d) for d in range(DC)]
    rot = self.__dict__.setdefault("_rot", {})
    ga0 = 2 * c.DA + 3 * c.DB
    fv = lambda ap: ap.rearrange("(kc p) t -> p kc t", p=128)
    for t0 in range(0, c.L, TT):
        ntok = TT
        halves = self.halves_of(ntok)
        yak = [("act", f) for f in range(AC)]
        self.dma("sp", act[:, 0:AC, 0:ntok], fv(self.YA[:, t0:t0 + ntok]), [], yak)
        self.dma("sp", sq[:, 0:BC, 0:ntok], fv(self.YB[:, t0:t0 + ntok]), [], ["sq"])
        hk = [("h", d) for d in range(DC)]
        self.dma("sp", h[:, :, 0:ntok], fv(self.PROJ[ga0:ga0 + c.D, t0:t0 + ntok]), [], hk)
        self.dma("sp", g2[:, :, 0:ntok], fv(self.PROJ[ga0 + c.D:ga0 + 2 * c.D, t0:t0 + ntok]), [], ["g2"])
        self.dma("sp", xt[:, :, 0:ntok], fv(self.X1T[:, t0:t0 + ntok]), [], xkeys)

        def evac_a(j, hi, c0, cw, ps, pk):
            S.op("dve", lambda e: e.tensor_tensor(out=mixt[:, j, c0:c0 + cw], in0=h[:, j, c0:c0 + cw], in1=ps[:, 0:cw], op=ALU.mult),
                 reads=[pk, ("h", j)], writes=[("mixt", j)])
        self.linear(self.w_pa, AC, 0, c.D, lambda kc, c0, cw: (act[:, kc, c0:c0 + cw], yak), halves, evac_a, T["wB"], "wB", 256)

        def evac_b(j, hi, c0, cw, ps, pk):
            si = rot.get("sg", 0)
            rot["sg"] = si + 1
            sg = T["sg"][si % 2]
            sgk = ("sg", si % 2)
            S.op("dve", lambda e: e.tensor_tensor(out=sg[:, 0:cw], in0=g2[:, j, c0:c0 + cw], in1=ps[:, 0:cw], op=ALU.mult),
                 reads=[pk, "g2"], writes=[sgk])
            S.op("dve", lambda e: e.tensor_tensor(out=mixt[:, j, c0:c0 + cw], in0=mixt[:, j, c0:c0 + cw], in1=sg[:, 0:cw], op=ALU.add),
                 reads=[sgk, ("mixt", j)], writes=[("mixt", j)])
        self.linear(self.w_pb, BC, 0, c.D, lambda kc, c0, cw: (sq[:, kc, c0:c0 + cw], ["sq"]), halves, evac_b, T["wA"], "wA", 512)

        mk = [("mixt", d) for d in range(DC)]

        def evac_o(j, hi, c0, cw, ps, pk):
            scal = self.gt[:, 1, j, 0:1]
            S.op("dve", lambda e: e.scalar_tensor_tensor(out=xt[:, j, c0:c0 + cw], in0=ps[:, 0:cw], scalar=scal,
                                                         in1=xt[:, j, c0:c0 + cw], op0=ALU.mult, op1=ALU.add),
                 reads=[pk, ("xt", j), "gt"], writes=[("xt", j)])
        self.linear(self.w_out, DC, 0, c.D, lambda kc, c0, cw: (mixt[:, kc, c0:c0 + cw], mk), halves, evac_o, T["wA"], "wA", 512)
        self.rmsnorm_mod(T, ntok, 2, 0)
        self.ffn(T, ntok, self.ffn2_up, self.ffn2_down, 2, 0)
        self.rmsnorm_mod(T, ntok, 0, 0, final=True, out_f32=xt)
        self.dma("sp", fv(self.outT[:, t0:t0 + ntok]), xt[:, :, 0:ntok], [("of", d) for d in range(DC)] + xkeys, [("outT", t0)])


Prog.phase_C = _phase_C


def _phase_B1(self, st):
    S, c, nc = self.S, self.cfg, self.nc
    AC, KH, NTC, NCC, NT, LT, HD = self.AC, self.KH, self.NTC, self.NCC, self.NT, self.LT, c.HD
    HW = c.HW
    self.psum_alloc(st, 8)
    rot = self.__dict__.setdefault("_rot", {})
    cwc = self.sb(st, "cwc", [128, 3, AC], F32)
    cbc = self.sb(st, "cbc", [128, AC], F32)
    dg = self.sb(st, "dg", [128, 3, AC, 128], BF16)
    bd = [self.sb(st, f"bdt{n}", [128, AC, 128], BF16) for n in "qkv"]
    wg = self.sb(st, "wg", [128, 3 * AC, 16], BF16)
    gacc = self.sb(st, "gacc", [128, NTC, 16], F32)
    self.dma("sp", cwc[:], self.a_convw, [], ["cwc"])
    self.dma("sp", cbc[:], self.a_convb, [], ["cbc"])
    for n, src in zip(range(3), (self.bdq, self.bdk, self.bdv)):
        self.dma("pool", bd[n][:], src, [], [("bd", n)])
    self.dma("pool", wg[:], self.wgate, [], ["wg"])
    for j in range(3):
        for cc in range(AC):
            S.op("dve", lambda e, j=j, cc=cc: e.tensor_scalar(out=dg[:, j, cc, :], in0=self.idf[:], scalar1=cwc[:, j, cc:cc + 1],
                                                               scalar2=None, op0=ALU.mult), reads=["idf", "cwc"], writes=[("dg", cc)])
    xm = self.sb(st, "xm", [128, KH, LT + 4], BF16)
    xc = self.sb(st, "xc", [128, KH, LT], BF16)
    fT = [[self.sb(st, f"fT{n}{i}", [128, LT], BF16) for i in range(2)] for n in "qkv"]
    stg = [[self.sb(st, f"tm{n}{i}", [128, 4, HD], BF16) for i in range(2)] for n in "kv"]
    S.op("pool", lambda e: e.memset(xm[:], 0.0), writes=[("xm", k) for k in range(KH)])
    colx = c.CTX + 3
    ttiles = [(0, c.CTX, 1)] + [(c.CTX + t0, min(HW, c.L - t0), colx + t0) for t0 in range(0, c.L, HW)]
    first_g = True
    for hd in range(c.H):
        for kc in range(KH):
            cc = hd * KH + kc
            xk = ("xm", kc)
            self.dma("sp", xm[:, kc, 1:1 + c.CTX], self.CXM[cc * 128:(cc + 1) * 128, :], [], [xk])
            self.dma("sp", xm[:, kc, colx:colx + c.L], self.PROJ[cc * 128:(cc + 1) * 128, :], [], [xk])
            bi = rot.get("fT", 0)
            rot["fT"] = bi + 1
            qT, kT, vT = fT[0][bi % 2], fT[1][bi % 2], fT[2][bi % 2]
            fk = [("fT", n, bi % 2) for n in range(3)]
            for (tau0, n, col) in ttiles:
                ps, pk = self.psn()
                for j in range(3):
                    self.mm(ps[:, 0:n], dg[:, j, cc, :], xm[:, kc, col + j - 1:col + j - 1 + n], j == 0, j == 2, [("dg", cc), xk], [pk])
                S.op("act", lambda e, ps=ps, n=n, tau0=tau0, kc=kc, cc=cc: e.activation(
                    out=xc[:, kc, tau0:tau0 + n], in_=ps[:, 0:n], func=AF.Silu, bias=cbc[:, cc:cc + 1], scale=1.0),
                    reads=[pk, "cbc"], writes=[("xc", kc)])
                for ni, (dst, src_ap, srck) in enumerate(((qT, xc[:, kc, tau0:tau0 + n], ("xc", kc)),
                                                          (kT, xc[:, kc, tau0:tau0 + n], ("xc", kc)),
                                                          (vT, xm[:, kc, col:col + n], xk))):
                    ps2, pk2 = self.psn()
                    self.mm(ps2[:, 0:n], bd[ni][:, cc, :], src_ap, True, True, [("bd", ni), srck], [pk2])
                    if ni == 1:
                        S.op("act", lambda e, ps2=ps2, n=n, tau0=tau0, dst=dst: e.activation(out=dst[:, tau0:tau0 + n], in_=ps2[:, 0:n], func=AF.Identity),
                             reads=[pk2], writes=[fk[ni]])
                    else:
                        S.op("dve", lambda e, ps2=ps2, n=n, tau0=tau0, dst=dst: e.tensor_copy(out=dst[:, tau0:tau0 + n], in_=ps2[:, 0:n]),
                             reads=[pk2], writes=[fk[ni]])
            g0 = 0
            while g0 < NTC:
                gn = min(32, NTC - g0)
                ps, pk = self.psn()
                for tci in range(gn):
                    tc = g0 + tci
                    for ni, src in enumerate((qT, kT, vT)):
                        self.mm(ps[:, tci * 16:(tci + 1) * 16], src[:, tc * 128:(tc + 1) * 128], wg[:, ni * AC + cc, :],
                                ni == 0, ni == 2, [fk[ni], "wg"], [pk])
                if first_g:
                    S.op("dve", lambda e, ps=ps, g0=g0, gn=gn: e.tensor_copy(out=gacc[:, g0:g0 + gn, :], in_=ps[:, 0:gn * 16].rearrange("p (a b) -> p a b", b=16)),
                         reads=[pk], writes=["gacc"])
                else:
                    S.op("dve", lambda e, ps=ps, g0=g0, gn=gn: e.tensor_tensor(out=gacc[:, g0:g0 + gn, :], in0=gacc[:, g0:g0 + gn, :],
                                                                               in1=ps[:, 0:gn * 16].rearrange("p (a b) -> p a b", b=16), op=ALU.add),
                         reads=[pk, "gacc"], writes=["gacc"])
                g0 += gn
            first_g = False
            self.dma("sp", self.XC[cc * 128:(cc + 1) * 128, :], xc[:, kc, c.CTX:], [("xc", kc)], [("XC", cc)])
            self.dma("sp", self.QT[cc * 128:(cc + 1) * 128, :], qT[:, c.CTX:], [fk[0]], [("QT", cc)])
            self.dma("sp", self.KT[cc * 128:(cc + 1) * 128, :], kT[:, :], [fk[1]], [("KT", cc)])
        for g0 in range(0, NTC, 4):
            gn = min(4, NTC - g0)
            bi = rot.get("tm", 0)
            rot["tm"] = bi + 1
            for ni, (srct, srckey, dstD) in enumerate(((xc, "xc", self.KTM), (xm, "xm", self.VTM))):
                sg = stg[ni][bi % 2]
                sk = ("tm", ni, bi % 2)
                for tci in range(gn):
                    tc = g0 + tci
                    ps, pk = self.psn()
                    for kc in range(KH):
                        cc = hd * KH + kc
                        if ni == 0:
                            lhs = xc[:, kc, tc * 128:(tc + 1) * 128]
                        else:
                            colt = (1 + tc * 128) if tc < NCC else (colx + (tc - NCC) * 128)
                            lhs = xm[:, kc, colt:colt + 128]
                        self.mm(ps[:, kc * 128:(kc + 1) * 128], lhs, bd[1 + ni][:, cc, :], True, True,
                                [(srckey, kc), ("bd", 1 + ni)], [pk])
                    if ni == 0:
                        S.op("act", lambda e, ps=ps, sg=sg, tci=tci: e.activation(out=sg[:, tci, :], in_=ps[:, 0:HD], func=AF.Identity),
                             reads=[pk], writes=[sk])
                    else:
                        S.op("dve", lambda e, ps=ps, sg=sg, tci=tci: e.tensor_copy(out=sg[:, tci, :], in_=ps[:, 0:HD]),
                             reads=[pk], writes=[sk])
                self.dma("sp", dstD[g0 * 128:(g0 + gn) * 128, hd * HD:(hd + 1) * HD].rearrange("(tc p) ch -> p tc ch", p=128),
                         sg[:, 0:gn, :], [sk], [("TM", ni, hd, g0)])
    self.gate_prologue(st, gacc)


Prog.phase_B1 = _phase_B1


def _gate_prologue(self, st, gacc):
    S, c = self.S, self.cfg
    P, H, NCC = self.NTC, c.H, self.NCC
    shp = [128, H, P]
    names = ["ig", "fg", "l", "bl", "tt", "pfx", "aloc", "e2", "aa", "gg", "pg", "tmp"]
    t = {n: self.sb(st, "gp_" + n, shp, F32) for n in names}
    bgb = self.sb(st, "bgb", [128, 16], F32)
    ones_hp = self.sb(st, "ones_hp", [128, P], F32)
    self.dma("sp", bgb[:], self.bgate.partition_broadcast(128), [], ["bgb"])
    S.op("pool", lambda e: e.memset(ones_hp[:], 1.0), writes=["ones_hp"])
    flat = lambda a: a[:].rearrange("p h c -> p (h c)")
    one = self.onesf[:, 0:1]

    def tt_op(eng, out, a, b, op, r, w):
        S.op(eng, lambda e: e.tensor_tensor(out=out, in0=a, in1=b, op=op), reads=r, writes=w)

    for d in range(2):
        ci = 0 if d == 0 else 8
        for nm, col in (("ig", ci), ("fg", ci + 4)):
            bias = bgb[:, col:col + 4]
            if d == 0:
                src = gacc[:, :, col:col + 4].rearrange("p c g -> p g c")
                tt_op("dve", t[nm][:], src, bias.unsqueeze(2).broadcast_to(shp), ALU.add, ["gacc", "bgb"], [nm])
            else:
                src1 = gacc[:, NCC - 1::-1, col:col + 4].rearrange("p c g -> p g c")
                tt_op("dve", t[nm][:, :, 0:NCC], src1, bias.unsqueeze(2).broadcast_to([128, H, NCC]), ALU.add, ["gacc", "bgb"], [nm])
                src2 = gacc[:, P - 1:NCC - 1:-1, col:col + 4].rearrange("p c g -> p g c")
                tt_op("dve", t[nm][:, :, NCC:P], src2, bias.unsqueeze(2).broadcast_to([128, H, P - NCC]), ALU.add, ["gacc", "bgb"], [nm])
        S.op("act", lambda e: e.activation(out=t["e2"][:], in_=t["fg"][:], func=AF.Exp, scale=-1.0), reads=["fg"], writes=["e2"])
        S.op("act", lambda e: e.activation(out=t["l"][:], in_=t["e2"][:], func=AF.Ln, bias=one, scale=1.0), reads=["e2", "onesf"], writes=["l"])
        tri = self.triu_f if d == 0 else self.tril_f
        ps, pk = self.psn()
        self.mm(ps[:, 0:H * P], tri[:], flat(t["l"]), True, True, ["l", "triu_f", "tril_f"], [pk])
        S.op("dve", lambda e, ps=ps: e.tensor_scalar(out=flat(t["bl"]), in0=ps[:, 0:H * P], scalar1=-1.0, scalar2=None, op0=ALU.mult),
             reads=[pk], writes=["bl"])
        ps2, pk2 = self.psn()
        self.mm(ps2[:, 0:H * P], self.onesf[:], flat(t["l"]), True, True, ["l", "onesf"], [pk2])
        S.op("dve", lambda e, ps2=ps2: e.tensor_scalar(out=flat(t["tt"]), in0=ps2[:, 0:H * P], scalar1=-1.0, scalar2=None, op0=ALU.mult),
             reads=[pk2], writes=["tt"])
        for h in range(H):
            S.op("dve", lambda e, h=h: e.tensor_tensor_scan(out=t["tmp"][:, h, :], data0=ones_hp[:], data1=t["tt"][:, h, :], initial=0.0,
                                                            op0=ALU.mult, op1=ALU.add), reads=["tt", "ones_hp"], writes=["tmp"])
        tt_op("dve", t["pfx"][:], t["tmp"][:], t["tt"][:], ALU.subtract, ["tmp", "tt"], ["pfx"])
        tt_op("dve", t["aloc"][:], t["ig"][:], t["bl"][:], ALU.subtract, ["ig", "bl"], ["aloc"])
        S.op("dve", lambda e: e.tensor_scalar(out=t["e2"][:], in0=t["aloc"][:], scalar1=60.0, scalar2=None, op0=ALU.min),
             reads=["aloc"], writes=["e2"])
        S.op("act", lambda e: e.activation(out=t["e2"][:], in_=t["e2"][:], func=AF.Exp), reads=["e2"], writes=["e2"])
        ps3, pk3 = self.psn()
        self.mm(ps3[:, 0:H * P], self.onesf[:], flat(t["e2"]), True, True, ["e2", "onesf"], [pk3])
        S.op("act", lambda e, ps3=ps3: e.activation(out=flat(t["aa"]), in_=ps3[:, 0:H * P], func=AF.Ln), reads=[pk3], writes=["aa"])
        tt_op("dve", t["aa"][:], t["aa"][:], t["pfx"][:], ALU.subtract, ["aa", "pfx"], ["aa"])
        for h in range(H):
            S.op("dve", lambda e, h=h: e.tensor_tensor_scan(out=t["gg"][:, h, :], data0=ones_hp[:], data1=t["aa"][:, h, :], initial=-1e30,
                                                            op0=ALU.mult, op1=ALU.max), reads=["aa", "ones_hp"], writes=["gg"])
        tt_op("dve", t["pg"][:], t["pfx"][:], t["gg"][:], ALU.add, ["pfx", "gg"], ["pg"])
        tt_op("dve", t["tmp"][:], t["aloc"][:], t["pg"][:], ALU.subtract, ["aloc", "pg"], ["tmp"])
        S.op("act", lambda e, d=d: e.activation(out=self.gw[d][:], in_=t["tmp"][:], func=AF.Exp), reads=["tmp"], writes=[("gw", d)])
        tt_op("dve", t["tmp"][:], t["bl"][:], t["pg"][:], ALU.add, ["bl", "pg"], ["tmp"])
        S.op("act", lambda e, d=d: e.activation(out=self.gcl[d][:], in_=t["tmp"][:], func=AF.Exp, scale=-1.0), reads=["tmp"], writes=[("gcl", d)])
        S.op("dve", lambda e: e.tensor_copy(out=t["tmp"][:, :, 1:P], in_=t["gg"][:, :, 0:P - 1]), reads=["gg"], writes=["tmp"])
        S.op("dve", lambda e: e.tensor_copy(out=t["tmp"][:, :, 0:1], in_=t["gg"][:, :, 0:1]), reads=["gg"], writes=["tmp"])
        tt_op("dve", t["tmp"][:], t["tmp"][:], t["gg"][:], ALU.subtract, ["tmp", "gg"], ["tmp"])
        S.op("act", lambda e, d=d: e.activation(out=self.gr[d][:], in_=t["tmp"][:], func=AF.Exp), reads=["tmp"], writes=[("gr", d)])
        S.op("dve", lambda e, d=d: e.tensor_scalar(out=self.grq[d][:], in0=self.gr[d][:], scalar1=float(c.HD) ** -0.5, scalar2=None, op0=ALU.mult),
             reads=[("gr", d)], writes=[("grq", d)])
    self.dbg_gates = t


Prog.gate_prologue = _gate_prologue


def _phase_B2(self, st):
    S, c, nc = self.S, self.cfg, self.nc
    KH, NTC, NCC, NT, LT, HD, H = self.KH, self.NTC, self.NCC, self.NT, self.LT, c.HD, c.H
    pst = lambda n: st.enter_context(nc.psum_tensor(f"{n}_{self.u()[1]}", [128, 512], F32))
    PC = [pst(f"b2C{k}") for k in range(2)]
    PN = [pst(f"b2N{d}") for d in range(2)]
    PM = [pst(f"b2M{d}") for d in range(2)]
    PS = [pst(f"b2S{d}") for d in range(2)]
    QTt = self.sb(st, "QTt", [128, KH, c.L], BF16)
    KTt = self.sb(st, "KTt", [128, KH, LT], BF16)
    Ktm = self.sb(st, "Ktm", [128, NTC, HD], BF16)
    Vtm = self.sb(st, "Vtm", [128, NTC, HD], BF16)
    C32 = [self.sb(st, f"C32{d}", [128, KH, HD], F32) for d in range(2)]
    C16 = [[self.sb(st, f"C16{d}{p}", [128, KH, HD], BF16) for p in range(2)] for d in range(2)]
    n32 = [self.sb(st, f"n32{d}", [128, KH], F32) for d in range(2)]
    n16 = [[self.sb(st, f"n16{d}{p}", [128, KH], BF16) for p in range(2)] for d in range(2)]
    msk = [self.sb(st, f"msk{d}", [128, 128], F32) for d in range(2)]
    SD = [[self.sb(st, f"SD{d}{i}", [128, 128], BF16) for i in range(2)] for d in range(2)]
    qr = [[self.sb(st, f"qr{d}{i}", [128, KH, 128], BF16) for i in range(2)] for d in range(2)]
    kw = [[self.sb(st, f"kw{d}{i}", [128, HD], BF16) for i in range(2)] for d in range(2)]
    dn = [[self.sb(st, f"dn{d}{i}", [128, 2], F32) for i in range(2)] for d in range(2)]
    hfs = [[self.sb(st, f"hfs{d}{i}", [128, HD], F32) for i in range(2)] for d in range(2)]
    qs = float(HD) ** -0.5
    for d, tri in enumerate((self.triu_f, self.tril_f)):
        S.op("dve", lambda e, d=d, tri=tri: e.tensor_scalar(out=msk[d][:], in0=tri[:], scalar1=qs, scalar2=None, op0=ALU.mult),
             reads=["triu_f", "tril_f"], writes=[("msk", d)])
    order = [list(range(NTC)), list(range(NCC - 1, -1, -1)) + list(range(NTC - 1, NCC - 1, -1))]
    one_col = self.onesb[:, 0:1]
    HOUT = [self.HF, self.HB]
    pcrot = [0]

    def gkeys(d):
        return [("gw", d), ("gr", d), ("grq", d), ("gcl", d)]

    def is_x(d, i):
        return i < NTC and order[d][i] >= NCC

    def pre_act(hd, d, i):
        if i >= NTC:
            return
        sc = order[d][i]
        b = i % 2
        if i < NTC - 1:
            wc = self.gw[d][:, hd, i:i + 1]
            S.op("act", lambda e: e.activation(out=kw[d][b][:], in_=Ktm[:, sc, :], func=AF.Copy, scale=wc),
                 reads=["Ktm"] + gkeys(d), writes=[("kw", d, b)])
        if is_x(d, i):
            xi = sc - NCC
            rqc = self.grq[d][:, hd, i:i + 1]
            S.op("act", lambda e: e.activation(out=qr[d][b][:], in_=QTt[:, :, xi * 128:(xi + 1) * 128], func=AF.Copy, scale=rqc),
                 reads=["QTt"] + gkeys(d), writes=[("qr", d, b)])

    def pre_scores(hd, d, i):
        if not is_x(d, i):
            return
        sc = order[d][i]
        xi = sc - NCC
        b = i % 2
        wc = self.gw[d][:, hd, i:i + 1]
        for kc in range(KH):
            self.mm(PS[d][:, 0:128], KTt[:, kc, sc * 128:(sc + 1) * 128], QTt[:, kc, xi * 128:(xi + 1) * 128],
                    kc == 0, kc == KH - 1, ["KTt", "QTt"], [("psS", d)])
        S.op("dve", lambda e: e.scalar_tensor_tensor(out=SD[d][b][:], in0=PS[d][:, 0:128], scalar=wc, in1=msk[d][:], op0=ALU.mult, op1=ALU.mult),
             reads=[("psS", d), ("msk", d)] + gkeys(d), writes=[("SD", d, b)])

    def state_pe(hd, d, i):
        if i >= NTC - 1:
            return []
        sc = order[d][i]
        b = i % 2
        banks = []
        for kc in range(KH):
            pi = pcrot[0] % 2
            pcrot[0] += 1
            banks.append(pi)
            self.mm(PC[pi][:, 0:HD], kw[d][b][:, kc * 128:(kc + 1) * 128], Vtm[:, sc, :], True, True, [("kw", d, b), "Vtm"], [("psC", pi)])
            self.mm(PM[d][:, kc:kc + 1], kw[d][b][:, kc * 128:(kc + 1) * 128], one_col, True, True, [("kw", d, b), "onesb"], [("psM", d)])
            rc_ = self.gr[d][:, hd, i:i + 1]
            S.op("dve", lambda e, kc=kc, pi=pi, rc_=rc_: e.scalar_tensor_tensor(
                out=C32[d][:, kc, :], in0=C32[d][:, kc, :], scalar=rc_, in1=PC[pi][:, 0:HD], op0=ALU.mult, op1=ALU.add),
                reads=[("psC", pi), ("C32", d, kc)] + gkeys(d), writes=[("C32", d, kc)])
        return banks

    def state_post(hd, d, i):
        if i >= NTC - 1:
            return
        nxt = (i + 1) % 2
        rc_ = self.gr[d][:, hd, i:i + 1]
        hk = KH // 2 if KH >= 2 else KH
        S.op("act", lambda e: e.activation(out=C16[d][nxt][:, 0:hk, :], in_=C32[d][:, 0:hk, :], func=AF.Copy),
             reads=[("C32", d, k) for k in range(hk)], writes=[("C16", d, nxt, 0)])
        if hk < KH:
            S.op("dve", lambda e: e.tensor_copy(out=C16[d][nxt][:, hk:KH, :], in_=C32[d][:, hk:KH, :]),
                 reads=[("C32", d, k) for k in range(hk, KH)], writes=[("C16", d, nxt, 1)])
        S.op("dve", lambda e: e.scalar_tensor_tensor(out=n32[d][:], in0=n32[d][:], scalar=rc_, in1=PM[d][:, 0:KH], op0=ALU.mult, op1=ALU.add),
             reads=[("psM", d), ("n32", d)] + gkeys(d), writes=[("n32", d)])
        S.op("dve", lambda e: e.tensor_copy(out=n16[d][nxt][:], in_=n32[d][:]), reads=[("n32", d)], writes=[("n16", d, nxt)])

    def read_pe(hd, d, i):
        if not is_x(d, i):
            return
        sc = order[d][i]
        b, cur = i % 2, i % 2
        ck = [("C16", d, cur, 0), ("C16", d, cur, 1)]
        self.mm(PN[d][:, 0:HD], SD[d][b][:], Vtm[:, sc, :], True, False, [("SD", d, b), "Vtm"], [("psN", d)])
        for kc in range(KH):
            self.mm(PN[d][:, 0:HD], qr[d][b][:, kc, :], C16[d][cur][:, kc, :], False, kc == KH - 1, [("qr", d, b)] + ck, [("psN", d)])
        self.mm(PS[d][:, 128:129], SD[d][b][:], one_col, True, False, [("SD", d, b), "onesb"], [("psS", d)])
        for kc in range(KH):
            self.mm(PS[d][:, 128:129], qr[d][b][:, kc, :], n16[d][cur][:, kc:kc + 1], False, kc == KH - 1, [("qr", d, b), ("n16", d, cur)], [("psS", d)])

    def read_post(hd, d, i):
        if not is_x(d, i):
            return
        sc = order[d][i]
        xi = sc - NCC
        r0 = hd * HD
        b = i % 2
        clc = self.gcl[d][:, hd, i:i + 1]
        S.op("act", lambda e: e.activation(out=dn[d][b][:, 0:1], in_=PS[d][:, 128:129], func=AF.Abs), reads=[("psS", d)], writes=[("dn", d, b)])
        S.op("dve", lambda e: e.tensor_scalar(out=dn[d][b][:, 0:1], in0=dn[d][b][:, 0:1], scalar1=clc, scalar2=None, op0=ALU.max),
             reads=[("dn", d, b)] + gkeys(d), writes=[("dn", d, b)])
        S.op("dve", lambda e: e.reciprocal(out=dn[d][b][:, 1:2], in_=dn[d][b][:, 0:1]), reads=[("dn", d, b)], writes=[("dn", d, b)])
        S.op("act", lambda e: e.activation(out=hfs[d][b][:], in_=PN[d][:, 0:HD], func=AF.Copy, scale=dn[d][b][:, 1:2]),
             reads=[("psN", d), ("dn", d, b)], writes=[("hfs", d, b)])
        self.dma("sp", HOUT[d][xi * 128:(xi + 1) * 128, r0:r0 + HD], hfs[d][b][:], [("hfs", d, b)], [("HFB", d, hd, xi)])

    for hd in range(H):
        r0 = hd * HD
        self.dma("sp", QTt[:], self.QT[r0:r0 + HD, :].rearrange("(kc p) t -> p kc t", p=128), [], ["QTt"])
        self.dma("sp", KTt[:], self.KT[r0:r0 + HD, :].rearrange("(kc p) t -> p kc t", p=128), [], ["KTt"])
        self.dma("sp", Ktm[:], self.KTM[:, r0:r0 + HD].rearrange("(tc p) ch -> p tc ch", p=128), [], ["Ktm"])
        self.dma("sp", Vtm[:], self.VTM[:, r0:r0 + HD].rearrange("(tc p) ch -> p tc ch", p=128), [], ["Vtm"])
        for d in range(2):
            S.op("pool", lambda e, d=d: e.memset(C32[d][:], 0.0), writes=[("C32", d, k) for k in range(KH)])
            S.op("pool", lambda e, d=d: e.memset(C16[d][0][:], 0.0), writes=[("C16", d, 0, 0), ("C16", d, 0, 1)])
            S.op("pool", lambda e, d=d: e.memset(n32[d][:], 0.0), writes=[("n32", d)])
            S.op("pool", lambda e, d=d: e.memset(n16[d][0][:], 0.0), writes=[("n16", d, 0)])
        for d in range(2):
            pre_act(hd, d, 0)
        for d in range(2):
            pre_scores(hd, d, 0)
        for i in range(NTC):
            for d in range(2):
                pre_act(hd, d, i + 1)
            for d in range(2):
                state_pe(hd, d, i)
                read_pe(hd, d, i)
                state_post(hd, d, i)
            for d in range(2):
                read_post(hd, d, i)
            for d in range(2):
                pre_scores(hd, d, i + 1)


Prog.phase_B2 = _phase_B2


def _phase_B4(self, st):
    S, c = self.S, self.cfg
    AC, HW = self.AC, c.HW
    self.psum_alloc(st, 8)
    ntc = HW // 128
    gnc = self.sb(st, "gnc", [128, AC], F32)
    skc = self.sb(st, "skc", [128, AC], F32)
    self.dma("sp", gnc[:], self.a_normg, [], ["gnc"])
    self.dma("sp", skc[:], self.a_skip, [], ["skc"])
    hnt = [self.sb(st, f"hnt{i}", [128, ntc, c.DA], BF16) for i in range(2)]
    xct = [self.sb(st, f"xct{i}", [128, AC, HW], BF16) for i in range(2)]
    szt = [self.sb(st, f"szt{i}", [128, AC, HW], BF16) for i in range(2)]
    yat = [self.sb(st, f"yat{i}", [128, AC, HW], BF16) for i in range(2)]
    xs = [self.sb(st, f"xs{i}", [128, HW], F32) for i in range(2)]
    u1 = [self.sb(st, f"u1{i}", [128, HW], F32) for i in range(2)]
    hft = [self.sb(st, f"hft{i}", [128, c.DA], F32) for i in range(2)]
    hbt = [self.sb(st, f"hbt{i}", [128, c.DA], F32) for i in range(2)]
    junk4 = self.sb(st, "junk4", [128, c.HD], BF16)
    ss4 = [self.sb(st, f"ss4{i}", [128, 2 * c.H], F32) for i in range(2)]
    rot4 = {}
    fv = lambda ap: ap.rearrange("(kc p) t -> p kc t", p=128)
    for ti, t0 in enumerate(range(0, c.L, HW)):
        b = ti % 2
        for tci in range(ntc):
            tok0 = t0 + tci * 128
            pi_ = rot4.get("hf", 0)
            rot4["hf"] = pi_ + 1
            pb = pi_ % 2
            self.dma("sp", hft[pb][:], self.HF[tok0:tok0 + 128, :], [], [("hft", pb)])
            self.dma("sp", hbt[pb][:], self.HB[tok0:tok0 + 128, :], [], [("hbt", pb)])
            S.op("dve", lambda e, pb=pb: e.tensor_tensor(out=hft[pb][:], in0=hft[pb][:], in1=hbt[pb][:], op=ALU.add),
                 reads=[("hft", pb), ("hbt", pb)], writes=[("hft", pb)])
            for hh in range(c.H):
                S.op("act", lambda e, pb=pb, hh=hh: e.activation(out=junk4[:], in_=hft[pb][:, hh * c.HD:(hh + 1) * c.HD], func=AF.Square,
                                                                 accum_out=ss4[pb][:, hh:hh + 1]), reads=[("hft", pb)], writes=["junk4", ("ss4", pb)])
            S.op("act", lambda e, pb=pb: e.activation(out=ss4[pb][:, c.H:2 * c.H], in_=ss4[pb][:, 0:c.H], func=AF.Sqrt, bias=self.epsD[:, 1:2], scale=1.0 / c.HD),
                 reads=[("ss4", pb), "epsD"], writes=[("ss4", pb)])
            S.op("dve", lambda e, pb=pb: e.reciprocal(out=ss4[pb][:, c.H:2 * c.H], in_=ss4[pb][:, c.H:2 * c.H]), reads=[("ss4", pb)], writes=[("ss4", pb)])
            S.op("dve", lambda e, pb=pb, b=b, tci=tci: e.tensor_tensor(
                out=hnt[b][:, tci, :].rearrange("p (h e) -> p h e", h=c.H), in0=hft[pb][:].rearrange("p (h e) -> p h e", h=c.H),
                in1=ss4[pb][:, c.H:2 * c.H].unsqueeze(2).broadcast_to([128, c.H, c.HD]), op=ALU.mult),
                reads=[("hft", pb), ("ss4", pb)], writes=[("hnt", b)])
        self.dma("sp", xct[b][:], fv(self.XC[:, t0:t0 + HW]), [], [("xct", b)])
        self.dma("sp", szt[b][:], fv(self.PROJ[c.DA:2 * c.DA, t0:t0 + HW]), [], [("szt", b)])
        for cc in range(AC):
            ps, pk = self.psn()
            psb = ps[:].bitcast(BF16)
            for tc in range(ntc):
                S.op("pe", lambda e, psb=psb, tc=tc, cc=cc, b=b: e.transpose(out=psb[:, tc * 128:(tc + 1) * 128],
                                                                         in_=hnt[b][:, tc, cc * 128:(cc + 1) * 128], identity=self.idb[:]),
                     reads=[("hnt", b), "idb"], writes=[pk])
            j = cc % 2
            S.op("act", lambda e, cc=cc, b=b, j=j: e.activation(out=xs[j][:], in_=xct[b][:, cc, :], func=AF.Copy, scale=skc[:, cc:cc + 1]),
                 reads=[("xct", b), "skc"], writes=[("xs", j)])
            S.op("dve", lambda e, psb=psb, cc=cc, j=j: e.scalar_tensor_tensor(out=u1[j][:], in0=psb[:, 0:HW], scalar=gnc[:, cc:cc + 1], in1=xs[j][:],
                                                                             op0=ALU.mult, op1=ALU.add), reads=[pk, ("xs", j), "gnc"], writes=[("u1", j)])
            S.op("dve", lambda e, cc=cc, b=b, j=j: e.tensor_tensor(out=yat[b][:, cc, :], in0=u1[j][:], in1=szt[b][:, cc, :], op=ALU.mult),
                 reads=[("u1", j), ("szt", b)], writes=[("yat", b)])
        self.dma("sp", fv(self.YA[:, t0:t0 + HW]), yat[b][:], [("yat", b)], [("YA", t0)])


Prog.phase_B4 = _phase_B4


def _phase_B3(self, st):
    S, c = self.S, self.cfg
    NT, L, CB, HW, FH, DB = self.NT, c.L, c.CB, c.HW, c.FH, c.DB
    CBC = CB // 128
    NBLK = DB // CB
    ntc = HW // 128
    N2 = 2 * L
    self.psum_alloc(st, 8)
    rot = self.__dict__.setdefault("_rot", {})
    sbt = lambda n, shp, dt: self.sb(st, n, shp, dt)
    hy0 = 2 * c.DA
    cw = sbt("h_cw", [128, 3, 3 * self.BC], F32)
    cbv = sbt("h_cb", [128, 3 * self.BC], F32)
    dsk = sbt("h_dsk", [128, self.BC], F32)
    altc = sbt("h_altc", [128, 2], F32)
    altcb = sbt("h_altcb", [128, 1], BF16)
    altrf = sbt("h_altrf", [1, 128], F32)
    altrb = sbt("h_altrb", [1, 128], BF16)
    tnc = sbt("h_tn", [128, NT], F32)
    dlb = sbt("h_dl", [128, DB], F32)
    fbt = sbt("h_fb", [FH, 4], F32)
    fbb = sbt("h_fbb", [FH, 3], F32)
    aA = sbt("h_aA", [FH, L], F32)
    st_mlp = ExitStack()
    sbm = lambda n, shp, dt: self.sb(st_mlp, n, shp, dt)
    w1 = sbm("h_w1", [c.FE, FH], F32)
    w2 = sbm("h_w2", [FH, FH], F32)
    w3 = sbm("h_w3", [FH, FH], F32)
    zt = sbm("h_zt", [c.FE, L], F32)
    aB = sbm("h_aB", [FH, L], F32)
    for dst, src, k in ((cw, self.b_convw, "h_cw"), (cbv, self.b_convb, "h_cb"), (dsk, self.b_skip, "h_dsk"), (altc, self.altc, "h_altc"),
                        (altrf, self.altr, "h_altrf"), (tnc, self.tn, "h_tn"), (fbt, self.fb, "h_fb"), (w1, self.fw1, "h_w1"),
                        (w2, self.fw2, "h_w2"), (w3, self.fw3, "h_w3"), (zt, self.zT, "h_zt")):
        self.dma("sp", dst[:], src, [], [k])
    self.dma("sp", dlb[:], self.deltas.partition_broadcast(128), [], ["h_dl"])
    S.op("dve", lambda e: e.tensor_copy(out=altcb[:], in_=altc[:, 0:1]), reads=["h_altc"], writes=["h_altcb"])
    S.op("dve", lambda e: e.tensor_copy(out=altrb[:], in_=altrf[:]), reads=["h_altrf"], writes=["h_altrb"])
    S.op("dve", lambda e: e.tensor_scalar(out=tnc[:], in0=tnc[:], scalar1=-1.0, scalar2=None, op0=ALU.mult), reads=["h_tn"], writes=["h_tn"])
    for l in range(3):
        S.op("dve", lambda e, l=l: e.tensor_scalar(out=fbb[:, l:l + 1], in0=fbt[:, l:l + 1], scalar1=fbt[:, 3:4], scalar2=0.0,
                                                    op0=ALU.mult, op1=ALU.add), reads=["h_fb"], writes=["h_fbb"])
    tmpa = [sbm(f"h_tmpa{i}", [FH, HW], F32) for i in range(2)]
    tmpk = [sbm(f"h_tmpk{i}", [FH, HW], F32) for i in range(2)]
    tmpm = [sbm(f"h_tmpm{i}", [FH, HW], F32) for i in range(2)]
    tmpi = [sbm(f"h_tmpi{i}", [FH, HW], mybir.dt.int32) for i in range(2)]
    src_t, srck = zt, "h_zt"
    for l, (wl, wk, dst, dk) in enumerate(((w1, "h_w1", aA, "h_aA"), (w2, "h_w2", aB, "h_aB"), (w3, "h_w3", aA, "h_aA"))):
        for t0 in range(0, L, HW):
            ps, pk = self.psn()
            self.mm(ps[0:FH, 0:HW], wl[:], src_t[:, t0:t0 + HW], True, True, [wk, srck], [pk])
            j = (t0 // HW) % 2
            S.op("dve", lambda e, ps=ps, j=j, l=l: e.tensor_scalar(out=tmpa[j][:], in0=ps[0:FH, 0:HW], scalar1=fbt[:, 3:4], scalar2=fbb[:, l:l + 1],
                                                                   op0=ALU.mult, op1=ALU.add), reads=[pk, "h_fb", "h_fbb"], writes=[("h_tmpa", j)])
            tk = ("h_tmpa", j)
            S.op("dve", lambda e, j=j: e.tensor_scalar(out=tmpk[j][:], in0=tmpa[j][:], scalar1=1.0 / (2.0 * math.pi), scalar2=None, op0=ALU.mult),
                 reads=[tk], writes=[("h_tmpk", j)])
            S.op("dve", lambda e, j=j: e.tensor_copy(out=tmpi[j][:], in_=tmpk[j][:]), reads=[("h_tmpk", j)], writes=[("h_tmpi", j)])
            S.op("dve", lambda e, j=j: e.tensor_copy(out=tmpk[j][:], in_=tmpi[j][:]), reads=[("h_tmpi", j)], writes=[("h_tmpk", j)])
            S.op("dve", lambda e, j=j: e.scalar_tensor_tensor(out=tmpa[j][:], in0=tmpk[j][:], scalar=-2.0 * math.pi, in1=tmpa[j][:], op0=ALU.mult, op1=ALU.add),
                 reads=[tk, ("h_tmpk", j)], writes=[tk])
            S.op("dve", lambda e, j=j: e.tensor_scalar(out=tmpk[j][:], in0=tmpa[j][:], scalar1=math.pi, scalar2=None, op0=ALU.is_gt),
                 reads=[tk], writes=[("h_tmpk", j)])
            S.op("dve", lambda e, j=j: e.tensor_scalar(out=tmpm[j][:], in0=tmpa[j][:], scalar1=-math.pi, scalar2=None, op0=ALU.is_lt),
                 reads=[tk], writes=[("h_tmpm", j)])
            S.op("dve", lambda e, j=j: e.scalar_tensor_tensor(out=tmpa[j][:], in0=tmpk[j][:], scalar=-2.0 * math.pi, in1=tmpa[j][:], op0=ALU.mult, op1=ALU.add),
                 reads=[tk, ("h_tmpk", j)], writes=[tk])
            S.op("dve", lambda e, j=j: e.scalar_tensor_tensor(out=tmpa[j][:], in0=tmpm[j][:], scalar=2.0 * math.pi, in1=tmpa[j][:], op0=ALU.mult, op1=ALU.add),
                 reads=[tk, ("h_tmpm", j)], writes=[tk])
            S.op("act", lambda e, j=j, dst=dst, t0=t0: e.activation(out=dst[:, t0:t0 + HW], in_=tmpa[j][:], func=AF.Sin),
                 reads=[tk], writes=[dk])
        src_t, srck = dst, dk
    a3 = aA
    S.barrier()
    st_mlp.close()
    dgb = sbt("h_dgb", [128, 3, 3 * CBC, 128], BF16)
    hyp = [sbt(f"h_hyp{i}", [128, L + 2], BF16) for i in range(2)]
    uT = sbt("h_uT", [128, CBC, L], BF16)
    x1s = [sbt(f"h_x1s{i}", [128, HW], BF16) for i in range(2)]
    UHD = sbt("h_UHD", [128, NT, 3 * CB], BF16)
    YF = sbt("h_YF", [128, NT, 2, CB], BF16)
    w4b = sbt("h_w4b", [FH, 2, CB], F32)
    win = [sbt(f"h_win{i}", [128, CB], F32) for i in range(2)]
    hfw = [sbt(f"h_hfw{i}", [128, CB], F32) for i in range(2)]
    hbw = [sbt(f"h_hbw{i}", [128, CB], F32) for i in range(2)]
    tabs = [[sbt(f"h_tab{n}{i}", [128, NT, 128], BF16) for i in range(2)] for n in "cs"]
    kre = [sbt(f"h_kre{i}", [128, CB], F32) for i in range(1)]
    kim = [sbt(f"h_kim{i}", [128, CB], F32) for i in range(1)]
    tq = [[sbt(f"h_t{n}{i}", [128, CB], F32) for i in range(1)] for n in range(4)]
    ny = sbt("h_ny", [1, 2 * CB], F32)
    ynyb = sbt("h_ynyb", [1, CB], BF16)
    tmpf = [sbt(f"h_tmpf{i}", [128, HW], F32) for i in range(2)]
    ybs = [sbt(f"h_ybs{i}", [128, HW], BF16) for i in range(2)]
    for i in range(2):
        S.op("pool", lambda e, i=i: e.memset(hyp[i][:], 0.0), writes=[("h_hyp", i)])
    ytm = UHD

    def load_stream(s_idx, cb, kc):
        row0 = hy0 + s_idx * DB + cb * CB + kc * 128
        bi = rot.get("hyp", 0)
        rot["hyp"] = bi + 1
        b = bi % 2
        self.dma("sp", hyp[b][:, 1:1 + L], self.PROJ[row0:row0 + 128, :], [], [("h_hyp", b)])
        return b

    def conv_tile(b, s_idx, kc, t0, n):
        ps, pk = self.psn()
        for j in range(3):
            self.mm(ps[:, 0:n], dgb[:, j, s_idx * CBC + kc, :], hyp[b][:, t0 + j:t0 + j + n], j == 0, j == 2, ["h_dgb", ("h_hyp", b)], [pk])
        return ps, pk

    for cb in getattr(self, 'dbg_blocks', range(NBLK)):
        ch0 = cb * CB
        if getattr(self, 'dbg_bar', False):
            S.barrier()
        for s_idx in range(3):
            for kc in range(CBC):
                col = s_idx * self.BC + (ch0 // 128) + kc
                for j in range(3):
                    S.op("dve", lambda e, j=j, s_idx=s_idx, kc=kc, col=col: e.tensor_scalar(
                        out=dgb[:, j, s_idx * CBC + kc, :], in0=self.idf[:], scalar1=cw[:, j, col:col + 1], scalar2=None, op0=ALU.mult),
                        reads=["idf", "h_cw"], writes=["h_dgb"])
        self.dma("sp", w4b[:, 0, :], self.fw4[:, ch0:ch0 + CB], [], ["h_w4b"])
        self.dma("sp", w4b[:, 1, :], self.fw4[:, DB + ch0:DB + ch0 + CB], [], ["h_w4b"])
        for kc in range(CBC):
            b1 = load_stream(1, cb, kc)
            b2 = load_stream(2, cb, kc)
            c1 = 1 * self.BC + ch0 // 128 + kc
            c2 = 2 * self.BC + ch0 // 128 + kc
            for t0 in range(0, L, HW):
                p1, k1 = conv_tile(b1, 1, kc, t0, HW)
                j = (t0 // HW) % 2
                S.op("act", lambda e, p1=p1, j=j, c1=c1: e.activation(out=x1s[j][:], in_=p1[:, 0:HW], func=AF.Identity, bias=cbv[:, c1:c1 + 1], scale=1.0),
                     reads=[k1, "h_cb"], writes=[("h_x1s", j)])
                p2, k2 = conv_tile(b2, 2, kc, t0, HW)
                S.op("dve", lambda e, p2=p2, j=j, c2=c2, kc=kc, t0=t0: e.scalar_tensor_tensor(
                    out=uT[:, kc, t0:t0 + HW], in0=p2[:, 0:HW], scalar=cbv[:, c2:c2 + 1], in1=x1s[j][:], op0=ALU.add, op1=ALU.mult),
                    reads=[k2, ("h_x1s", j), "h_cb"], writes=[("h_uT", kc)])
        for tc0 in range(0, NT, ntc):
            for kc in range(CBC):
                ps, pk = self.psn()
                psb = ps[:].bitcast(BF16)
                for i in range(ntc):
                    tc = tc0 + i
                    S.op("pe", lambda e, psb=psb, i=i, tc=tc, kc=kc: e.transpose(out=psb[:, i * 128:(i + 1) * 128], in_=uT[:, kc, tc * 128:(tc + 1) * 128],
                                                                             identity=self.idb[:]), reads=[("h_uT", kc), "idb"], writes=[pk])
                S.op("act", lambda e, psb=psb, tc0=tc0, kc=kc: e.activation(
                    out=UHD[:, tc0:tc0 + ntc, CB + kc * 128:CB + (kc + 1) * 128], in_=psb[:, 0:ntc * 128].rearrange("p (a b) -> p a b", b=128), func=AF.Copy),
                    reads=[pk], writes=[("h_UHD", tc0 + i) for i in range(ntc)])
        for tc in range(NT):
            j = tc % 2
            ps, pk = self.psn()
            self.mm(ps[:, 0:2 * CB], a3[:, tc * 128:(tc + 1) * 128], w4b[:].rearrange("p a b -> p (a b)"), True, True, ["h_aA", "h_w4b"], [pk])
            S.op("act", lambda e, j=j, tc=tc, ch0=ch0: e.activation(out=win[j][:], in_=dlb[:, ch0:ch0 + CB], func=AF.Exp, scale=tnc[:, tc:tc + 1]),
                 reads=["h_dl", "h_tn"], writes=[("h_win", j)])
            S.op("dve", lambda e, ps=ps, j=j: e.scalar_tensor_tensor(out=hfw[j][:], in0=win[j][:], scalar=0.05, in1=ps[:, 0:CB], op0=ALU.add, op1=ALU.mult),
                 reads=[pk, ("h_win", j)], writes=[("h_hfw", j)])
            S.op("dve", lambda e, ps=ps, j=j: e.scalar_tensor_tensor(out=hbw[j][:], in0=win[j][:], scalar=0.05, in1=ps[:, CB:2 * CB], op0=ALU.add, op1=ALU.mult),
                 reads=[pk, ("h_win", j)], writes=[("h_hbw", j)])
            if tc == 0:
                S.op("dve", lambda e, j=j: e.memset(hbw[j][0:1, :], 0.0), reads=[("h_hbw", j)], writes=[("h_hbw", j)])
            S.op("dve", lambda e, j=j, tc=tc: e.tensor_tensor(out=UHD[:, tc, 0:CB], in0=hfw[j][:], in1=hbw[j][:], op=ALU.add),
                 reads=[("h_hfw", j), ("h_hbw", j)], writes=[("h_UHD", tc)])
            S.op("dve", lambda e, j=j, tc=tc: e.tensor_tensor(out=UHD[:, tc, 2 * CB:3 * CB], in0=hfw[j][:], in1=hbw[j][:], op=ALU.subtract),
                 reads=[("h_hfw", j), ("h_hbw", j)], writes=[("h_UHD", tc)])
        ukeys = [("h_UHD", tc) for tc in range(NT)]
        for fc in range(NT):
            bi = rot.get("tab", 0)
            rot["tab"] = bi + 1
            tb = bi % 2
            self.dma("sp", tabs[0][tb][:], self.ctab[fc], [], [("h_tabc", tb)])
            self.dma("sp", tabs[1][tb][:], self.stab[fc], [], [("h_tabs", tb)])
            psR, kR = self.psn()
            psI, kI = self.psn()
            for tc in range(NT):
                self.mm(psR[:, 0:2 * CB], tabs[0][tb][:, tc, :], UHD[:, tc, 0:2 * CB], tc == 0, tc == NT - 1, [("h_tabc", tb), ("h_UHD", tc)], [kR])
            for tc in range(NT):
                self.mm(psI[:, 0:2 * CB], tabs[1][tb][:, tc, :], UHD[:, tc, CB:3 * CB], tc == 0, tc == NT - 1, [("h_tabs", tb), ("h_UHD", tc)], [kI])
            j = 0
            S.op("act", lambda e, psR=psR, j=j: e.activation(out=kre[j][:], in_=psR[:, 0:CB], func=AF.Copy), reads=[kR], writes=[("h_kre", j)])
            S.op("act", lambda e, psI=psI, j=j: e.activation(out=kim[j][:], in_=psI[:, CB:2 * CB], func=AF.Copy), reads=[kI], writes=[("h_kim", j)])
            xre, xim = psR[:, CB:2 * CB], psI[:, 0:CB]
            for n, (a, bsrc, ak, bk) in enumerate(((xre, kre, kR, "h_kre"), (xim, kim, kI, "h_kim"), (xre, kim, kR, "h_kim"), (xim, kre, kI, "h_kre"))):
                S.op("dve", lambda e, n=n, a=a, bsrc=bsrc, j=j: e.tensor_tensor(out=tq[n][j][:], in0=bsrc[j][:], in1=a, op=ALU.mult),
                     reads=[ak, (bk, j)], writes=[("h_tq", n, j)])
            S.op("dve", lambda e, j=j, fc=fc: e.tensor_tensor(out=YF[:, fc, 0, :], in0=tq[0][j][:], in1=tq[1][j][:], op=ALU.subtract),
                 reads=[("h_tq", 0, j), ("h_tq", 1, j)], writes=[("h_YF", fc)])
            S.op("dve", lambda e, j=j, fc=fc: e.tensor_tensor(out=YF[:, fc, 1, :], in0=tq[2][j][:], in1=tq[3][j][:], op=ALU.add),
                 reads=[("h_tq", 2, j), ("h_tq", 3, j)], writes=[("h_YF", fc)])
            if fc == 0:
                S.op("dve", lambda e, j=j: e.tensor_scalar(out=YF[0:1, 0, 0, :], in0=tq[0][j][0:1, :], scalar1=0.5, scalar2=None, op0=ALU.mult),
                     reads=[("h_tq", 0, j), ("h_YF", 0)], writes=[("h_YF", 0)])
                psNy, kNy = self.psn()
                for tc in range(NT):
                    self.mm(psNy[0:1, 0:2 * CB], altcb[:], UHD[:, tc, 0:2 * CB], tc == 0, tc == NT - 1, ["h_altcb", ("h_UHD", tc)], [kNy])
                S.op("dve", lambda e, psNy=psNy: e.tensor_copy(out=ny[:], in_=psNy[0:1, 0:2 * CB]), reads=[kNy], writes=["h_ny"])
                S.op("dve", lambda e: e.scalar_tensor_tensor(out=ynyb[:], in0=ny[:, 0:CB], scalar=0.5, in1=ny[:, CB:2 * CB], op0=ALU.mult, op1=ALU.mult),
                     reads=["h_ny"], writes=["h_ynyb"])
        ykeys = [("h_YF", fc) for fc in range(NT)]
        for tc in range(NT):
            bi = rot.get("tab", 0)
            rot["tab"] = bi + 1
            tb = bi % 2
            self.dma("sp", tabs[0][tb][:], self.ctab[tc], [], [("h_tabc", tb)])
            self.dma("sp", tabs[1][tb][:], self.stab[tc], [], [("h_tabs", tb)])
            psY, kY = self.psn()
            for fc in range(NT):
                self.mm(psY[:, 0:CB], tabs[0][tb][:, fc, :], YF[:, fc, 0, :], fc == 0, False, [("h_tabc", tb), ("h_YF", fc)], [kY])
                self.mm(psY[:, 0:CB], tabs[1][tb][:, fc, :], YF[:, fc, 1, :], False, False, [("h_tabs", tb), ("h_YF", fc)], [kY])
            self.mm(psY[:, 0:CB], altrb[:], ynyb[:], False, True, ["h_altrb", "h_ynyb"], [kY])
            S.op("act", lambda e, psY=psY, tc=tc: e.activation(out=ytm[:, tc, 0:CB], in_=psY[:, 0:CB], func=AF.Copy, scale=2.0 / N2),
                 reads=[kY] + ukeys, writes=[("h_UHD", tc)])
        for kc in range(CBC):
            b0 = load_stream(0, cb, kc)
            c0 = 0 * self.BC + ch0 // 128 + kc
            dcol = ch0 // 128 + kc
            for t0 in range(0, L, HW):
                tc0 = t0 // 128
                ps, pk = self.psn()
                psb = ps[:].bitcast(BF16)
                for i in range(ntc):
                    S.op("pe", lambda e, psb=psb, i=i, tc0=tc0, kc=kc: e.transpose(out=psb[:, i * 128:(i + 1) * 128],
                                                                               in_=ytm[:, tc0 + i, kc * 128:(kc + 1) * 128], identity=self.idb[:]),
                         reads=[("h_UHD", tc0 + i), "idb"], writes=[pk])
                j = (t0 // HW) % 2
                S.op("dve", lambda e, psb=psb, j=j, kc=kc, t0=t0, dcol=dcol: e.scalar_tensor_tensor(
                    out=tmpf[j][:], in0=uT[:, kc, t0:t0 + HW], scalar=dsk[:, dcol:dcol + 1], in1=psb[:, 0:HW], op0=ALU.mult, op1=ALU.add),
                    reads=[pk, ("h_uT", kc), "h_dsk"], writes=[("h_tmpf", j)])
                p0, k0 = conv_tile(b0, 0, kc, t0, HW)
                S.op("dve", lambda e, p0=p0, j=j, c0=c0: e.scalar_tensor_tensor(
                    out=ybs[j][:], in0=p0[:, 0:HW], scalar=cbv[:, c0:c0 + 1], in1=tmpf[j][:], op0=ALU.add, op1=ALU.mult),
                    reads=[k0, ("h_tmpf", j), "h_cb"], writes=[("h_ybs", j)])
                self.dma("sp", self.YB[ch0 + kc * 128:ch0 + (kc + 1) * 128, t0:t0 + HW], ybs[j][:], [("h_ybs", j)], [("YB", cb, kc, t0)])


Prog.phase_B3 = _phase_B3


_CACHE = {}


def _get_prog(cfg_key=()):
    if cfg_key not in _CACHE:
        cfg = Cfg(*cfg_key) if cfg_key else Cfg()
        P = Prog(cfg)
        P.build()
        _CACHE[cfg_key] = (cfg, P, const_tables(cfg))
    return _CACHE[cfg_key]


def kernel(**inputs):
    cfg, P, tabs = _get_prog()
    inp = {k: np.asarray(v) for k, v in inputs.items()}
    nb = inp["x"].shape[0]
    n_cores = 8
    maps = [prep_core_inputs(cfg, inp, b, tabs) for b in range(nb)]
    zero_map = {k: np.zeros_like(v) for k, v in maps[0].items()}
    in_maps = [maps[i // 2] if (i % 2 == 0 and i // 2 < nb) else zero_map for i in range(n_cores)]
    res = run_bass_kernel_spmd(P.nc, in_maps, core_ids=list(range(n_cores)))
    out = np.stack([np.ascontiguousarray(np.asarray(res.results[2 * b]["outT"], np.float32).T) for b in range(nb)], 0)
    return out
```
